# Optimizing a Trainium2 kernel written in Bass

```python
import math
import jax, jax.numpy as jnp
from jax import lax
import numpy as np

D_MODEL = 2048
BATCH = 2
SEQ = 8192
DEPTH = 2

GDN_HEADS = 8
GDN_DK = 128
GDN_DV = 128
RET_HEADS = 8
RET_DK = 128
RET_DV = 128
CONV_WIDTH = 4
LA_CHUNK = 64
ROPE_BASE = 10000.0
GDN_QK = GDN_HEADS * GDN_DK
GDN_V = GDN_HEADS * GDN_DV
RET_QK = RET_HEADS * RET_DK
RET_V = RET_HEADS * RET_DV
MIX_IN = 2 * GDN_QK + 2 * GDN_V + 2 * GDN_HEADS + 2 * RET_QK + 2 * RET_V
MIX_OUT = GDN_V + RET_V
SG_CHUNK = 128
SG_GROUPS = 8
SG_WIDTH = 2 * D_MODEL
SG_GROUP_DIM = SG_WIDTH // SG_GROUPS
FFN_HIDDEN = 4 * D_MODEL
EPS = 1e-6
N_EVEN = (DEPTH + 1) // 2
N_ODD = DEPTH // 2

kernel_name = "hybrid_gdn_retention_gmlp_block"


def rmsnorm(x, w):
    xf = x.astype(jnp.float32)
    y = xf * lax.rsqrt(jnp.mean(jnp.square(xf), axis=-1, keepdims=True) + EPS)
    return (y * w.astype(jnp.float32)).astype(x.dtype)


def head_rmsnorm(x):
    return x * lax.rsqrt(jnp.mean(jnp.square(x), axis=-1, keepdims=True) + EPS)


def layernorm(x, w, b):
    xf = x.astype(jnp.float32)
    mu = jnp.mean(xf, axis=-1, keepdims=True)
    xc = xf - mu
    var = jnp.mean(jnp.square(xc), axis=-1, keepdims=True)
    y = xc * lax.rsqrt(var + EPS) * w.astype(jnp.float32) + b.astype(jnp.float32)
    return y.astype(x.dtype)


def l2norm(x):
    return x * lax.rsqrt(jnp.sum(jnp.square(x), axis=-1, keepdims=True) + 1e-6)


def causal_conv(x, w):
    k_w = w.shape[-1]
    kern = jnp.transpose(w)[:, None, :].astype(x.dtype)
    return lax.conv_general_dilated(x, kern, window_strides=(1,), padding=[(k_w - 1, 0)],
                                    dimension_numbers=('NWC', 'WIO', 'NWC'),
                                    feature_group_count=x.shape[-1])


def rotary(x, pos):
    half = x.shape[-1] // 2
    inv_freq = 1.0 / (ROPE_BASE ** jnp.linspace(0.0, 1.0, half, dtype=jnp.float32))
    ang = pos[:, None] * inv_freq[None, :]
    cos = jnp.cos(ang)[None, :, None, :]
    sin = jnp.sin(ang)[None, :, None, :]
    x1, x2 = x[..., :half], x[..., half:]
    return jnp.concatenate([x1 * cos - x2 * sin, x2 * cos + x1 * sin], axis=-1)


def to_chunks(t, c):
    b_, l_ = t.shape[:2]
    t = t.reshape(b_, l_ // c, c, *t.shape[2:])
    return jnp.swapaxes(t, 2, 3)


def from_chunks(t):
    t = jnp.swapaxes(t, 2, 3)
    return t.reshape(t.shape[0], -1, *t.shape[3:])


def gated_delta_rule(q, k, v, beta, g):
    c = LA_CHUNK
    dk = q.shape[-1]
    dv = v.shape[-1]
    q, k, v, beta, g = (to_chunks(t, c) for t in (q * dk ** -0.5, k, v, beta, g))
    g = jnp.cumsum(g, axis=-1)
    causal = jnp.tril(jnp.ones((c, c), dtype=bool))
    strict = jnp.tril(jnp.ones((c, c), dtype=bool), k=-1)
    decay = jnp.exp(jnp.where(causal, g[..., :, None] - g[..., None, :], -jnp.inf))
    kb = k * beta[..., None]
    a = jnp.where(strict, jnp.einsum('bnhtk,bnhsk->bnhts', kb, k) * decay, 0.0)
    eye = jnp.eye(c, dtype=q.dtype)
    t_inv = lax.linalg.triangular_solve(a + eye, jnp.broadcast_to(eye, a.shape), left_side=True,
                                        lower=True, unit_diagonal=True)
    u = jnp.einsum('bnhts,bnhsv->bnhtv', t_inv, v * beta[..., None])
    w = jnp.einsum('bnhts,bnhsk->bnhtk', t_inv, kb * jnp.exp(g)[..., None])
    qk = jnp.where(causal, jnp.einsum('bnhtk,bnhsk->bnhts', q, k) * decay, 0.0)
    q_dec = q * jnp.exp(g)[..., None]
    k_tail = k * jnp.exp(g[..., -1:] - g)[..., None]
    chunk_decay = jnp.exp(g[..., -1])

    def step(state, xs):
        q_c, k_c, u_c, w_c, qk_c, d_c = xs
        v_new = u_c - jnp.einsum('bhtk,bhkv->bhtv', w_c, state)
        o = jnp.einsum('bhtk,bhkv->bhtv', q_c, state) + jnp.einsum('bhts,bhsv->bhtv', qk_c, v_new)
        state = state * d_c[..., None, None] + jnp.einsum('bhtk,bhtv->bhkv', k_c, v_new)
        return state, o

    b_, _, h_ = g.shape[:3]
    state0 = jnp.zeros((b_, h_, dk, dv), q.dtype)
    xs = tuple(jnp.moveaxis(t, 1, 0) for t in (q_dec, k_tail, u, w, qk, chunk_decay))
    _, o = lax.scan(step, state0, xs)
    return from_chunks(jnp.moveaxis(o, 0, 1))


def retention(q, k, v, log_gamma):
    c = LA_CHUNK
    dk = q.shape[-1]
    dv = v.shape[-1]
    q, k, v = (to_chunks(t, c) for t in (q, k, v))
    pos = jnp.arange(c, dtype=jnp.float32)
    causal = jnp.tril(jnp.ones((c, c), dtype=bool))
    lg = log_gamma[:, None]
    d_mat = jnp.exp(jnp.where(causal, (pos[:, None] - pos[None, :]) * log_gamma[:, None, None], -jnp.inf))
    inner = jnp.einsum('bnhts,bnhsv->bnhtv', jnp.einsum('bnhtk,bnhsk->bnhts', q, k) * d_mat, v)
    q_dec = q * jnp.exp((pos + 1.0) * lg)[..., None]
    k_dec = k * jnp.exp((c - 1.0 - pos) * lg)[..., None]
    chunk_decay = jnp.exp(c * log_gamma)[:, None, None]

    def step(state, xs):
        q_c, k_c, v_c = xs
        o = jnp.einsum('bhtk,bhkv->bhtv', q_c, state)
        state = state * chunk_decay + jnp.einsum('bhtk,bhtv->bhkv', k_c, v_c)
        return state, o

    state0 = jnp.zeros((q.shape[0], q.shape[2], dk, dv), q.dtype)
    xs = tuple(jnp.moveaxis(t, 1, 0) for t in (q_dec, k_dec, v))
    _, cross = lax.scan(step, state0, xs)
    return from_chunks(inner + jnp.moveaxis(cross, 0, 1))


def linear_attention_mixer(h, w_in, conv_w, a_log, dt_bias, out_norm_w, w_out):
    f32 = jnp.float32
    b_, l_, _ = h.shape
    proj = h @ w_in
    sizes = [GDN_QK, GDN_QK, GDN_V, GDN_V, GDN_HEADS, GDN_HEADS, RET_QK, RET_QK, RET_V, RET_V]
    cuts = [int(s) for s in np.cumsum(sizes)[:-1]]
    gq, gk, gv, gz, gb, ga, rq, rk, rv, rg = jnp.split(proj, cuts, axis=-1)

    qkv = jax.nn.silu(causal_conv(jnp.concatenate([gq, gk, gv], axis=-1), conv_w))
    gq, gk, gv = jnp.split(qkv, [GDN_QK, 2 * GDN_QK], axis=-1)
    q = l2norm(gq.astype(f32).reshape(b_, l_, GDN_HEADS, GDN_DK))
    k = l2norm(gk.astype(f32).reshape(b_, l_, GDN_HEADS, GDN_DK))
    v = gv.astype(f32).reshape(b_, l_, GDN_HEADS, GDN_DV)
    beta = jax.nn.sigmoid(gb.astype(f32))
    g = -jnp.exp(a_log.astype(f32)) * jax.nn.softplus(ga.astype(f32) + dt_bias.astype(f32))
    o_a = gated_delta_rule(q, k, v, beta, g)
    z = gz.astype(f32).reshape(b_, l_, GDN_HEADS, GDN_DV)
    o_a = head_rmsnorm(o_a) * out_norm_w.astype(f32) * jax.nn.silu(z)
    o_a = o_a.reshape(b_, l_, GDN_V)

    pos = jnp.arange(l_, dtype=f32)
    log_gamma = jnp.log1p(-jnp.power(2.0, -5.0 - jnp.arange(RET_HEADS, dtype=f32)))
    q = rotary(rq.astype(f32).reshape(b_, l_, RET_HEADS, RET_DK), pos)
    k = rotary(rk.astype(f32).reshape(b_, l_, RET_HEADS, RET_DK), pos) * RET_DK ** -0.5
    v = rv.astype(f32).reshape(b_, l_, RET_HEADS, RET_DV)
    o_b = head_rmsnorm(retention(q, k, v, log_gamma)).reshape(b_, l_, RET_V)
    o_b = jax.nn.silu(rg.astype(f32)) * o_b

    o = jnp.concatenate([o_a, o_b], axis=-1).astype(h.dtype)
    return o @ w_out


def spatial_gating_mixer(h, w_in, ln_w, ln_b, w_s, b_s, w_out):
    b_, l_, _ = h.shape
    proj = jax.nn.gelu(h @ w_in)
    u, v = jnp.split(proj, 2, axis=-1)
    v = layernorm(v, ln_w, ln_b)
    vc = v.reshape(b_, l_ // SG_CHUNK, SG_CHUNK, SG_GROUPS, SG_GROUP_DIM)
    causal = jnp.tril(jnp.ones((SG_CHUNK, SG_CHUNK), dtype=bool))
    ws = jnp.where(causal, w_s, 0.0)
    s = jnp.einsum('gts,bnsgd->bntgd', ws, vc) + jnp.swapaxes(b_s, 0, 1)[:, :, None]
    s = s.reshape(b_, l_, SG_WIDTH)
    return (u * s) @ w_out


def squared_relu_mlp(h, w_up, w_down):
    return jnp.square(jax.nn.relu(h @ w_up)) @ w_down


def setup_inputs(seed: int = 0) -> dict:
    key = jax.random.key(seed)
    ks = jax.random.split(key, 16)
    f32 = jnp.float32

    def dense(k, shape, fan_in):
        return jax.random.normal(k, shape, f32) * fan_in ** -0.5

    x = jax.random.normal(ks[0], (BATCH, SEQ, D_MODEL), f32)
    norm_w = 1.0 + 0.1 * jax.random.normal(ks[1], (DEPTH, 4, D_MODEL), f32)
    la_w_in = dense(ks[2], (N_EVEN, D_MODEL, MIX_IN), D_MODEL)
    la_conv_w = dense(ks[3], (N_EVEN, 2 * GDN_QK + GDN_V, CONV_WIDTH), CONV_WIDTH)
    la_a_log = jnp.log(jax.random.uniform(ks[4], (N_EVEN, GDN_HEADS), f32, 1.0, 16.0))
    dt = jnp.exp(jax.random.uniform(ks[5], (N_EVEN, GDN_HEADS), f32) * (math.log(0.1) - math.log(0.001))
                 + math.log(0.001))
    la_dt_bias = dt + jnp.log(-jnp.expm1(-dt))
    la_out_norm_w = 1.0 + 0.1 * jax.random.normal(ks[6], (N_EVEN, GDN_DV), f32)
    la_w_out = dense(ks[7], (N_EVEN, MIX_OUT, D_MODEL), MIX_OUT)
    sg_w_in = dense(ks[8], (N_ODD, D_MODEL, 2 * SG_WIDTH), D_MODEL)
    sg_ln_w = 1.0 + 0.1 * jax.random.normal(ks[9], (N_ODD, SG_WIDTH), f32)
    sg_ln_b = 0.02 * jax.random.normal(ks[10], (N_ODD, SG_WIDTH), f32)
    sg_w_s = dense(ks[11], (N_ODD, SG_GROUPS, SG_CHUNK, SG_CHUNK), SG_CHUNK)
    sg_b_s = 1.0 + 0.1 * jax.random.normal(ks[12], (N_ODD, SG_GROUPS, SG_CHUNK), f32)
    sg_w_out = dense(ks[13], (N_ODD, SG_WIDTH, D_MODEL), SG_WIDTH)
    ffn_w_up = dense(ks[14], (DEPTH, D_MODEL, FFN_HIDDEN), D_MODEL)
    ffn_w_down = dense(ks[15], (DEPTH, FFN_HIDDEN, D_MODEL), FFN_HIDDEN)
    return {"x": x, "norm_w": norm_w, "la_w_in": la_w_in, "la_conv_w": la_conv_w,
            "la_a_log": la_a_log, "la_dt_bias": la_dt_bias, "la_out_norm_w": la_out_norm_w,
            "la_w_out": la_w_out, "sg_w_in": sg_w_in, "sg_ln_w": sg_ln_w, "sg_ln_b": sg_ln_b,
            "sg_w_s": sg_w_s, "sg_b_s": sg_b_s, "sg_w_out": sg_w_out,
            "ffn_w_up": ffn_w_up, "ffn_w_down": ffn_w_down}


def reference(x, norm_w, la_w_in, la_conv_w, la_a_log, la_dt_bias, la_out_norm_w, la_w_out,
              sg_w_in, sg_ln_w, sg_ln_b, sg_w_s, sg_b_s, sg_w_out, ffn_w_up, ffn_w_down):
    h = x
    for layer in range(DEPTH):
        i = layer // 2
        y = rmsnorm(h, norm_w[layer, 0])
        if layer % 2 == 0:
            y = linear_attention_mixer(y, la_w_in[i], la_conv_w[i], la_a_log[i], la_dt_bias[i],
                                       la_out_norm_w[i], la_w_out[i])
        else:
            y = spatial_gating_mixer(y, sg_w_in[i], sg_ln_w[i], sg_ln_b[i], sg_w_s[i], sg_b_s[i],
                                     sg_w_out[i])
        h = h + rmsnorm(y, norm_w[layer, 1])
        y = squared_relu_mlp(rmsnorm(h, norm_w[layer, 2]), ffn_w_up[layer], ffn_w_down[layer])
        h = h + rmsnorm(y, norm_w[layer, 3])
    return h
```

```python
import contextlib
import numpy as np
import concourse.bass as bass
import concourse.mybir as mybir
from concourse.bass_utils import run_bass_kernel_spmd

F32 = mybir.dt.float32
BF16 = mybir.dt.bfloat16
ALU = mybir.AluOpType
AF = mybir.ActivationFunctionType
AX = mybir.AxisListType

ENGS = ['tensor', 'vector', 'scalar', 'gpsimd', 'sync']
SAME_ENG_SYNC = True


class Buf:
    def __init__(self, name, multi=False):
        self.name = name
        self.w = None
        self.r = {}
        self.multi = multi
        self.sem = None
        self.semcnt = 0


class Sched:
    def __init__(self, nc, stack, semstack=None, prefix=""):
        self.nc = nc
        self.stack = stack
        self.semstack = semstack if semstack is not None else stack
        self.prefix = prefix
        self.ops = {e: [] for e in ENGS}
        self.seen = {e: {} for e in ENGS}
        self.esem = {e: self.semstack.enter_context(nc.semaphore(prefix + "prog_" + e)) for e in ENGS}
        self.dma_bufs = []
        self.nsem = len(ENGS)

    def sb(self, name, shape, dtype=F32, multi=False):
        t = self.stack.enter_context(self.nc.sbuf_tensor(self.prefix + "s_" + name, list(shape), dtype))
        b = Buf(name, multi)
        return t, b

    def ps(self, name, shape, dtype=F32):
        t = self.stack.enter_context(self.nc.psum_tensor(self.prefix + "p_" + name, list(shape), dtype))
        return t, Buf(name)

    def _dsem(self, b):
        if b.sem is None:
            b.sem = self.semstack.enter_context(self.nc.semaphore(self.prefix + "d_" + b.name))
            self.dma_bufs.append(b)
            self.nsem += 1
        return b.sem

    def _add(self, eng, fn, reads, writes, dma_buf=None, inc=16):
        needs = []
        for b in reads:
            if b.w is not None:
                needs.append(b.w)
        for b in writes:
            if b.w is not None and not (dma_buf is not None and b.multi):
                needs.append(b.w)
            needs.extend(b.r.values())
        waits = []
        seen = self.seen[eng]
        for ev in needs:
            if ev[0] == 'e' and ev[1] == eng and (eng == 'tensor' or not SAME_ENG_SYNC):
                continue
            key = (ev[0], ev[1] if ev[0] == 'e' else id(ev[1]))
            if seen.get(key, -1) >= ev[2]:
                continue
            seen[key] = ev[2]
            waits.append(ev)
            if ev[0] == 'e':
                self.ops[ev[1]][ev[2]]['ms'] = True
        idx = len(self.ops[eng])
        if dma_buf is not None:
            sem = self._dsem(dma_buf)
            dma_buf.semcnt += inc
            event = ('d', sem, dma_buf.semcnt)
            rkey = ('d', id(sem))
        else:
            sem = None
            event = ('e', eng, idx)
            rkey = ('e', eng)
        self.ops[eng].append(dict(fn=fn, waits=waits, ms=False, dma=sem, inc=inc))
        for b in reads:
            b.r[rkey] = event
        for b in writes:
            b.w = event
            b.r = {}
        return event

    def op(self, eng, fn, reads=(), writes=()):
        return self._add(eng, fn, list(reads), list(writes))

    def dma(self, eng, out, in_, reads=(), writes=(), sembuf=None):
        reads = list(reads)
        writes = list(writes)
        if sembuf is None:
            sembuf = writes[0] if writes else reads[0]
        return self._add(eng, lambda e: e.dma_start(out=out, in_=in_), reads, writes, dma_buf=sembuf)

    def mm(self, out, lhsT, rhs, start, stop, reads, writes):
        return self.op('tensor', lambda e: e.matmul(out, lhsT, rhs, start=start, stop=stop), reads, writes)

    def tr(self, out, in_, ident, reads, writes):
        return self.op('tensor', lambda e: e.transpose(out, in_, ident), reads, writes)

    def act(self, out, in_, func, reads, writes, eng='scalar', **kw):
        return self.op(eng, lambda e: e.activation(out, in_, func, **kw), reads, writes)

    def tt(self, eng, out, in0, in1, op, reads, writes):
        return self.op(eng, lambda e: e.tensor_tensor(out, in0, in1, op), reads, writes)

    def ts(self, eng, out, in0, s1, s2, op0, op1, reads, writes):
        if op1 is None:
            return self.op(eng, lambda e: e.tensor_scalar(out, in0, s1, None, op0), reads, writes)
        return self.op(eng, lambda e: e.tensor_scalar(out, in0, s1, s2, op0, op1), reads, writes)

    def stt(self, out, in0, scalar, in1, op0, op1, reads, writes, eng='vector'):
        return self.op(eng, lambda e: e.scalar_tensor_tensor(out, in0, scalar, in1, op0, op1), reads, writes)

    def copy(self, eng, out, in_, reads, writes):
        if eng == 'scalar':
            return self.op(eng, lambda e: e.copy(out, in_), reads, writes)
        return self.op(eng, lambda e: e.tensor_copy(out, in_), reads, writes)

    def finish(self, eng='sync', barrier=False):
        waits = [('d', b.sem, b.semcnt) for b in self.dma_bufs if b.semcnt > 0]
        if not barrier:
            self.ops[eng].append(dict(fn=None, waits=waits, ms=False, dma=None, inc=0))
            return
        last = {}
        for e in ENGS:
            idx = None
            for i in range(len(self.ops[e]) - 1, -1, -1):
                o = self.ops[e][i]
                if o['fn'] is not None and o['dma'] is None:
                    idx = i
                    break
            if idx is not None:
                self.ops[e][idx]['ms'] = True
                last[e] = idx
        for e in ENGS:
            w = list(waits) + [('e', e2, i2) for e2, i2 in last.items() if e2 != e]
            self.ops[e].append(dict(fn=None, waits=w, ms=False, dma=None, inc=0))

    def emit(self):
        rank = {}
        for eng in ENGS:
            c = 0
            r = []
            for o in self.ops[eng]:
                if o['ms']:
                    c += 1
                r.append(c)
            rank[eng] = r
        self.rank = rank
        esem = self.esem

        def make(eng):
            ops = self.ops[eng]

            def body(e):
                for o in ops:
                    for ev in o['waits']:
                        if ev[0] == 'e':
                            e.wait_ge(esem[ev[1]], rank[ev[1]][ev[2]])
                        else:
                            e.wait_ge(ev[1], ev[2])
                    if o['fn'] is None:
                        continue
                    ins = o['fn'](e)
                    if o['dma'] is not None:
                        ins.then_inc(o['dma'], o['inc'])
                    elif o['ms']:
                        ins.then_inc(esem[eng], 1)
            return body

        with self.nc.Block() as block:
            block.tensor(make('tensor'))
            block.vector(make('vector'))
            block.scalar(make('scalar'))
            block.gpsimd(make('gpsimd'))
            block.sync(make('sync'))

    def stats(self):
        return {e: (len(self.ops[e]), sum(1 for o in self.ops[e] if o['ms'])) for e in ENGS}


P1_TG = 512
P1_NT = 4
P1_D = 2048
P1_EPS = 1e-6
P1_ISQ = float(128 ** -0.5)


class P1_Ctx:
    pass


P1_STAGE = [99]


class P1_StopBuild(Exception):
    pass


def P1_chk(k):
    if P1_STAGE[0] == k:
        raise P1_StopBuild()


def p1_dram(nc, L, fused=False):
    dr = {}

    def inp(name, shape):
        dr[name] = nc.dram_tensor(name, list(shape), F32, kind="ExternalInput").ap()
    inp('x', [L, P1_D]); inp('w_loc', [2048, 2304]); inp('nw0T', [128, 16]); inp('cw', [128, 24])
    inp('nexpA_in', [2, 1]); inp('dtb', [2, 1]); inp('onw', [128, 1])
    inp('ident', [128, 128]); inp('triU', [128, 128]); inp('SL', [128, 128]); inp('negstrict', [128, 128])
    inp('permT', [128, 128]); inp('DT', [128, 256]); inp('qdrow', [128, 1024]); inp('kdcol', [128, 4])
    inp('sel', [2, 256]); inp('cosT', [128, L]); inp('sinT', [128, L])
    if fused:
        dr['oloc'] = nc.dram_tensor("oloc", [L // 512, 512, 512], F32, kind="Internal").ap()
        dr['ogath'] = nc.dram_tensor("ogath", [L // 512, 2048, 512], F32, kind="Internal").ap()
    else:
        dr['oT'] = nc.dram_tensor("oT", [512, L], F32, kind="ExternalOutput").ap()
    return dr


def p1_alloc(S, nc):
    C = P1_Ctx()
    C.S = S
    sb = S.sb
    C.consts = Buf("consts", multi=True)
    for nm, shp in [('ident', [128, 128]), ('triU', [128, 128]), ('SL', [128, 128]), ('negstrict', [128, 128]),
                    ('permT', [128, 128]), ('DT', [128, 2, 128]), ('qdrow', [128, 2, 512]), ('kdcol', [128, 4]),
                    ('sel', [2, 2, 128]), ('nw0T', [128, 16]), ('cw', [128, 6, 4]), ('nexpA', [2, 1]), ('dtb', [2, 1]),
                    ('onw', [128, 1])]:
        t, _ = sb(nm, shp, F32)
        setattr(C, nm, t)
    C.ones_f, C.ones_f_b = sb("ones_f", [128, 128], F32)
    C.xt = [sb(f"xt{i}", [128, P1_D], F32) for i in range(2)]
    C.XN, C.XN_b = sb("XN", [128, 16, P1_TG], BF16)
    C.wb = [sb(f"wb{i}", [128, 16, 256], BF16) for i in range(4)]
    C.PRc, C.PRc_b = sb("PRc", [128, 6, P1_TG + 3], F32)
    C.PRo, C.PRo_b = sb("PRo", [128, 10, P1_TG], F32)
    C.CV = [sb(f"CV{i}", [128, P1_TG], F32) for i in range(6)]
    C.SZ = [sb(f"SZ{i}", [128, P1_TG], F32) for i in range(4)]
    C.cs = [sb(f"cs{i}", [128, P1_TG], F32) for i in range(2)]
    C.RQ = [sb(f"RQ{i}", [128, P1_TG], F32) for i in range(4)]
    C.QD = [sb(f"QD{i}", [128, P1_TG], F32) for i in range(2)]
    C.BBC = [sb(f"BBC{i}", [128, P1_TG], F32) for i in range(2)]
    C.t512 = [sb(f"t512_{i}", [128, P1_TG], F32) for i in range(3)]
    C.bg, C.bg_b = sb("bg", [2, 3, P1_TG], F32)
    C.bcol, C.bcol_b = sb("bcol", [128, P1_NT, 2], F32)
    C.gcol, C.gcol_b = sb("gcol", [128, P1_NT, 2], F32)
    C.sm = [sb(f"sm{i}", [128, 16], F32) for i in range(P1_NT)]
    C.xs, C.xs_b = sb("xs", [128, 8], F32)
    names = ['GT', 'E', 'DTi', 'NDs', 'egc', 'Kbg', 'Ktail', 'Vb', 'KbT', 'Ma', 'Mb', 'MTa', 'MTb', 'P', 'QKT', 'QdT',
             'negWT', 'On']
    C.slot = {}
    for nm in names:
        C.slot[nm] = [sb(f"{nm}{j}", [128, 128], F32) for j in range(P1_NT)]
    C.Vn = [sb(f"Vn{i}", [128, 128], F32) for i in range(2)]
    C.junk = [sb(f"junk{i}", [128, 128], F32) for i in range(2)]
    C.St = [sb(f"S{i}", [128, 128], F32) for i in range(4)]
    C.OUT, C.OUT_b = sb("OUT", [128, 4, P1_TG], F32)
    C.ps = []
    C.psq = []
    for i in range(8):
        t, bb = S.ps(f"ps{i}", [128, 512], F32)
        C.ps.append(t)
        C.psq.append([bb])
    C.oT_b = None
    C.bank_i = 0
    C.q_i = 0
    C.w_i = 0
    C.i2 = 0
    C.ev_i = 0
    return C


def P1_next_bank(C):
    b = C.bank_i % 4
    C.bank_i += 1
    return C.ps[b], C.psq[b]


def P1_next_q(C):
    i = C.q_i % 16
    C.q_i += 1
    b = 4 + i % 4
    q = i // 4
    return C.ps[b][:, q * 128:(q + 1) * 128], C.psq[b][0]


def P1_next_w(C):
    w = C.wb[C.w_i % 4]
    C.w_i += 1
    return w


def P1_rot2(C, lst):
    C.i2 += 1
    return lst[C.i2 % 2]


def P1_ev_eng(C):
    C.ev_i += 1
    return 'scalar' if C.ev_i % 2 else 'vector'


def p1_setup(C, dr):
    S = C.S
    cd = lambda t, src: S.dma('sync', t, src, writes=[C.consts])
    cd(C.ident[:], dr['ident']); cd(C.triU[:], dr['triU']); cd(C.SL[:], dr['SL']); cd(C.negstrict[:], dr['negstrict'])
    cd(C.permT[:], dr['permT']); cd(C.DT[:], dr['DT'].rearrange("p (h t) -> p h t", h=2))
    cd(C.qdrow[:], dr['qdrow'].rearrange("p (h t) -> p h t", h=2)); cd(C.kdcol[:], dr['kdcol'])
    cd(C.sel[:], dr['sel'].rearrange("p (h t) -> p h t", h=2)); cd(C.nw0T[:], dr['nw0T'])
    cd(C.cw[:], dr['cw'].rearrange("p (i k) -> p i k", i=6)); cd(C.nexpA[:], dr['nexpA_in']); cd(C.dtb[:], dr['dtb'])
    cd(C.onw[:], dr['onw'])
    S.op('vector', lambda e: e.memset(C.ones_f[:], 1.0), writes=[C.ones_f_b])
    S.act(C.nexpA[:], C.nexpA[:], AF.Exp, [C.consts], [C.consts])
    S.ts('vector', C.nexpA[:], C.nexpA[:], -1.0, None, ALU.mult, None, [C.consts], [C.consts])
    for i in range(4):
        S.op('vector', lambda e, i=i: e.memset(C.St[i][0][:], 0.0), writes=[C.St[i][1]])
    S.op('vector', lambda e: e.memset(C.PRc[:, :, 0:3], 0.0), writes=[C.PRc_b])


def P1_head_epilogue_a(C, psO, psO_b, j):
    S = C.S
    jk, jk_b = P1_rot2(C, C.junk)
    sm, sm_b = C.sm[j]
    on, on_b = C.slot['On'][j]
    S.act(jk[:], psO, AF.Square, [psO_b], [jk_b, sm_b], accum_out=sm[:, 8:9])
    S.ts('vector', sm[:, 9:10], sm[:, 8:9], 1.0 / 128, P1_EPS, ALU.mult, ALU.add, [sm_b], [sm_b])
    S.act(sm[:, 9:10], sm[:, 9:10], AF.Sqrt, [sm_b], [sm_b])
    S.op('vector', lambda e: e.reciprocal(sm[:, 9:10], sm[:, 9:10]), [sm_b], [sm_b])
    S.ts('vector', on[:], psO, sm[:, 9:10], None, ALU.mult, None, [psO_b, sm_b], [on_b])


def P1_head_epilogue_b(C, gate, gate_b, oidx, j, use_onw):
    S = C.S
    js = slice(j * 128, (j + 1) * 128)
    on, on_b = C.slot['On'][j]
    pq, pq_b = P1_next_q(C)
    S.tr(pq, on[:], C.ident[:], [on_b, C.consts], [pq_b])
    if use_onw:
        S.stt(C.OUT[:, oidx, js], pq, C.onw[:, 0:1], gate[:, js], ALU.mult, ALU.mult, [pq_b, C.consts, gate_b], [C.OUT_b])
    else:
        S.tt('vector', C.OUT[:, oidx, js], pq, gate[:, js], ALU.mult, [pq_b, gate_b], [C.OUT_b])


def p1_group(C, dr, gi, L, hook=None):
    S = C.S
    t0 = gi * P1_TG
    sl = C.slot
    for j in range(P1_NT):
        xt, xt_b = C.xt[j % 2]
        S.dma('sync', xt[:], dr['x'][t0 + j * 128:t0 + (j + 1) * 128, :], writes=[xt_b])
        jt, junk_b = P1_next_w(C)
        junk = jt[:].rearrange("p a b -> p (a b)")[:, 0:P1_D]
        S.act(junk, xt[:], AF.Square, [xt_b], [junk_b, C.xs_b], accum_out=C.xs[:, j:j + 1])
        S.ts('vector', C.xs[:, 4 + j:5 + j], C.xs[:, j:j + 1], 1.0 / P1_D, P1_EPS, ALU.mult, ALU.add, [C.xs_b], [C.xs_b])
        S.act(C.xs[:, 4 + j:5 + j], C.xs[:, 4 + j:5 + j], AF.Sqrt, [C.xs_b], [C.xs_b])
        S.op('vector', lambda e, j=j: e.reciprocal(C.xs[:, 4 + j:5 + j], C.xs[:, 4 + j:5 + j]), [C.xs_b], [C.xs_b])
        S.ts('vector', xt[:], xt[:], C.xs[:, 4 + j:5 + j], None, ALU.mult, None, [xt_b, C.xs_b], [xt_b])
        for c4 in range(4):
            pt, pbs = P1_next_bank(C)
            for q in range(4):
                c = c4 * 4 + q
                S.tr(pt[:, q * 128:(q + 1) * 128], xt[:, c * 128:(c + 1) * 128], C.ident[:], [xt_b, C.consts], pbs)
            S.tt('vector', C.XN[:, c4 * 4:(c4 + 1) * 4, j * 128:(j + 1) * 128],
                 pt[:].rearrange("p (q n) -> p q n", q=4),
                 C.nw0T[:, c4 * 4:(c4 + 1) * 4].unsqueeze(2).to_broadcast([128, 4, 128]), ALU.mult,
                 pbs + [C.consts], [C.XN_b])
    P1_chk(1)
    S.dma('sync', C.cs[0][0][:], dr['cosT'][:, t0:t0 + P1_TG], writes=[C.cs[0][1]])
    S.dma('sync', C.cs[1][0][:], dr['sinT'][:, t0:t0 + P1_TG], writes=[C.cs[1][1]])
    def main_block(cb):
        banks = [P1_next_bank(C), P1_next_bank(C)]
        wt, wbuf = P1_next_w(C)
        S.dma('gpsimd', wt[:], dr['w_loc'][:, cb * 256:(cb + 1) * 256].rearrange("(kc p) n -> p kc n", p=128),
              writes=[wbuf])
        for cc in range(2):
            pt, pbs = banks[cc]
            for kc in range(16):
                S.mm(pt[:, :P1_TG], wt[:, kc, cc * 128:(cc + 1) * 128], C.XN[:, kc, :], kc == 0, kc == 15,
                     [wbuf, C.XN_b], pbs)
        for cc in range(2):
            pt, pbs = banks[cc]
            ci = cb * 2 + cc
            if ci < 6:
                S.copy(P1_ev_eng(C), C.PRc[:, ci, 3:3 + P1_TG], pt[:, :P1_TG], pbs, [C.PRc_b])
            else:
                S.copy('scalar', C.PRo[:, ci - 6, :], pt[:, :P1_TG], pbs, [C.PRo_b])

    sbanks = [P1_next_bank(C), P1_next_bank(C)]
    wt, wbuf = P1_next_w(C)
    S.dma('gpsimd', wt[:], dr['w_loc'][:, 2048:2304].rearrange("(kc p) n -> p kc n", p=128), writes=[wbuf])
    for cc in range(2):
        pt, pbs = sbanks[cc]
        for kc in range(16):
            S.mm(pt[0:2, :P1_TG], wt[:, kc, cc * 2:cc * 2 + 2], C.XN[:, kc, :], kc == 0, kc == 15, [wbuf, C.XN_b], pbs)
    S.act(C.bg[:, 0, :], sbanks[0][0][0:2, :P1_TG], AF.Sigmoid, sbanks[0][1], [C.bg_b])
    S.act(C.bg[:, 2, :], sbanks[1][0][0:2, :P1_TG], AF.Exp, sbanks[1][1] + [C.consts], [C.bg_b], bias=C.dtb[:, 0:1])
    S.act(C.bg[:, 2, :], C.bg[:, 2, :], AF.Ln, [C.bg_b], [C.bg_b], bias=1.0)
    S.ts('vector', C.bg[:, 1, :], C.bg[:, 2, :], C.nexpA[:, 0:1], None, ALU.mult, None, [C.bg_b, C.consts], [C.bg_b])
    for cb in range(3):
        main_block(cb)
    if hook is not None:
        hook()
    P1_chk(3)
    for i in range(6):
        cv, cv_b = C.CV[i]
        S.ts('vector', cv[:], C.PRc[:, i, 3:3 + P1_TG], C.cw[:, i, 3:4], None, ALU.mult, None, [C.PRc_b, C.consts], [cv_b])
        for k in (2, 1, 0):
            S.stt(cv[:], C.PRc[:, i, k:k + P1_TG], C.cw[:, i, k:k + 1], cv[:], ALU.mult, ALU.add,
                  [C.PRc_b, C.consts, cv_b], [cv_b])
        S.act(cv[:], cv[:], AF.Silu, [cv_b], [cv_b])
    S.copy('vector', C.PRc[:, :, 0:3], C.PRc[:, :, P1_TG:P1_TG + 3], [C.PRc_b], [C.PRc_b])
    for cb in range(3, 8):
        main_block(cb)
    for j in range(P1_NT):
        pq, pq_b = P1_next_q(C)
        S.tr(pq[:, 0:2], C.bg[:, 0, j * 128:(j + 1) * 128], C.ident[0:2, 0:2], [C.bg_b, C.consts], [pq_b])
        S.tr(pq[:, 2:4], C.bg[:, 1, j * 128:(j + 1) * 128], C.ident[0:2, 0:2], [C.bg_b, C.consts], [pq_b])
        S.copy('vector', C.bcol[:, j, :], pq[:, 0:2], [pq_b], [C.bcol_b])
        S.copy('vector', C.gcol[:, j, :], pq[:, 2:4], [pq_b], [C.gcol_b])
    for h in range(2):
        pt, pbs = P1_next_bank(C)
        S.mm(pt[:, :P1_TG], C.sel[:, h, :], C.bg[:, 0, :], True, True, [C.consts, C.bg_b], pbs)
        S.copy('vector', C.BBC[h][0][:], pt[:, :P1_TG], pbs, [C.BBC[h][1]])
    for h in range(2):
        S.act(C.SZ[h][0][:], C.PRo[:, h, :], AF.Silu, [C.PRo_b], [C.SZ[h][1]])
        S.act(C.SZ[2 + h][0][:], C.PRo[:, 8 + h, :], AF.Silu, [C.PRo_b], [C.SZ[2 + h][1]])
    sqb = [C.RQ[0], C.RQ[1], C.RQ[2], C.RQ[3]]
    rnb = [C.QD[0], C.QD[1], C.t512[1], C.t512[2]]
    lbanks = []
    for i in range(4):
        cv, cv_b = C.CV[i]
        S.act(sqb[i][0][:], cv[:], AF.Square, [cv_b], [sqb[i][1]])
    for i in range(4):
        pt, pbs = P1_next_bank(C)
        lbanks.append((pt, pbs))
        S.mm(pt[:, :P1_TG], C.ones_f[:], sqb[i][0][:], True, True, [C.ones_f_b, sqb[i][1]], pbs)
    for i in range(4):
        pt, pbs = lbanks[i]
        S.ts('vector', rnb[i][0][:], pt[:, :P1_TG], 1e-6, None, ALU.add, None, pbs, [rnb[i][1]])
    for i in range(4):
        S.act(rnb[i][0][:], rnb[i][0][:], AF.Sqrt, [rnb[i][1]], [rnb[i][1]])
    for i in range(4):
        cv, cv_b = C.CV[i]
        rn, rn_b = rnb[i]
        S.op('vector', lambda e, rn=rn: e.reciprocal(rn[:], rn[:]), [rn_b], [rn_b])
        if i < 2:
            S.stt(cv[:], cv[:], P1_ISQ, rn[:], ALU.mult, ALU.mult, [cv_b, rn_b], [cv_b])
        else:
            S.tt('vector', cv[:], cv[:], rn[:], ALU.mult, [cv_b, rn_b], [cv_b])
    P1_chk(4)
    for h in range(2):
        qT, qT_b = C.CV[h]
        kT, kT_b = C.CV[2 + h]
        vT, vT_b = C.CV[4 + h]
        St, St_b = C.St[h]
        JS = [slice(j * 128, (j + 1) * 128) for j in range(P1_NT)]
        for j in range(P1_NT):
            GT, GT_b = sl['GT'][j]
            S.ts('vector', GT[:], C.triU[:], C.gcol[:, j, h:h + 1], None, ALU.mult, None, [C.consts, C.gcol_b], [GT_b])
            KbT, KbT_b = sl['KbT'][j]
            S.tt('vector', KbT[:], kT[:, JS[j]], C.BBC[h][0][:, JS[j]], ALU.mult, [kT_b, C.BBC[h][1]], [KbT_b])
        pAs, pBs, pCs = [], [], []
        for j in range(P1_NT):
            GT, GT_b = sl['GT'][j]
            pA, pA_b = P1_next_q(C)
            S.mm(pA, C.SL[:], GT[:], True, True, [C.consts, GT_b], [pA_b])
            pB, pB_b = P1_next_q(C)
            S.mm(pB, C.ones_f[:], GT[:], True, True, [C.ones_f_b, GT_b], [pB_b])
            pC, pC_b = P1_next_q(C)
            S.mm(pC[:, 0:1], GT[:], C.ones_f[:, 0:1], True, True, [C.ones_f_b, GT_b], [pC_b])
            pAs.append((pA, pA_b)); pBs.append((pB, pB_b)); pCs.append((pC, pC_b))
        for j in range(P1_NT):
            sm, sm_b = C.sm[j]
            pA, pA_b = pAs[j]; pB, pB_b = pBs[j]; pC, pC_b = pCs[j]
            E, E_b = sl['E'][j]
            S.act(E[:], pA, AF.Exp, [pA_b], [E_b])
            egc, egc_b = sl['egc'][j]
            S.act(egc[:], pB, AF.Exp, [pB_b], [egc_b])
            S.copy('vector', sm[:, 0:1], pB[:, 127:128], [pB_b], [sm_b])
            S.act(sm[:, 1:2], pC[:, 0:1], AF.Exp, [pC_b], [sm_b])
            S.act(sm[:, 2:3], pC[:, 0:1], AF.Exp, [pC_b, sm_b], [sm_b], scale=-1.0, bias=sm[:, 0:1])
            S.tt('vector', sm[:, 3:4], sm[:, 1:2], C.bcol[:, j, h:h + 1], ALU.mult, [sm_b, C.bcol_b], [sm_b])
            S.tt('vector', sl['DTi'][j][0][:], E[:], C.triU[:], ALU.mult, [E_b, C.consts], [sl['DTi'][j][1]])
            S.tt('vector', sl['NDs'][j][0][:], E[:], C.negstrict[:], ALU.mult, [E_b, C.consts], [sl['NDs'][j][1]])
            S.tt('vector', sl['QdT'][j][0][:], qT[:, JS[j]], egc[:], ALU.mult, [qT_b, egc_b], [sl['QdT'][j][1]])
        pKs, pVs, pPs, pQs = [], [], [], []
        for j in range(P1_NT):
            pK, pK_b = P1_next_q(C)
            S.tr(pK, kT[:, JS[j]], C.ident[:], [kT_b, C.consts], [pK_b])
            pV, pV_b = P1_next_q(C)
            S.tr(pV, vT[:, JS[j]], C.ident[:], [vT_b, C.consts], [pV_b])
            pP, pP_b = P1_next_q(C)
            S.mm(pP, kT[:, JS[j]], sl['KbT'][j][0][:], True, True, [kT_b, sl['KbT'][j][1]], [pP_b])
            pQ, pQ_b = P1_next_q(C)
            S.mm(pQ, kT[:, JS[j]], qT[:, JS[j]], True, True, [kT_b, qT_b], [pQ_b])
            pKs.append((pK, pK_b)); pVs.append((pV, pV_b)); pPs.append((pP, pP_b)); pQs.append((pQ, pQ_b))
        for j in range(P1_NT):
            sm, sm_b = C.sm[j]
            pK, pK_b = pKs[j]; pV, pV_b = pVs[j]; pP, pP_b = pPs[j]; pQ, pQ_b = pQs[j]
            Ma, Ma_b = sl['Ma'][j]
            S.tt('vector', Ma[:], pP, sl['NDs'][j][0][:], ALU.mult, [pP_b, sl['NDs'][j][1]], [Ma_b])
            S.tt('vector', sl['P'][j][0][:], Ma[:], C.ident[:], ALU.add, [Ma_b, C.consts], [sl['P'][j][1]])
            S.ts('vector', sl['Kbg'][j][0][:], pK, sm[:, 3:4], None, ALU.mult, None, [pK_b, sm_b], [sl['Kbg'][j][1]])
            S.ts('vector', sl['Ktail'][j][0][:], pK, sm[:, 2:3], None, ALU.mult, None, [pK_b, sm_b], [sl['Ktail'][j][1]])
            S.ts('vector', sl['Vb'][j][0][:], pV, C.bcol[:, j, h:h + 1], None, ALU.mult, None, [pV_b, C.bcol_b], [sl['Vb'][j][1]])
            S.tt('vector', sl['QKT'][j][0][:], pQ, sl['DTi'][j][0][:], ALU.mult, [pQ_b, sl['DTi'][j][1]], [sl['QKT'][j][1]])
        for j in range(P1_NT):
            Ma, Ma_b = sl['Ma'][j]
            pT, pT_b = P1_next_q(C)
            S.tr(pT, Ma[:], C.ident[:], [Ma_b, C.consts], [pT_b])
            S.copy('scalar', sl['MTa'][j][0][:], pT, [pT_b], [sl['MTa'][j][1]])
        P1_chk(5)
        cur = ('Ma', 'MTa')
        nxt = ('Mb', 'MTb')
        for lvl in range(1, 7):
            for j in range(P1_NT):
                M, M_b = sl[cur[0]][j]
                MT, MT_b = sl[cur[1]][j]
                M2, M2_b = sl[nxt[0]][j]
                MT2, MT2_b = sl[nxt[1]][j]
                if lvl < 6:
                    p1, p1_b = P1_next_q(C)
                    S.mm(p1, MT[:], M[:], True, True, [MT_b, M_b], [p1_b])
                    S.copy('scalar', M2[:], p1, [p1_b], [M2_b])
                p2, p2_b = P1_next_q(C)
                S.mm(p2, M[:], MT[:], True, True, [MT_b, M_b], [p2_b])
                S.copy('vector' if lvl < 6 else 'scalar', MT2[:], p2, [p2_b], [MT2_b])
            for j in range(P1_NT):
                MT2, MT2_b = sl[nxt[1]][j]
                P, P_b = sl['P'][j]
                p3, p3_b = P1_next_q(C)
                S.mm(p3, MT2[:], P[:], True, True, [MT2_b, P_b], [p3_b])
                S.tt('vector', P[:], P[:], p3, ALU.add, [P_b, p3_b], [P_b])
            cur, nxt = nxt, cur
        P1_chk(6)
        for j in range(P1_NT):
            pW, pW_b = P1_next_q(C)
            S.mm(pW, sl['Kbg'][j][0][:], sl['P'][j][0][:], True, True, [sl['Kbg'][j][1], sl['P'][j][1]], [pW_b])
            S.op('scalar', lambda e, j=j, pW=pW: e.mul(sl['negWT'][j][0][:], pW, -1.0), [pW_b], [sl['negWT'][j][1]])
        P1_chk(7)
        for j in range(P1_NT):
            sm, sm_b = C.sm[j]
            TT, TT_b = sl['P'][j]
            pVn, pVn_b = P1_next_q(C)
            S.mm(pVn, TT[:], sl['Vb'][j][0][:], True, False, [TT_b, sl['Vb'][j][1]], [pVn_b])
            S.mm(pVn, sl['negWT'][j][0][:], St[:], False, True, [sl['negWT'][j][1], St_b], [pVn_b])
            Vn, Vn_b = P1_rot2(C, C.Vn)
            S.copy('scalar', Vn[:], pVn, [pVn_b], [Vn_b])
            pO, pO_b = P1_next_q(C)
            S.mm(pO, sl['QdT'][j][0][:], St[:], True, False, [sl['QdT'][j][1], St_b], [pO_b])
            S.mm(pO, sl['QKT'][j][0][:], Vn[:], False, True, [sl['QKT'][j][1], Vn_b], [pO_b])
            pS, pS_b = P1_next_q(C)
            S.mm(pS, sl['Ktail'][j][0][:], Vn[:], True, True, [sl['Ktail'][j][1], Vn_b], [pS_b])
            S.stt(St[:], St[:], sl['egc'][j][0][:, 127:128], pS, ALU.mult, ALU.add, [St_b, sl['egc'][j][1], pS_b], [St_b])
            P1_head_epilogue_a(C, pO, pO_b, j)
            if j > 0:
                P1_head_epilogue_b(C, C.SZ[h][0], C.SZ[h][1], h, j - 1, True)
        P1_head_epilogue_b(C, C.SZ[h][0], C.SZ[h][1], h, P1_NT - 1, True)
    P1_chk(8)
    for h in range(2):
        St, St_b = C.St[2 + h]
        for which in range(2):
            src = C.PRo[:, 2 + 2 * which + h, :]
            rq, rq_b = C.RQ[2 * which + h]
            xs, xs_b = C.t512[0]
            if which == 1:
                S.op('scalar', lambda e, xs=xs, src=src: e.mul(xs[:], src, P1_ISQ), [C.PRo_b], [xs_b])
            else:
                S.copy('scalar', xs[:], src, [C.PRo_b], [xs_b])
            pt, pbs = P1_next_bank(C)
            S.mm(pt[:, :P1_TG], C.permT[:], xs[:], True, True, [C.consts, xs_b], pbs)
            S.tt('vector', rq[:], xs[:], C.cs[0][0][:], ALU.mult, [xs_b, C.cs[0][1]], [rq_b])
            t2, t2_b = C.t512[2]
            S.tt('vector', t2[:], pt[:, :P1_TG], C.cs[1][0][:], ALU.mult, pbs + [C.cs[1][1]], [t2_b])
            S.tt('vector', rq[:], rq[:], t2[:], ALU.add, [rq_b, t2_b], [rq_b])
        qr, qr_b = C.RQ[h]
        kr, kr_b = C.RQ[2 + h]
        vT = C.PRo[:, 6 + h, :]
        qd, qd_b = C.QD[h]
        S.tt('vector', qd[:], qr[:], C.qdrow[:, h, :], ALU.mult, [qr_b, C.consts], [qd_b])
        for j in range(P1_NT):
            js = slice(j * 128, (j + 1) * 128)
            pK, pK_b = P1_next_q(C)
            S.tr(pK, kr[:, js], C.ident[:], [kr_b, C.consts], [pK_b])
            Kd, Kd_b = sl['Kbg'][j]
            S.ts('vector', Kd[:], pK, C.kdcol[:, h:h + 1], None, ALU.mult, None, [pK_b, C.consts], [Kd_b])
            pV, pV_b = P1_next_q(C)
            S.tr(pV, vT[:, js], C.ident[:], [C.PRo_b, C.consts], [pV_b])
            V, V_b = sl['Vb'][j]
            S.copy('vector', V[:], pV, [pV_b], [V_b])
            pQ, pQ_b = P1_next_q(C)
            S.mm(pQ, kr[:, js], qr[:, js], True, True, [kr_b, qr_b], [pQ_b])
            QKT, QKT_b = sl['QKT'][j]
            S.tt('vector', QKT[:], pQ, C.DT[:, h, :], ALU.mult, [pQ_b, C.consts], [QKT_b])
            pO, pO_b = P1_next_q(C)
            S.mm(pO, QKT[:], V[:], True, False, [QKT_b, V_b], [pO_b])
            S.mm(pO, qd[:, js], St[:], False, True, [qd_b, St_b], [pO_b])
            pS, pS_b = P1_next_q(C)
            S.mm(pS, Kd[:], V[:], True, True, [Kd_b, V_b], [pS_b])
            S.stt(St[:], St[:], C.kdcol[:, 2 + h:3 + h], pS, ALU.mult, ALU.add, [St_b, C.consts, pS_b], [St_b])
            P1_head_epilogue_a(C, pO, pO_b, j)
            if j > 0:
                P1_head_epilogue_b(C, C.SZ[2 + h][0], C.SZ[2 + h][1], 2 + h, j - 1, False)
        P1_head_epilogue_b(C, C.SZ[2 + h][0], C.SZ[2 + h][1], 2 + h, P1_NT - 1, False)
    for o in range(4):
        if C.oT_b is not None:
            S.dma('sync', dr['oloc'][gi][o * 128:(o + 1) * 128, :], C.OUT[:, o, :], reads=[C.OUT_b],
                  writes=[C.oT_b], sembuf=C.OUT_b)
        else:
            S.dma('sync', dr['oT'][o * 128:(o + 1) * 128, t0:t0 + P1_TG], C.OUT[:, o, :], reads=[C.OUT_b])


def build_p1(L, ngroups=None):
    nc = bass.Bass("TRN2", target_bir_lowering=False)
    dr = p1_dram(nc, L)
    with contextlib.ExitStack() as st:
        S = Sched(nc, st)
        C = p1_alloc(S, nc)
        p1_setup(C, dr)
        try:
            for gi in range(ngroups if ngroups else L // P1_TG):
                p1_group(C, dr, gi, L)
        except P1_StopBuild:
            pass
        S.finish()
        print("p1 ops", S.stats(), "sems", S.nsem, "sbuf left", nc.sbuf_bytes_remaining)
        S.emit()
    return nc


def p1_host_consts(L):
    f = np.float32
    d = {}
    d['ident'] = np.eye(128, dtype=f)
    d['triU'] = np.triu(np.ones((128, 128), f))
    d['SL'] = np.tril(np.ones((128, 128), f), -1)
    d['negstrict'] = -np.triu(np.ones((128, 128), f), 1)
    pm = np.zeros((128, 128), f)
    for m in range(64):
        pm[m + 64, m] = -1.0
        pm[m, m + 64] = 1.0
    d['permT'] = pm
    half = 64
    inv_freq = (np.float32(1.0) / np.power(np.float32(10000.0), np.linspace(0.0, 1.0, half, dtype=np.float32))).astype(f)
    pos = np.arange(L, dtype=f)
    ang = (pos[:, None] * inv_freq[None, :]).astype(f).astype(np.float64)
    cos = np.cos(ang).astype(f).T
    sin = np.sin(ang).astype(f).T
    d['cosT'] = np.ascontiguousarray(np.concatenate([cos, cos], 0))
    d['sinT'] = np.ascontiguousarray(np.concatenate([sin, sin], 0))
    return d


def p1_host_core(inp, b, hg, L, consts):
    f = np.float32
    d = dict(consts)
    d['x'] = np.ascontiguousarray(inp['x'][b, :L])
    w_in = inp['la_w_in'][0]
    offs = {'gq': 0, 'gk': 1024, 'gv': 2048, 'gz': 3072, 'gb': 4096, 'ga': 4104, 'rq': 4112, 'rk': 5136, 'rv': 6160,
            'rg': 7184}
    cols = []
    for nm in ['gq', 'gk', 'gv', 'gz', 'rq', 'rk', 'rv', 'rg']:
        for h in range(2):
            hh = 2 * hg + h
            cols.append(w_in[:, offs[nm] + hh * 128: offs[nm] + (hh + 1) * 128])
    sc = np.zeros((2048, 256), f)
    for h in range(2):
        sc[:, h] = w_in[:, offs['gb'] + 2 * hg + h]
        sc[:, 2 + h] = w_in[:, offs['ga'] + 2 * hg + h]
    cols.append(sc)
    d['w_loc'] = np.ascontiguousarray(np.concatenate(cols, 1))
    d['nw0T'] = np.ascontiguousarray(inp['norm_w'][0, 0].reshape(16, 128).T)
    cw = inp['la_conv_w'][0]
    cwl = np.zeros((128, 6, 4), f)
    for t, base in enumerate([0, 1024, 2048]):
        for h in range(2):
            hh = 2 * hg + h
            cwl[:, t * 2 + h, :] = cw[base + hh * 128: base + (hh + 1) * 128, :]
    d['cw'] = np.ascontiguousarray(cwl.reshape(128, 24))
    d['nexpA_in'] = np.ascontiguousarray(inp['la_a_log'][0, 2 * hg:2 * hg + 2].reshape(2, 1))
    d['dtb'] = np.ascontiguousarray(inp['la_dt_bias'][0, 2 * hg:2 * hg + 2].reshape(2, 1))
    d['onw'] = np.ascontiguousarray(inp['la_out_norm_w'][0].reshape(128, 1))
    DT = np.zeros((128, 2, 128), np.float64)
    qd = np.zeros((128, 2, 512), np.float64)
    kd = np.zeros((128, 4), np.float64)
    s = np.arange(128)
    for h in range(2):
        lg = np.log1p(-np.float64(2.0) ** (-5.0 - (2 * hg + h)))
        diff = s[None, :] - s[:, None]
        DT[:, h, :] = np.where(diff >= 0, np.exp(diff * lg), 0.0)
        qd[:, h, :] = np.tile(np.exp((s + 1.0) * lg), 4)[None, :]
        kd[:, h] = np.exp((127.0 - s) * lg)
        kd[:, 2 + h] = np.exp(128.0 * lg)
    d['DT'] = np.ascontiguousarray(DT.reshape(128, 256).astype(f))
    d['qdrow'] = np.ascontiguousarray(qd.reshape(128, 1024).astype(f))
    d['kdcol'] = np.ascontiguousarray(kd.astype(f))
    sel = np.zeros((2, 2, 128), f)
    sel[0, 0, :] = 1.0
    sel[1, 1, :] = 1.0
    d['sel'] = np.ascontiguousarray(sel.reshape(2, 256))
    return d


P2_TG = 512
P2_D = 2048
P2_EPS = 1e-6


class P2_Ctx:
    pass


def p2_alloc(S, nc):
    C = P2_Ctx()
    C.S = S
    C.consts = Buf("consts", multi=True)
    C.ident, _ = S.sb("ident", [128, 128], F32)
    C.ones_f, C.ones_f_b = S.sb("ones_f", [128, 128], F32)
    C.ones16, C.ones16_b = S.sb("ones16", [128, 128], BF16)
    C.nwT, _ = S.sb("nwT", [128, 8, 16], F32)
    C.lnwT, _ = S.sb("lnwT", [128, 32], F32)
    C.lnbT, _ = S.sb("lnbT", [128, 32], F32)
    C.bs_bc, _ = S.sb("bs_bc", [128, 8, 128], F32)
    C.wsm_f, C.wsm_f_b = S.sb("wsm_f", [128, 8, 128], F32)
    C.wsm16, C.wsm16_b = S.sb("wsm16", [128, 8, 128], BF16)
    C.maskT, _ = S.sb("maskT", [128, 128], F32)
    C.bias2, C.bias2_b = S.sb("bias2", [128, 32, 128], F32)
    C.H, C.H_b = S.sb("H", [128, 16, P2_TG], F32)
    C.XN, C.XN_b = S.sb("XN", [128, 16, P2_TG], BF16)
    C.A, C.A_b = S.sb("A", [128, 64, P2_TG], BF16)
    C.Y, C.Y_b = S.sb("Y", [128, 16, P2_TG], F32)
    C.wb = [S.sb(f"wb{i}", [128, 16, 256], BF16) for i in range(3)]
    C.tmp = [S.sb(f"tmp{i}", [128, P2_TG], F32) for i in range(2)]
    C.rstd, C.rstd_b = S.sb("rstd", [128, P2_TG], F32)
    C.st, C.st_b = S.sb("st", [128, 32], F32)
    C.ps = [S.ps(f"ps{i}", [128, 512], F32) for i in range(8)]
    C.wscr = None
    C.blk_i = 0
    C.bank_i = 0
    C.w_i = 0
    C.tmp_i = 0
    C.ev_i = 0
    return C


def getq(C, e, name, mkview):
    if not hasattr(C, 'qcache'):
        C.qcache = {}
    if name not in C.qcache:
        C.qcache[name] = mkview(e.snap(e.partition_id() % 4))
    return C.qcache[name]


def P2_next_bank(C):
    b = C.ps[C.bank_i % 8]
    C.bank_i += 1
    return b


def P2_next_w(C):
    w = C.wb[C.w_i % 3]
    C.w_i += 1
    return w


def P2_next_tmp(C):
    t = C.tmp[C.tmp_i % 2]
    C.tmp_i += 1
    return t


def P2_ev_eng(C):
    C.ev_i += 1
    return 'scalar' if C.ev_i % 2 else 'vector'


def wload(C, wt, wbuf, src):
    S = C.S
    if getattr(C, 'wscr', None) is not None:
        i = C.blk_i % len(C.blk_list)
        C.blk_i += 1
        assert str(C.blk_list[i]) == str(src), (i, C.blk_list[i], src)
        S.dma('sync', wt[:].rearrange("p a b -> p (a b)"), C.wscr[i], writes=[wbuf])
    else:
        S.dma('gpsimd', wt[:], src.rearrange("(kc p) n -> p kc n", p=128), writes=[wbuf])


def p2_block_list(dr):
    L_ = []
    def lin(W, K, n0, N):
        for cb in range(N // 256):
            for kb in range(K // 2048):
                L_.append(W[kb * 2048:(kb + 1) * 2048, n0 + cb * 256:n0 + (cb + 1) * 256])
    lin(dr['w_out0'], 2048, 0, 2048)
    lin(dr['w_up'][0], 2048, 0, 8192)
    lin(dr['w_dn'][0], 8192, 0, 2048)
    lin(dr['sg_in'], 2048, 4096, 4096)
    lin(dr['sg_in'], 2048, 0, 4096)
    lin(dr['sg_out'], 4096, 0, 2048)
    lin(dr['w_up'][1], 2048, 0, 8192)
    lin(dr['w_dn'][1], 8192, 0, 2048)
    return L_


def P2_linear_ws(C, xview, xbuf, K, W, n0, N, evac, tg=P2_TG):
    S = C.S
    KB = K // 2048
    for cb in range(N // 256):
        banks = [P2_next_bank(C), P2_next_bank(C)]
        for kb in range(KB):
            wt, wbuf = P2_next_w(C)
            wload(C, wt, wbuf, W[kb * 2048:(kb + 1) * 2048, n0 + cb * 256:n0 + (cb + 1) * 256])
            for cc in range(2):
                pt, pb = banks[cc]
                for kc in range(16):
                    S.mm(pt[:, :tg], wt[:, kc, cc * 128:(cc + 1) * 128], xview(kb * 16 + kc),
                         kb == 0 and kc == 0, kb == KB - 1 and kc == 15, [wbuf, xbuf], [pb])
        for cc in range(2):
            evac(cb * 2 + cc, banks[cc][0], banks[cc][1])


def P2_rms_rstd(C, src, src_b, nch, dim, tg=P2_TG):
    S = C.S
    sq = C.A[:, 0:nch, :]
    S.act(sq[:, :, :tg], src, AF.Square, [src_b], [C.A_b])
    pt, pb = P2_next_bank(C)
    for c in range(nch):
        S.mm(pt[:, :tg], C.ones16[:], sq[:, c, :tg], c == 0, c == nch - 1, [C.ones16_b, C.A_b], [pb])
    S.ts('vector', C.rstd[:, :tg], pt[:, :tg], 1.0 / dim, P2_EPS, ALU.mult, ALU.add, [pb], [C.rstd_b])
    S.act(C.rstd[:, :tg], C.rstd[:, :tg], AF.Sqrt, [C.rstd_b], [C.rstd_b])
    S.op('vector', lambda e: e.reciprocal(C.rstd[:, :tg], C.rstd[:, :tg]), [C.rstd_b], [C.rstd_b])


def P2_residual_norm(C, j):
    S = C.S
    P2_rms_rstd(C, C.Y[:], C.Y_b, 16, P2_D)
    for c in range(16):
        S.stt(C.Y[:, c, :], C.Y[:, c, :], C.nwT[:, j, c:c + 1], C.rstd[:], ALU.mult, ALU.mult,
              [C.Y_b, C.consts, C.rstd_b], [C.Y_b])
        S.tt('vector', C.H[:, c, :], C.H[:, c, :], C.Y[:, c, :], ALU.add, [C.H_b, C.Y_b], [C.H_b])


def P2_prenorm(C, j):
    S = C.S
    P2_rms_rstd(C, C.H[:], C.H_b, 16, P2_D)
    for c in range(16):
        S.stt(C.XN[:, c, :], C.H[:, c, :], C.nwT[:, j, c:c + 1], C.rstd[:], ALU.mult, ALU.mult,
              [C.H_b, C.consts, C.rstd_b], [C.XN_b])


def P2_ffn(C, w_up, w_dn):
    S = C.S

    def ev_up(c, pt, pb):
        t, tb = P2_next_tmp(C)
        S.act(t[:], pt[:, :P2_TG], AF.Relu, [pb], [tb])
        S.tt('vector', C.A[:, c, :], t[:], t[:], ALU.mult, [tb], [C.A_b])

    P2_linear_ws(C, lambda kc: C.XN[:, kc, :], C.XN_b, 2048, w_up, 0, 8192, ev_up)

    def ev_dn(c, pt, pb):
        S.copy(P2_ev_eng(C), C.Y[:, c, :], pt[:, :P2_TG], [pb], [C.Y_b])

    P2_linear_ws(C, lambda kc: C.A[:, kc, :], C.A_b, 8192, w_dn, 0, 2048, ev_dn)


def p2_setup(C, dr):
    S = C.S
    S.dma('sync', C.ident[:], dr['ident'], writes=[C.consts])
    S.dma('sync', C.nwT[:], dr['nwT'].rearrange("p (j c) -> p j c", j=8), writes=[C.consts])
    S.dma('sync', C.lnwT[:], dr['lnwT'], writes=[C.consts])
    S.dma('sync', C.lnbT[:], dr['lnbT'], writes=[C.consts])
    S.dma('sync', C.maskT[:], dr['maskT'], writes=[C.consts])
    S.dma('sync', C.wsm_f[:], dr['wsT'].rearrange("p (g t) -> p g t", g=8), writes=[C.wsm_f_b])
    S.dma('sync', C.bs_bc[:].rearrange("p g t -> p (g t)"), dr['bs'].partition_broadcast(128), writes=[C.consts])
    S.op('vector', lambda e: e.memset(C.ones_f[:], 1.0), writes=[C.ones_f_b])
    S.copy('vector', C.ones16[:], C.ones_f[:], [C.ones_f_b], [C.ones16_b])
    for g in range(8):
        S.tt('vector', C.wsm_f[:, g, :], C.wsm_f[:, g, :], C.maskT[:], ALU.mult, [C.wsm_f_b, C.consts], [C.wsm_f_b])
    S.copy('vector', C.wsm16[:], C.wsm_f[:], [C.wsm_f_b], [C.wsm16_b])
    for g in range(8):
        pt, pb = P2_next_bank(C)
        S.mm(pt[:, :128], C.ones_f[:], C.wsm_f[:, g, :], True, True, [C.ones_f_b, C.wsm_f_b], [pb])
        for q in range(4):
            c = g * 4 + q
            S.stt(C.bias2[:, c, :], pt[:, :128], C.lnbT[:, c:c + 1], C.bs_bc[:, g, :], ALU.mult, ALU.add,
                  [pb, C.consts], [C.bias2_b])


def p2_group(C, dr, gi, do_l0=True, do_l1=True, fused=False, TQ=2048):
    S = C.S
    t0 = gi * P2_TG
    NT = P2_TG // 128
    Yt = C.Y[:].rearrange("p c t -> p (c t)").rearrange("p (j d) -> p j d", j=NT)
    C.Y_b.multi = True
    for j in range(NT):
        if fused:
            def fx(e, j=j):
                xv = getq(C, e, 'sync', lambda qb: dr['x'][bass.ds(qb * TQ, TQ), :])
                return e.dma_start(out=Yt[:, j, :], in_=xv[t0 + j * 128:t0 + (j + 1) * 128, :])
            S._add('sync', fx, [], [C.Y_b], dma_buf=C.Y_b)
        else:
            S.dma('sync', Yt[:, j, :], dr['x'][t0 + j * 128:t0 + (j + 1) * 128, :], writes=[C.Y_b])
    for j in range(NT):
        for c4 in range(4):
            pt, pb = P2_next_bank(C)
            for q in range(4):
                c = c4 * 4 + q
                S.tr(pt[:, q * 128:(q + 1) * 128], Yt[:, j, c * 128:(c + 1) * 128], C.ident[:], [C.Y_b, C.consts], [pb])
            S.copy(P2_ev_eng(C), C.H[:, c4 * 4:(c4 + 1) * 4, j * 128:(j + 1) * 128],
                   pt[:].rearrange("p (q n) -> p q n", q=4), [pb], [C.H_b])

    def ev_y(c, pt, pb):
        S.copy(P2_ev_eng(C), C.Y[:, c, :], pt[:, :P2_TG], [pb], [C.Y_b])

    if do_l0:
        if fused:
            def fo(e):
                gv = getq(C, e, 'gpsimd', lambda qb: dr['ogath'][bass.ds(qb * (TQ // P2_TG), TQ // P2_TG)])
                return e.dma_start(out=C.XN[:], in_=gv[gi:gi + 1].rearrange("o (c p) t -> p (o c) t", p=128))
            S._add('gpsimd', fo, [C.ogath_b], [C.XN_b], dma_buf=C.XN_b)
        else:
            S.dma('gpsimd', C.XN[:], dr['oT'][:, t0:t0 + P2_TG].rearrange("(c p) t -> p c t", p=128), writes=[C.XN_b])
        P2_linear_ws(C, lambda kc: C.XN[:, kc, :], C.XN_b, 2048, dr['w_out0'], 0, 2048, ev_y)
        P2_residual_norm(C, 1)
        P2_prenorm(C, 2)
        P2_ffn(C, dr['w_up'][0], dr['w_dn'][0])
        P2_residual_norm(C, 3)
    if do_l1:
        P2_prenorm(C, 4)
        Vt = C.A[:].rearrange("p c t -> p (c t)").bitcast(F32).rearrange("p (j n) -> p j n", j=NT)
        Vh = C.Y[:].rearrange("p c t -> p (c t)").bitcast(BF16).rearrange("p (j n) -> p j n", j=NT)
        for cb in range(16):
            banks = [P2_next_bank(C) for _ in range(NT)]
            wt, wbuf = P2_next_w(C)
            wload(C, wt, wbuf, dr['sg_in'][0:2048, 4096 + cb * 256:4096 + (cb + 1) * 256])
            for j in range(NT):
                pt, pb = banks[j]
                for kc in range(16):
                    S.mm(pt[:, :256], C.XN[:, kc, j * 128:(j + 1) * 128], wt[:, kc, :], kc == 0, kc == 15,
                         [wbuf, C.XN_b], [pb])
            for j in range(NT):
                pt, pb = banks[j]
                S.act(Vt[:, j, cb * 256:(cb + 1) * 256], pt[:, :256], AF.Gelu_apprx_tanh, [pb], [C.A_b])
        s1 = C.st[:, 0:4]
        s2 = C.st[:, 4:8]
        mu = C.st[:, 8:12]
        rs = C.st[:, 12:16]
        nmr = C.st[:, 16:20]
        jt, junk_b = P2_next_w(C)
        junk = jt[:].rearrange("p a b -> p (a b)")
        for j in range(NT):
            S.act(junk, Vt[:, j, :], AF.Copy, [C.A_b], [junk_b, C.st_b], accum_out=s1[:, j:j + 1])
            S.act(junk, Vt[:, j, :], AF.Square, [C.A_b], [junk_b, C.st_b], accum_out=s2[:, j:j + 1])
        S.ts('vector', mu, s1, 1.0 / 4096, None, ALU.mult, None, [C.st_b], [C.st_b])
        S.ts('vector', s2, s2, 1.0 / 4096, None, ALU.mult, None, [C.st_b], [C.st_b])
        S.tt('vector', rs, mu, mu, ALU.mult, [C.st_b], [C.st_b])
        S.tt('vector', rs, s2, rs, ALU.subtract, [C.st_b], [C.st_b])
        S.ts('vector', rs, rs, P2_EPS, None, ALU.add, None, [C.st_b], [C.st_b])
        S.act(rs, rs, AF.Sqrt, [C.st_b], [C.st_b])
        S.op('vector', lambda e: e.reciprocal(rs, rs), [C.st_b], [C.st_b])
        S.tt('vector', nmr, mu, rs, ALU.mult, [C.st_b], [C.st_b])
        S.ts('vector', nmr, nmr, -1.0, None, ALU.mult, None, [C.st_b], [C.st_b])
        for j in range(NT):
            S.act(Vh[:, j, :], Vt[:, j, :], AF.Identity, [C.A_b, C.st_b], [C.Y_b],
                  scale=rs[:, j:j + 1], bias=nmr[:, j:j + 1])

        def ev_u(c, pt, pb):
            S.act(C.A[:, c, :], pt[:, :P2_TG], AF.Gelu_apprx_tanh, [pb], [C.A_b])

        P2_linear_ws(C, lambda kc: C.XN[:, kc, :], C.XN_b, 2048, dr['sg_in'], 0, 4096, ev_u)
        for c in range(32):
            g = c // 4
            pt, pb = P2_next_bank(C)
            for j in range(NT):
                S.mm(pt[:, j * 128:(j + 1) * 128], Vh[:, j, c * 128:(c + 1) * 128], C.wsm16[:, g, :], True, True,
                     [C.Y_b, C.wsm16_b], [pb])
            t, tb = P2_next_tmp(C)
            S.stt(t[:].rearrange("p (j n) -> p j n", j=NT), pt[:].rearrange("p (j n) -> p j n", j=NT),
                  C.lnwT[:, c:c + 1], C.bias2[:, c:c + 1, :].to_broadcast([128, NT, 128]), ALU.mult, ALU.add,
                  [pb, C.consts, C.bias2_b], [tb])
            S.tt('vector', C.A[:, c, :], t[:], C.A[:, c, :], ALU.mult, [tb, C.A_b], [C.A_b])
        P2_linear_ws(C, lambda kc: C.A[:, kc, :], C.A_b, 4096, dr['sg_out'], 0, 2048, ev_y)
        P2_residual_norm(C, 5)
        P2_prenorm(C, 6)
        P2_ffn(C, dr['w_up'][1], dr['w_dn'][1])
        P2_residual_norm(C, 7)
    for j in range(NT):
        for c4 in range(4):
            pt, pb = P2_next_bank(C)
            for q in range(4):
                c = c4 * 4 + q
                S.tr(pt[:, q * 128:(q + 1) * 128], C.H[:, c, j * 128:(j + 1) * 128], C.ident[:], [C.H_b, C.consts], [pb])
            S.copy(P2_ev_eng(C), Yt[:, j, c4 * 512:(c4 + 1) * 512], pt[:], [pb], [C.Y_b])
        S.dma('sync', dr['out'][t0 + j * 128:t0 + (j + 1) * 128, :], Yt[:, j, :], reads=[C.Y_b])


def p2_dram(nc, T, fused=False):
    dr = {}
    def inp(name, shape):
        dr[name] = nc.dram_tensor(name, list(shape), F32, kind="ExternalInput").ap()
    if not fused:
        inp('x', [T, P2_D]); inp('oT', [P2_D, T]); inp('ident', [128, 128])
    inp('w_out0', [2048, 2048]); inp('w_up', [2, 2048, 8192]); inp('w_dn', [2, 8192, 2048])
    inp('sg_in', [2048, 8192]); inp('sg_out', [4096, 2048])
    inp('nwT', [128, 128]); inp('lnwT', [128, 32]); inp('lnbT', [128, 32]); inp('wsT', [128, 1024])
    inp('maskT', [128, 128]); inp('bs', [1024])
    dr['out'] = nc.dram_tensor("out", [T, P2_D], F32, kind="ExternalOutput").ap()
    return dr


def build_p2(T=2048, ngroups=None, do_l0=True, do_l1=True):
    nc = bass.Bass("TRN2", target_bir_lowering=False)
    dr = p2_dram(nc, T)
    with contextlib.ExitStack() as st:
        S = Sched(nc, st)
        C = p2_alloc(S, nc)
        p2_setup(C, dr)
        for gi in range(ngroups if ngroups else T // P2_TG):
            p2_group(C, dr, gi, do_l0, do_l1)
        S.finish()
        print("p2 ops", S.stats(), "sems", S.nsem, "sbuf left", nc.sbuf_bytes_remaining)
        S.emit()
    return nc


def p2_host_inputs(inp):
    f = np.float32
    d = {}
    d['w_out0'] = np.ascontiguousarray(inp['la_w_out'][0])
    d['w_up'] = inp['ffn_w_up']
    d['w_dn'] = inp['ffn_w_down']
    d['sg_in'] = np.ascontiguousarray(inp['sg_w_in'][0])
    d['sg_out'] = np.ascontiguousarray(inp['sg_w_out'][0])
    d['nwT'] = np.ascontiguousarray(inp['norm_w'].reshape(8, 16, 128).transpose(2, 0, 1).reshape(128, 128))
    d['lnwT'] = np.ascontiguousarray(inp['sg_ln_w'][0].reshape(32, 128).T)
    d['lnbT'] = np.ascontiguousarray(inp['sg_ln_b'][0].reshape(32, 128).T)
    d['wsT'] = np.ascontiguousarray(inp['sg_w_s'][0].transpose(2, 0, 1).reshape(128, 1024))
    d['maskT'] = np.triu(np.ones((128, 128), f))
    d['bs'] = np.ascontiguousarray(inp['sg_b_s'][0].reshape(1024))
    d['ident'] = np.eye(128, dtype=f)
    return d


def gath_perm():
    perm = []
    for r in range(4):
        for k in range(4):
            perm.append((0 if k < 2 else 8) + 2 * r + (k % 2))
    return perm


import os
DBG = int(os.environ.get('FUSED_DBG', '0'))


def build_fused(L=8192):
    nc = bass.Bass("TRN2", target_bir_lowering=False)
    TQ = L // 4
    dr = p1_dram(nc, L, fused=True)
    dr2 = p2_dram(nc, TQ, fused=True)
    for k, v in dr2.items():
        dr[k] = v
    with contextlib.ExitStack() as semstack:
        oloc_b = Buf("oloc", multi=True)
        ogath_b = Buf("ogath")
        with contextlib.ExitStack() as st1:
            S1 = Sched(nc, st1, semstack, prefix="a_")
            C1 = p1_alloc(S1, nc)
            C1.oT_b = oloc_b
            p1_setup(C1, dr)
            rg = [[0, 1, 2, 3], [4, 5, 6, 7]]
            ogath_b.multi = True

            def gather(g):
                S1._add('gpsimd', lambda e: e.collective_compute("AllGather", ALU.bypass, replica_groups=rg,
                                                                 ins=[dr['oloc'][g].opt()], outs=[dr['ogath'][g].opt()]),
                        [oloc_b], [ogath_b], dma_buf=ogath_b, inc=1)

            NG1 = L // 512
            blk_list = p2_block_list(dr)
            NB = len(blk_list)
            wscr = nc.dram_tensor("wscr", [NB, 128, 4096], BF16, kind="Internal").ap()
            wscr_b = Buf("wscr", multi=True)
            per = (NB + NG1 - 1) // NG1

            def convert(g):
                for i in range(g * per, min(NB, (g + 1) * per)):
                    S1.dma('gpsimd', wscr[i].rearrange("p (kc n) -> p kc n", kc=16),
                           blk_list[i].rearrange("(kc p) n -> p kc n", p=128), writes=[wscr_b])

            def hook(g):
                if g > 0:
                    gather(g - 1)
                convert(g)

            for gi in range(NG1):
                p1_group(C1, dr, gi, L, hook=(lambda g=gi: hook(g)))
            if DBG != 1:
                gather(NG1 - 1)
            S1.finish(barrier=True)
            S1.emit()
        with contextlib.ExitStack() as st2:
            S2 = Sched(nc, st2, semstack, prefix="b_")
            C2 = p2_alloc(S2, nc)
            C2.ogath_b = Buf("ogath2")
            C2.wscr = wscr
            C2.blk_list = blk_list
            p2_setup(C2, dr)
            for gi in range(TQ // 512 if DBG not in (1, 2) else 0):
                p2_group(C2, dr, gi, fused=True, TQ=TQ)
            S2.finish()
            S2.emit()
    return nc


def fused_in_maps(inp, L=8192):
    consts = p1_host_consts(L)
    shared = p2_host_inputs(inp)
    perm = gath_perm()
    w = shared['w_out0']
    shared['w_out0'] = np.ascontiguousarray(np.concatenate([w[f * 128:(f + 1) * 128] for f in perm], 0))
    maps = []
    for c in range(8):
        m = p1_host_core(inp, c // 4, c % 4, L, consts)
        for k, v in shared.items():
            if k not in m:
                m[k] = v
        maps.append(m)
    return maps


def kernel(**inputs):
    inp = {k: np.asarray(v) for k, v in inputs.items()}
    B, L = 2, 8192
    nc = build_fused(L)
    maps = fused_in_maps(inp, L)
    res = run_bass_kernel_spmd(nc, maps, core_ids=list(range(8)))
    out = np.empty((B, L, 2048), np.float32)
    for c in range(8):
        b, j = c // 4, c % 4
        out[b, j * (L // 4):(j + 1) * (L // 4)] = res.results[c]['out']
    return out
```

```python
import contextlib
import numpy as np
import concourse.bass as bass
import concourse.mybir as mybir
from concourse.bass_utils import run_bass_kernel_spmd

F32 = mybir.dt.float32
BF16 = mybir.dt.bfloat16
ALU = mybir.AluOpType
AF = mybir.ActivationFunctionType
AX = mybir.AxisListType

ENGS = ['tensor', 'vector', 'scalar', 'gpsimd', 'sync']
SAME_ENG_SYNC = True


class Buf:
    def __init__(self, name, multi=False):
        self.name = name
        self.w = None
        self.r = {}
        self.multi = multi
        self.sem = None
        self.semcnt = 0


class Sched:
    def __init__(self, nc, stack, semstack=None, prefix=""):
        self.nc = nc
        self.stack = stack
        self.semstack = semstack if semstack is not None else stack
        self.prefix = prefix
        self.ops = {e: [] for e in ENGS}
        self.seen = {e: {} for e in ENGS}
        self.esem = {e: self.semstack.enter_context(nc.semaphore(prefix + "prog_" + e)) for e in ENGS}
        self.dma_bufs = []
        self.nsem = len(ENGS)

    def sb(self, name, shape, dtype=F32, multi=False):
        t = self.stack.enter_context(self.nc.sbuf_tensor(self.prefix + "s_" + name, list(shape), dtype))
        b = Buf(name, multi)
        return t, b

    def ps(self, name, shape, dtype=F32):
        t = self.stack.enter_context(self.nc.psum_tensor(self.prefix + "p_" + name, list(shape), dtype))
        return t, Buf(name)

    def _dsem(self, b):
        if b.sem is None:
            b.sem = self.semstack.enter_context(self.nc.semaphore(self.prefix + "d_" + b.name))
            self.dma_bufs.append(b)
            self.nsem += 1
        return b.sem

    def _add(self, eng, fn, reads, writes, dma_buf=None, inc=16):
        needs = []
        for b in reads:
            if b.w is not None:
                needs.append(b.w)
        for b in writes:
            if b.w is not None and not (dma_buf is not None and b.multi):
                needs.append(b.w)
            needs.extend(b.r.values())
        waits = []
        seen = self.seen[eng]
        for ev in needs:
            if ev[0] == 'e' and ev[1] == eng and (eng == 'tensor' or not SAME_ENG_SYNC):
                continue
            key = (ev[0], ev[1] if ev[0] == 'e' else id(ev[1]))
            if seen.get(key, -1) >= ev[2]:
                continue
            seen[key] = ev[2]
            waits.append(ev)
            if ev[0] == 'e':
                self.ops[ev[1]][ev[2]]['ms'] = True
        idx = len(self.ops[eng])
        if dma_buf is not None:
            sem = self._dsem(dma_buf)
            dma_buf.semcnt += inc
            event = ('d', sem, dma_buf.semcnt)
            rkey = ('d', id(sem))
        else:
            sem = None
            event = ('e', eng, idx)
            rkey = ('e', eng)
        self.ops[eng].append(dict(fn=fn, waits=waits, ms=False, dma=sem, inc=inc))
        for b in reads:
            b.r[rkey] = event
        for b in writes:
            b.w = event
            b.r = {}
        return event

    def op(self, eng, fn, reads=(), writes=()):
        return self._add(eng, fn, list(reads), list(writes))

    def dma(self, eng, out, in_, reads=(), writes=(), sembuf=None):
        reads = list(reads)
        writes = list(writes)
        if sembuf is None:
            sembuf = writes[0] if writes else reads[0]
        return self._add(eng, lambda e: e.dma_start(out=out, in_=in_), reads, writes, dma_buf=sembuf)

    def mm(self, out, lhsT, rhs, start, stop, reads, writes):
        return self.op('tensor', lambda e: e.matmul(out, lhsT, rhs, start=start, stop=stop), reads, writes)

    def tr(self, out, in_, ident, reads, writes):
        return self.op('tensor', lambda e: e.transpose(out, in_, ident), reads, writes)

    def act(self, out, in_, func, reads, writes, eng='scalar', **kw):
        return self.op(eng, lambda e: e.activation(out, in_, func, **kw), reads, writes)

    def tt(self, eng, out, in0, in1, op, reads, writes):
        return self.op(eng, lambda e: e.tensor_tensor(out, in0, in1, op), reads, writes)

    def ts(self, eng, out, in0, s1, s2, op0, op1, reads, writes):
        if op1 is None:
            return self.op(eng, lambda e: e.tensor_scalar(out, in0, s1, None, op0), reads, writes)
        return self.op(eng, lambda e: e.tensor_scalar(out, in0, s1, s2, op0, op1), reads, writes)

    def stt(self, out, in0, scalar, in1, op0, op1, reads, writes, eng='vector'):
        return self.op(eng, lambda e: e.scalar_tensor_tensor(out, in0, scalar, in1, op0, op1), reads, writes)

    def copy(self, eng, out, in_, reads, writes):
        if eng == 'scalar':
            return self.op(eng, lambda e: e.copy(out, in_), reads, writes)
        return self.op(eng, lambda e: e.tensor_copy(out, in_), reads, writes)

    def finish(self, eng='sync', barrier=False):
        waits = [('d', b.sem, b.semcnt) for b in self.dma_bufs if b.semcnt > 0]
        if not barrier:
            self.ops[eng].append(dict(fn=None, waits=waits, ms=False, dma=None, inc=0))
            return
        last = {}
        for e in ENGS:
            idx = None
            for i in range(len(self.ops[e]) - 1, -1, -1):
                o = self.ops[e][i]
                if o['fn'] is not None and o['dma'] is None:
                    idx = i
                    break
            if idx is not None:
                self.ops[e][idx]['ms'] = True
                last[e] = idx
        for e in ENGS:
            w = list(waits) + [('e', e2, i2) for e2, i2 in last.items() if e2 != e]
            self.ops[e].append(dict(fn=None, waits=w, ms=False, dma=None, inc=0))

    def emit(self):
        rank = {}
        for eng in ENGS:
            c = 0
            r = []
            for o in self.ops[eng]:
                if o['ms']:
                    c += 1
                r.append(c)
            rank[eng] = r
        self.rank = rank
        esem = self.esem

        def make(eng):
            ops = self.ops[eng]

            def body(e):
                for o in ops:
                    for ev in o['waits']:
                        if ev[0] == 'e':
                            e.wait_ge(esem[ev[1]], rank[ev[1]][ev[2]])
                        else:
                            e.wait_ge(ev[1], ev[2])
                    if o['fn'] is None:
                        continue
                    ins = o['fn'](e)
                    if o['dma'] is not None:
                        ins.then_inc(o['dma'], o['inc'])
                    elif o['ms']:
                        ins.then_inc(esem[eng], 1)
            return body

        with self.nc.Block() as block:
            block.tensor(make('tensor'))
            block.vector(make('vector'))
            block.scalar(make('scalar'))
            block.gpsimd(make('gpsimd'))
            block.sync(make('sync'))

    def stats(self):
        return {e: (len(self.ops[e]), sum(1 for o in self.ops[e] if o['ms'])) for e in ENGS}


P1_TG = 512
P1_NT = 4
P1_D = 2048
P1_EPS = 1e-6
P1_ISQ = float(128 ** -0.5)


class P1_Ctx:
    pass


P1_STAGE = [99]


class P1_StopBuild(Exception):
    pass


def P1_chk(k):
    if P1_STAGE[0] == k:
        raise P1_StopBuild()


def p1_dram(nc, L, fused=False):
    dr = {}

    def inp(name, shape):
        dr[name] = nc.dram_tensor(name, list(shape), F32, kind="ExternalInput").ap()
    inp('x', [L, P1_D]); inp('w_loc', [2048, 2304]); inp('nw0T', [128, 16]); inp('cw', [128, 24])
    inp('nexpA_in', [2, 1]); inp('dtb', [2, 1]); inp('onw', [128, 1])
    inp('ident', [128, 128]); inp('triU', [128, 128]); inp('SL', [128, 128]); inp('negstrict', [128, 128])
    inp('permT', [128, 128]); inp('DT', [128, 256]); inp('qdrow', [128, 1024]); inp('kdcol', [128, 4])
    inp('sel', [2, 256]); inp('cosT', [128, L]); inp('sinT', [128, L])
    if fused:
        dr['oloc'] = nc.dram_tensor("oloc", [L // 512, 512, 512], F32, kind="Internal").ap()
        dr['ogath'] = nc.dram_tensor("ogath", [L // 512, 2048, 512], F32, kind="Internal").ap()
    else:
        dr['oT'] = nc.dram_tensor("oT", [512, L], F32, kind="ExternalOutput").ap()
    return dr


def p1_alloc(S, nc):
    C = P1_Ctx()
    C.S = S
    sb = S.sb
    C.consts = Buf("consts", multi=True)
    for nm, shp in [('ident', [128, 128]), ('triU', [128, 128]), ('SL', [128, 128]), ('negstrict', [128, 128]),
                    ('permT', [128, 128]), ('DT', [128, 2, 128]), ('qdrow', [128, 2, 512]), ('kdcol', [128, 4]),
                    ('sel', [2, 2, 128]), ('nw0T', [128, 16]), ('cw', [128, 6, 4]), ('nexpA', [2, 1]), ('dtb', [2, 1]),
                    ('onw', [128, 1])]:
        t, _ = sb(nm, shp, F32)
        setattr(C, nm, t)
    C.ones_f, C.ones_f_b = sb("ones_f", [128, 128], F32)
    C.xt = [sb(f"xt{i}", [128, P1_D], F32) for i in range(2)]
    C.XN, C.XN_b = sb("XN", [128, 16, P1_TG], BF16)
    C.wb = [sb(f"wb{i}", [128, 16, 256], BF16) for i in range(4)]
    C.PRc, C.PRc_b = sb("PRc", [128, 6, P1_TG + 3], F32)
    C.PRo, C.PRo_b = sb("PRo", [128, 10, P1_TG], F32)
    C.CV = [sb(f"CV{i}", [128, P1_TG], F32) for i in range(6)]
    C.SZ = [sb(f"SZ{i}", [128, P1_TG], F32) for i in range(4)]
    C.cs = [sb(f"cs{i}", [128, P1_TG], F32) for i in range(2)]
    C.RQ = [sb(f"RQ{i}", [128, P1_TG], F32) for i in range(4)]
    C.QD = [sb(f"QD{i}", [128, P1_TG], F32) for i in range(2)]
    C.BBC = [sb(f"BBC{i}", [128, P1_TG], F32) for i in range(2)]
    C.t512 = [sb(f"t512_{i}", [128, P1_TG], F32) for i in range(3)]
    C.bg, C.bg_b = sb("bg", [2, 3, P1_TG], F32)
    C.bcol, C.bcol_b = sb("bcol", [128, P1_NT, 2], F32)
    C.gcol, C.gcol_b = sb("gcol", [128, P1_NT, 2], F32)
    C.sm = [sb(f"sm{i}", [128, 16], F32) for i in range(P1_NT)]
    C.xs, C.xs_b = sb("xs", [128, 8], F32)
    names = ['GT', 'E', 'DTi', 'NDs', 'egc', 'Kbg', 'Ktail', 'Vb', 'KbT', 'Ma', 'Mb', 'MTa', 'MTb', 'P', 'QKT', 'QdT',
             'negWT', 'On']
    C.slot = {}
    for nm in names:
        C.slot[nm] = [sb(f"{nm}{j}", [128, 128], F32) for j in range(P1_NT)]
    C.Vn = [sb(f"Vn{i}", [128, 128], F32) for i in range(2)]
    C.junk = [sb(f"junk{i}", [128, 128], F32) for i in range(2)]
    C.St = [sb(f"S{i}", [128, 128], F32) for i in range(4)]
    C.OUT, C.OUT_b = sb("OUT", [128, 4, P1_TG], F32)
    C.ps = []
    C.psq = []
    for i in range(8):
        t, bb = S.ps(f"ps{i}", [128, 512], F32)
        C.ps.append(t)
        C.psq.append([bb])
    C.oT_b = None
    C.bank_i = 0
    C.q_i = 0
    C.w_i = 0
    C.i2 = 0
    C.ev_i = 0
    return C


def P1_next_bank(C):
    b = C.bank_i % 4
    C.bank_i += 1
    return C.ps[b], C.psq[b]


def P1_next_q(C):
    i = C.q_i % 16
    C.q_i += 1
    b = 4 + i % 4
    q = i // 4
    return C.ps[b][:, q * 128:(q + 1) * 128], C.psq[b][0]


def P1_next_w(C):
    w = C.wb[C.w_i % 4]
    C.w_i += 1
    return w


def P1_rot2(C, lst):
    C.i2 += 1
    return lst[C.i2 % 2]


def P1_ev_eng(C):
    C.ev_i += 1
    return 'scalar' if C.ev_i % 2 else 'vector'


def p1_setup(C, dr):
    S = C.S
    cd = lambda t, src: S.dma('sync', t, src, writes=[C.consts])
    cd(C.ident[:], dr['ident']); cd(C.triU[:], dr['triU']); cd(C.SL[:], dr['SL']); cd(C.negstrict[:], dr['negstrict'])
    cd(C.permT[:], dr['permT']); cd(C.DT[:], dr['DT'].rearrange("p (h t) -> p h t", h=2))
    cd(C.qdrow[:], dr['qdrow'].rearrange("p (h t) -> p h t", h=2)); cd(C.kdcol[:], dr['kdcol'])
    cd(C.sel[:], dr['sel'].rearrange("p (h t) -> p h t", h=2)); cd(C.nw0T[:], dr['nw0T'])
    cd(C.cw[:], dr['cw'].rearrange("p (i k) -> p i k", i=6)); cd(C.nexpA[:], dr['nexpA_in']); cd(C.dtb[:], dr['dtb'])
    cd(C.onw[:], dr['onw'])
    S.op('vector', lambda e: e.memset(C.ones_f[:], 1.0), writes=[C.ones_f_b])
    S.act(C.nexpA[:], C.nexpA[:], AF.Exp, [C.consts], [C.consts])
    S.ts('vector', C.nexpA[:], C.nexpA[:], -1.0, None, ALU.mult, None, [C.consts], [C.consts])
    for i in range(4):
        S.op('vector', lambda e, i=i: e.memset(C.St[i][0][:], 0.0), writes=[C.St[i][1]])
    S.op('vector', lambda e: e.memset(C.PRc[:, :, 0:3], 0.0), writes=[C.PRc_b])


def P1_head_epilogue_a(C, psO, psO_b, j):
    S = C.S
    jk, jk_b = P1_rot2(C, C.junk)
    sm, sm_b = C.sm[j]
    on, on_b = C.slot['On'][j]
    S.act(jk[:], psO, AF.Square, [psO_b], [jk_b, sm_b], accum_out=sm[:, 8:9])
    S.ts('vector', sm[:, 9:10], sm[:, 8:9], 1.0 / 128, P1_EPS, ALU.mult, ALU.add, [sm_b], [sm_b])
    S.act(sm[:, 9:10], sm[:, 9:10], AF.Sqrt, [sm_b], [sm_b])
    S.op('vector', lambda e: e.reciprocal(sm[:, 9:10], sm[:, 9:10]), [sm_b], [sm_b])
    S.ts('vector', on[:], psO, sm[:, 9:10], None, ALU.mult, None, [psO_b, sm_b], [on_b])


def P1_head_epilogue_b(C, gate, gate_b, oidx, j, use_onw):
    S = C.S
    js = slice(j * 128, (j + 1) * 128)
    on, on_b = C.slot['On'][j]
    pq, pq_b = P1_next_q(C)
    S.tr(pq, on[:], C.ident[:], [on_b, C.consts], [pq_b])
    if use_onw:
        S.stt(C.OUT[:, oidx, js], pq, C.onw[:, 0:1], gate[:, js], ALU.mult, ALU.mult, [pq_b, C.consts, gate_b], [C.OUT_b])
    else:
        S.tt('vector', C.OUT[:, oidx, js], pq, gate[:, js], ALU.mult, [pq_b, gate_b], [C.OUT_b])


def p1_group(C, dr, gi, L, hook=None):
    S = C.S
    t0 = gi * P1_TG
    sl = C.slot
    for j in range(P1_NT):
        xt, xt_b = C.xt[j % 2]
        S.dma('sync', xt[:], dr['x'][t0 + j * 128:t0 + (j + 1) * 128, :], writes=[xt_b])
        jt, junk_b = P1_next_w(C)
        junk = jt[:].rearrange("p a b -> p (a b)")[:, 0:P1_D]
        S.act(junk, xt[:], AF.Square, [xt_b], [junk_b, C.xs_b], accum_out=C.xs[:, j:j + 1])
        S.ts('vector', C.xs[:, 4 + j:5 + j], C.xs[:, j:j + 1], 1.0 / P1_D, P1_EPS, ALU.mult, ALU.add, [C.xs_b], [C.xs_b])
        S.act(C.xs[:, 4 + j:5 + j], C.xs[:, 4 + j:5 + j], AF.Sqrt, [C.xs_b], [C.xs_b])
        S.op('vector', lambda e, j=j: e.reciprocal(C.xs[:, 4 + j:5 + j], C.xs[:, 4 + j:5 + j]), [C.xs_b], [C.xs_b])
        S.ts('vector', xt[:], xt[:], C.xs[:, 4 + j:5 + j], None, ALU.mult, None, [xt_b, C.xs_b], [xt_b])
        for c4 in range(4):
            pt, pbs = P1_next_bank(C)
            for q in range(4):
                c = c4 * 4 + q
                S.tr(pt[:, q * 128:(q + 1) * 128], xt[:, c * 128:(c + 1) * 128], C.ident[:], [xt_b, C.consts], pbs)
            S.tt('vector', C.XN[:, c4 * 4:(c4 + 1) * 4, j * 128:(j + 1) * 128],
                 pt[:].rearrange("p (q n) -> p q n", q=4),
                 C.nw0T[:, c4 * 4:(c4 + 1) * 4].unsqueeze(2).to_broadcast([128, 4, 128]), ALU.mult,
                 pbs + [C.consts], [C.XN_b])
    P1_chk(1)
    S.dma('sync', C.cs[0][0][:], dr['cosT'][:, t0:t0 + P1_TG], writes=[C.cs[0][1]])
    S.dma('sync', C.cs[1][0][:], dr['sinT'][:, t0:t0 + P1_TG], writes=[C.cs[1][1]])
    def main_block(cb):
        banks = [P1_next_bank(C), P1_next_bank(C)]
        wt, wbuf = P1_next_w(C)
        S.dma('gpsimd', wt[:], dr['w_loc'][:, cb * 256:(cb + 1) * 256].rearrange("(kc p) n -> p kc n", p=128),
              writes=[wbuf])
        for cc in range(2):
            pt, pbs = banks[cc]
            for kc in range(16):
                S.mm(pt[:, :P1_TG], wt[:, kc, cc * 128:(cc + 1) * 128], C.XN[:, kc, :], kc == 0, kc == 15,
                     [wbuf, C.XN_b], pbs)
        for cc in range(2):
            pt, pbs = banks[cc]
            ci = cb * 2 + cc
            if ci < 6:
                S.copy(P1_ev_eng(C), C.PRc[:, ci, 3:3 + P1_TG], pt[:, :P1_TG], pbs, [C.PRc_b])
            else:
                S.copy('scalar', C.PRo[:, ci - 6, :], pt[:, :P1_TG], pbs, [C.PRo_b])

    sbanks = [P1_next_bank(C), P1_next_bank(C)]
    wt, wbuf = P1_next_w(C)
    S.dma('gpsimd', wt[:], dr['w_loc'][:, 2048:2304].rearrange("(kc p) n -> p kc n", p=128), writes=[wbuf])
    for cc in range(2):
        pt, pbs = sbanks[cc]
        for kc in range(16):
            S.mm(pt[0:2, :P1_TG], wt[:, kc, cc * 2:cc * 2 + 2], C.XN[:, kc, :], kc == 0, kc == 15, [wbuf, C.XN_b], pbs)
    S.act(C.bg[:, 0, :], sbanks[0][0][0:2, :P1_TG], AF.Sigmoid, sbanks[0][1], [C.bg_b])
    S.act(C.bg[:, 2, :], sbanks[1][0][0:2, :P1_TG], AF.Exp, sbanks[1][1] + [C.consts], [C.bg_b], bias=C.dtb[:, 0:1])
    S.act(C.bg[:, 2, :], C.bg[:, 2, :], AF.Ln, [C.bg_b], [C.bg_b], bias=1.0)
    S.ts('vector', C.bg[:, 1, :], C.bg[:, 2, :], C.nexpA[:, 0:1], None, ALU.mult, None, [C.bg_b, C.consts], [C.bg_b])
    for cb in range(3):
        main_block(cb)
    P1_chk(3)
    for i in range(6):
        cv, cv_b = C.CV[i]
        S.ts('vector', cv[:], C.PRc[:, i, 3:3 + P1_TG], C.cw[:, i, 3:4], None, ALU.mult, None, [C.PRc_b, C.consts], [cv_b])
        for k in (2, 1, 0):
            S.stt(cv[:], C.PRc[:, i, k:k + P1_TG], C.cw[:, i, k:k + 1], cv[:], ALU.mult, ALU.add,
                  [C.PRc_b, C.consts, cv_b], [cv_b])
        S.act(cv[:], cv[:], AF.Silu, [cv_b], [cv_b])
    S.copy('vector', C.PRc[:, :, 0:3], C.PRc[:, :, P1_TG:P1_TG + 3], [C.PRc_b], [C.PRc_b])
    for cb in range(3, 8):
        main_block(cb)
    if hook is not None:
        hook()
    for j in range(P1_NT):
        pq, pq_b = P1_next_q(C)
        S.tr(pq[:, 0:2], C.bg[:, 0, j * 128:(j + 1) * 128], C.ident[0:2, 0:2], [C.bg_b, C.consts], [pq_b])
        S.tr(pq[:, 2:4], C.bg[:, 1, j * 128:(j + 1) * 128], C.ident[0:2, 0:2], [C.bg_b, C.consts], [pq_b])
        S.copy('vector', C.bcol[:, j, :], pq[:, 0:2], [pq_b], [C.bcol_b])
        S.copy('vector', C.gcol[:, j, :], pq[:, 2:4], [pq_b], [C.gcol_b])
    for h in range(2):
        pt, pbs = P1_next_bank(C)
        S.mm(pt[:, :P1_TG], C.sel[:, h, :], C.bg[:, 0, :], True, True, [C.consts, C.bg_b], pbs)
        S.copy('vector', C.BBC[h][0][:], pt[:, :P1_TG], pbs, [C.BBC[h][1]])
    for h in range(2):
        S.act(C.SZ[h][0][:], C.PRo[:, h, :], AF.Silu, [C.PRo_b], [C.SZ[h][1]])
        S.act(C.SZ[2 + h][0][:], C.PRo[:, 8 + h, :], AF.Silu, [C.PRo_b], [C.SZ[2 + h][1]])
    sqb = [C.RQ[0], C.RQ[1], C.RQ[2], C.RQ[3]]
    rnb = [C.QD[0], C.QD[1], C.t512[1], C.t512[2]]
    lbanks = []
    for i in range(4):
        cv, cv_b = C.CV[i]
        S.act(sqb[i][0][:], cv[:], AF.Square, [cv_b], [sqb[i][1]])
    for i in range(4):
        pt, pbs = P1_next_bank(C)
        lbanks.append((pt, pbs))
        S.mm(pt[:, :P1_TG], C.ones_f[:], sqb[i][0][:], True, True, [C.ones_f_b, sqb[i][1]], pbs)
    for i in range(4):
        pt, pbs = lbanks[i]
        S.ts('vector', rnb[i][0][:], pt[:, :P1_TG], 1e-6, None, ALU.add, None, pbs, [rnb[i][1]])
    for i in range(4):
        S.act(rnb[i][0][:], rnb[i][0][:], AF.Sqrt, [rnb[i][1]], [rnb[i][1]])
    for i in range(4):
        cv, cv_b = C.CV[i]
        rn, rn_b = rnb[i]
        S.op('vector', lambda e, rn=rn: e.reciprocal(rn[:], rn[:]), [rn_b], [rn_b])
        if i < 2:
            S.stt(cv[:], cv[:], P1_ISQ, rn[:], ALU.mult, ALU.mult, [cv_b, rn_b], [cv_b])
        else:
            S.tt('vector', cv[:], cv[:], rn[:], ALU.mult, [cv_b, rn_b], [cv_b])
    P1_chk(4)
    for h in range(2):
        qT, qT_b = C.CV[h]
        kT, kT_b = C.CV[2 + h]
        vT, vT_b = C.CV[4 + h]
        St, St_b = C.St[h]
        JS = [slice(j * 128, (j + 1) * 128) for j in range(P1_NT)]
        for j in range(P1_NT):
            GT, GT_b = sl['GT'][j]
            S.ts('vector', GT[:], C.triU[:], C.gcol[:, j, h:h + 1], None, ALU.mult, None, [C.consts, C.gcol_b], [GT_b])
            KbT, KbT_b = sl['KbT'][j]
            S.tt('vector', KbT[:], kT[:, JS[j]], C.BBC[h][0][:, JS[j]], ALU.mult, [kT_b, C.BBC[h][1]], [KbT_b])
        pAs, pBs, pCs = [], [], []
        for j in range(P1_NT):
            GT, GT_b = sl['GT'][j]
            pA, pA_b = P1_next_q(C)
            S.mm(pA, C.SL[:], GT[:], True, True, [C.consts, GT_b], [pA_b])
            pB, pB_b = P1_next_q(C)
            S.mm(pB, C.ones_f[:], GT[:], True, True, [C.ones_f_b, GT_b], [pB_b])
            pC, pC_b = P1_next_q(C)
            S.mm(pC[:, 0:1], GT[:], C.ones_f[:, 0:1], True, True, [C.ones_f_b, GT_b], [pC_b])
            pAs.append((pA, pA_b)); pBs.append((pB, pB_b)); pCs.append((pC, pC_b))
        for j in range(P1_NT):
            sm, sm_b = C.sm[j]
            pA, pA_b = pAs[j]; pB, pB_b = pBs[j]; pC, pC_b = pCs[j]
            E, E_b = sl['E'][j]
            S.act(E[:], pA, AF.Exp, [pA_b], [E_b])
            egc, egc_b = sl['egc'][j]
            S.act(egc[:], pB, AF.Exp, [pB_b], [egc_b])
            S.copy('vector', sm[:, 0:1], pB[:, 127:128], [pB_b], [sm_b])
            S.act(sm[:, 1:2], pC[:, 0:1], AF.Exp, [pC_b], [sm_b])
            S.act(sm[:, 2:3], pC[:, 0:1], AF.Exp, [pC_b, sm_b], [sm_b], scale=-1.0, bias=sm[:, 0:1])
            S.tt('vector', sm[:, 3:4], sm[:, 1:2], C.bcol[:, j, h:h + 1], ALU.mult, [sm_b, C.bcol_b], [sm_b])
            S.tt('vector', sl['DTi'][j][0][:], E[:], C.triU[:], ALU.mult, [E_b, C.consts], [sl['DTi'][j][1]])
            S.tt('vector', sl['NDs'][j][0][:], E[:], C.negstrict[:], ALU.mult, [E_b, C.consts], [sl['NDs'][j][1]])
            S.tt('vector', sl['QdT'][j][0][:], qT[:, JS[j]], egc[:], ALU.mult, [qT_b, egc_b], [sl['QdT'][j][1]])
        pKs, pVs, pPs, pQs = [], [], [], []
        for j in range(P1_NT):
            pK, pK_b = P1_next_q(C)
            S.tr(pK, kT[:, JS[j]], C.ident[:], [kT_b, C.consts], [pK_b])
            pV, pV_b = P1_next_q(C)
            S.tr(pV, vT[:, JS[j]], C.ident[:], [vT_b, C.consts], [pV_b])
            pP, pP_b = P1_next_q(C)
            S.mm(pP, kT[:, JS[j]], sl['KbT'][j][0][:], True, True, [kT_b, sl['KbT'][j][1]], [pP_b])
            pQ, pQ_b = P1_next_q(C)
            S.mm(pQ, kT[:, JS[j]], qT[:, JS[j]], True, True, [kT_b, qT_b], [pQ_b])
            pKs.append((pK, pK_b)); pVs.append((pV, pV_b)); pPs.append((pP, pP_b)); pQs.append((pQ, pQ_b))
        for j in range(P1_NT):
            sm, sm_b = C.sm[j]
            pK, pK_b = pKs[j]; pV, pV_b = pVs[j]; pP, pP_b = pPs[j]; pQ, pQ_b = pQs[j]
            Ma, Ma_b = sl['Ma'][j]
            S.tt('vector', Ma[:], pP, sl['NDs'][j][0][:], ALU.mult, [pP_b, sl['NDs'][j][1]], [Ma_b])
            S.tt('vector', sl['P'][j][0][:], Ma[:], C.ident[:], ALU.add, [Ma_b, C.consts], [sl['P'][j][1]])
            S.ts('vector', sl['Kbg'][j][0][:], pK, sm[:, 3:4], None, ALU.mult, None, [pK_b, sm_b], [sl['Kbg'][j][1]])
            S.ts('vector', sl['Ktail'][j][0][:], pK, sm[:, 2:3], None, ALU.mult, None, [pK_b, sm_b], [sl['Ktail'][j][1]])
            S.ts('vector', sl['Vb'][j][0][:], pV, C.bcol[:, j, h:h + 1], None, ALU.mult, None, [pV_b, C.bcol_b], [sl['Vb'][j][1]])
            S.tt('vector', sl['QKT'][j][0][:], pQ, sl['DTi'][j][0][:], ALU.mult, [pQ_b, sl['DTi'][j][1]], [sl['QKT'][j][1]])
        for j in range(P1_NT):
            Ma, Ma_b = sl['Ma'][j]
            pT, pT_b = P1_next_q(C)
            S.tr(pT, Ma[:], C.ident[:], [Ma_b, C.consts], [pT_b])
            S.copy('scalar', sl['MTa'][j][0][:], pT, [pT_b], [sl['MTa'][j][1]])
        P1_chk(5)
        cur = ('Ma', 'MTa')
        nxt = ('Mb', 'MTb')
        for lvl in range(1, 7):
            for j in range(P1_NT):
                M, M_b = sl[cur[0]][j]
                MT, MT_b = sl[cur[1]][j]
                M2, M2_b = sl[nxt[0]][j]
                MT2, MT2_b = sl[nxt[1]][j]
                if lvl < 6:
                    p1, p1_b = P1_next_q(C)
                    S.mm(p1, MT[:], M[:], True, True, [MT_b, M_b], [p1_b])
                    S.copy('scalar', M2[:], p1, [p1_b], [M2_b])
                p2, p2_b = P1_next_q(C)
                S.mm(p2, M[:], MT[:], True, True, [MT_b, M_b], [p2_b])
                S.copy('vector' if lvl < 6 else 'scalar', MT2[:], p2, [p2_b], [MT2_b])
            for j in range(P1_NT):
                MT2, MT2_b = sl[nxt[1]][j]
                P, P_b = sl['P'][j]
                p3, p3_b = P1_next_q(C)
                S.mm(p3, MT2[:], P[:], True, True, [MT2_b, P_b], [p3_b])
                S.tt('vector', P[:], P[:], p3, ALU.add, [P_b, p3_b], [P_b])
            cur, nxt = nxt, cur
        P1_chk(6)
        for j in range(P1_NT):
            pW, pW_b = P1_next_q(C)
            S.mm(pW, sl['Kbg'][j][0][:], sl['P'][j][0][:], True, True, [sl['Kbg'][j][1], sl['P'][j][1]], [pW_b])
            S.op('scalar', lambda e, j=j, pW=pW: e.mul(sl['negWT'][j][0][:], pW, -1.0), [pW_b], [sl['negWT'][j][1]])
        P1_chk(7)
        for j in range(P1_NT):
            sm, sm_b = C.sm[j]
            TT, TT_b = sl['P'][j]
            pVn, pVn_b = P1_next_q(C)
            S.mm(pVn, TT[:], sl['Vb'][j][0][:], True, False, [TT_b, sl['Vb'][j][1]], [pVn_b])
            S.mm(pVn, sl['negWT'][j][0][:], St[:], False, True, [sl['negWT'][j][1], St_b], [pVn_b])
            Vn, Vn_b = P1_rot2(C, C.Vn)
            S.copy('scalar', Vn[:], pVn, [pVn_b], [Vn_b])
            pO, pO_b = P1_next_q(C)
            S.mm(pO, sl['QdT'][j][0][:], St[:], True, False, [sl['QdT'][j][1], St_b], [pO_b])
            S.mm(pO, sl['QKT'][j][0][:], Vn[:], False, True, [sl['QKT'][j][1], Vn_b], [pO_b])
            pS, pS_b = P1_next_q(C)
            S.mm(pS, sl['Ktail'][j][0][:], Vn[:], True, True, [sl['Ktail'][j][1], Vn_b], [pS_b])
            S.stt(St[:], St[:], sl['egc'][j][0][:, 127:128], pS, ALU.mult, ALU.add, [St_b, sl['egc'][j][1], pS_b], [St_b])
            P1_head_epilogue_a(C, pO, pO_b, j)
            if j > 0:
                P1_head_epilogue_b(C, C.SZ[h][0], C.SZ[h][1], h, j - 1, True)
        P1_head_epilogue_b(C, C.SZ[h][0], C.SZ[h][1], h, P1_NT - 1, True)
    P1_chk(8)
    for h in range(2):
        St, St_b = C.St[2 + h]
        for which in range(2):
            src = C.PRo[:, 2 + 2 * which + h, :]
            rq, rq_b = C.RQ[2 * which + h]
            xs, xs_b = C.t512[0]
            if which == 1:
                S.op('scalar', lambda e, xs=xs, src=src: e.mul(xs[:], src, P1_ISQ), [C.PRo_b], [xs_b])
            else:
                S.copy('scalar', xs[:], src, [C.PRo_b], [xs_b])
            pt, pbs = P1_next_bank(C)
            S.mm(pt[:, :P1_TG], C.permT[:], xs[:], True, True, [C.consts, xs_b], pbs)
            S.tt('vector', rq[:], xs[:], C.cs[0][0][:], ALU.mult, [xs_b, C.cs[0][1]], [rq_b])
            t2, t2_b = C.t512[2]
            S.tt('vector', t2[:], pt[:, :P1_TG], C.cs[1][0][:], ALU.mult, pbs + [C.cs[1][1]], [t2_b])
            S.tt('vector', rq[:], rq[:], t2[:], ALU.add, [rq_b, t2_b], [rq_b])
        qr, qr_b = C.RQ[h]
        kr, kr_b = C.RQ[2 + h]
        vT = C.PRo[:, 6 + h, :]
        qd, qd_b = C.QD[h]
        S.tt('vector', qd[:], qr[:], C.qdrow[:, h, :], ALU.mult, [qr_b, C.consts], [qd_b])
        for j in range(P1_NT):
            js = slice(j * 128, (j + 1) * 128)
            pK, pK_b = P1_next_q(C)
            S.tr(pK, kr[:, js], C.ident[:], [kr_b, C.consts], [pK_b])
            Kd, Kd_b = sl['Kbg'][j]
            S.ts('vector', Kd[:], pK, C.kdcol[:, h:h + 1], None, ALU.mult, None, [pK_b, C.consts], [Kd_b])
            pV, pV_b = P1_next_q(C)
            S.tr(pV, vT[:, js], C.ident[:], [C.PRo_b, C.consts], [pV_b])
            V, V_b = sl['Vb'][j]
            S.copy('vector', V[:], pV, [pV_b], [V_b])
            pQ, pQ_b = P1_next_q(C)
            S.mm(pQ, kr[:, js], qr[:, js], True, True, [kr_b, qr_b], [pQ_b])
            QKT, QKT_b = sl['QKT'][j]
            S.tt('vector', QKT[:], pQ, C.DT[:, h, :], ALU.mult, [pQ_b, C.consts], [QKT_b])
            pO, pO_b = P1_next_q(C)
            S.mm(pO, QKT[:], V[:], True, False, [QKT_b, V_b], [pO_b])
            S.mm(pO, qd[:, js], St[:], False, True, [qd_b, St_b], [pO_b])
            pS, pS_b = P1_next_q(C)
            S.mm(pS, Kd[:], V[:], True, True, [Kd_b, V_b], [pS_b])
            S.stt(St[:], St[:], C.kdcol[:, 2 + h:3 + h], pS, ALU.mult, ALU.add, [St_b, C.consts, pS_b], [St_b])
            P1_head_epilogue_a(C, pO, pO_b, j)
            if j > 0:
                P1_head_epilogue_b(C, C.SZ[2 + h][0], C.SZ[2 + h][1], 2 + h, j - 1, False)
        P1_head_epilogue_b(C, C.SZ[2 + h][0], C.SZ[2 + h][1], 2 + h, P1_NT - 1, False)
    for o in range(4):
        if C.oT_b is not None:
            S.dma('sync', dr['oloc'][gi][o * 128:(o + 1) * 128, :], C.OUT[:, o, :], reads=[C.OUT_b],
                  writes=[C.oT_b], sembuf=C.OUT_b)
        else:
            S.dma('sync', dr['oT'][o * 128:(o + 1) * 128, t0:t0 + P1_TG], C.OUT[:, o, :], reads=[C.OUT_b])


def build_p1(L, ngroups=None):
    nc = bass.Bass("TRN2", target_bir_lowering=False)
    dr = p1_dram(nc, L)
    with contextlib.ExitStack() as st:
        S = Sched(nc, st)
        C = p1_alloc(S, nc)
        p1_setup(C, dr)
        try:
            for gi in range(ngroups if ngroups else L // P1_TG):
                p1_group(C, dr, gi, L)
        except P1_StopBuild:
            pass
        S.finish()
        print("p1 ops", S.stats(), "sems", S.nsem, "sbuf left", nc.sbuf_bytes_remaining)
        S.emit()
    return nc


def p1_host_consts(L):
    f = np.float32
    d = {}
    d['ident'] = np.eye(128, dtype=f)
    d['triU'] = np.triu(np.ones((128, 128), f))
    d['SL'] = np.tril(np.ones((128, 128), f), -1)
    d['negstrict'] = -np.triu(np.ones((128, 128), f), 1)
    pm = np.zeros((128, 128), f)
    for m in range(64):
        pm[m + 64, m] = -1.0
        pm[m, m + 64] = 1.0
    d['permT'] = pm
    half = 64
    inv_freq = (np.float32(1.0) / np.power(np.float32(10000.0), np.linspace(0.0, 1.0, half, dtype=np.float32))).astype(f)
    pos = np.arange(L, dtype=f)
    ang = (pos[:, None] * inv_freq[None, :]).astype(f).astype(np.float64)
    cos = np.cos(ang).astype(f).T
    sin = np.sin(ang).astype(f).T
    d['cosT'] = np.ascontiguousarray(np.concatenate([cos, cos], 0))
    d['sinT'] = np.ascontiguousarray(np.concatenate([sin, sin], 0))
    return d


def p1_host_core(inp, b, hg, L, consts):
    f = np.float32
    d = dict(consts)
    d['x'] = np.ascontiguousarray(inp['x'][b, :L])
    w_in = inp['la_w_in'][0]
    offs = {'gq': 0, 'gk': 1024, 'gv': 2048, 'gz': 3072, 'gb': 4096, 'ga': 4104, 'rq': 4112, 'rk': 5136, 'rv': 6160,
            'rg': 7184}
    cols = []
    for nm in ['gq', 'gk', 'gv', 'gz', 'rq', 'rk', 'rv', 'rg']:
        for h in range(2):
            hh = 2 * hg + h
            cols.append(w_in[:, offs[nm] + hh * 128: offs[nm] + (hh + 1) * 128])
    sc = np.zeros((2048, 256), f)
    for h in range(2):
        sc[:, h] = w_in[:, offs['gb'] + 2 * hg + h]
        sc[:, 2 + h] = w_in[:, offs['ga'] + 2 * hg + h]
    cols.append(sc)
    d['w_loc'] = np.ascontiguousarray(np.concatenate(cols, 1))
    d['nw0T'] = np.ascontiguousarray(inp['norm_w'][0, 0].reshape(16, 128).T)
    cw = inp['la_conv_w'][0]
    cwl = np.zeros((128, 6, 4), f)
    for t, base in enumerate([0, 1024, 2048]):
        for h in range(2):
            hh = 2 * hg + h
            cwl[:, t * 2 + h, :] = cw[base + hh * 128: base + (hh + 1) * 128, :]
    d['cw'] = np.ascontiguousarray(cwl.reshape(128, 24))
    d['nexpA_in'] = np.ascontiguousarray(inp['la_a_log'][0, 2 * hg:2 * hg + 2].reshape(2, 1))
    d['dtb'] = np.ascontiguousarray(inp['la_dt_bias'][0, 2 * hg:2 * hg + 2].reshape(2, 1))
    d['onw'] = np.ascontiguousarray(inp['la_out_norm_w'][0].reshape(128, 1))
    DT = np.zeros((128, 2, 128), np.float64)
    qd = np.zeros((128, 2, 512), np.float64)
    kd = np.zeros((128, 4), np.float64)
    s = np.arange(128)
    for h in range(2):
        lg = np.log1p(-np.float64(2.0) ** (-5.0 - (2 * hg + h)))
        diff = s[None, :] - s[:, None]
        DT[:, h, :] = np.where(diff >= 0, np.exp(diff * lg), 0.0)
        qd[:, h, :] = np.tile(np.exp((s + 1.0) * lg), 4)[None, :]
        kd[:, h] = np.exp((127.0 - s) * lg)
        kd[:, 2 + h] = np.exp(128.0 * lg)
    d['DT'] = np.ascontiguousarray(DT.reshape(128, 256).astype(f))
    d['qdrow'] = np.ascontiguousarray(qd.reshape(128, 1024).astype(f))
    d['kdcol'] = np.ascontiguousarray(kd.astype(f))
    sel = np.zeros((2, 2, 128), f)
    sel[0, 0, :] = 1.0
    sel[1, 1, :] = 1.0
    d['sel'] = np.ascontiguousarray(sel.reshape(2, 256))
    return d


P2_TG = 512
P2_D = 2048
P2_EPS = 1e-6


class P2_Ctx:
    pass


def p2_alloc(S, nc):
    C = P2_Ctx()
    C.S = S
    C.consts = Buf("consts", multi=True)
    C.ident, _ = S.sb("ident", [128, 128], F32)
    C.ones_f, C.ones_f_b = S.sb("ones_f", [128, 128], F32)
    C.ones16, C.ones16_b = S.sb("ones16", [128, 128], BF16)
    C.nwT, _ = S.sb("nwT", [128, 8, 16], F32)
    C.lnwT, _ = S.sb("lnwT", [128, 32], F32)
    C.lnbT, _ = S.sb("lnbT", [128, 32], F32)
    C.bs_bc, _ = S.sb("bs_bc", [128, 8, 128], F32)
    C.wsm_f, C.wsm_f_b = S.sb("wsm_f", [128, 8, 128], F32)
    C.wsm16, C.wsm16_b = S.sb("wsm16", [128, 8, 128], BF16)
    C.maskT, _ = S.sb("maskT", [128, 128], F32)
    C.bias2, C.bias2_b = S.sb("bias2", [128, 32, 128], F32)
    C.H, C.H_b = S.sb("H", [128, 16, P2_TG], F32)
    C.XN, C.XN_b = S.sb("XN", [128, 16, P2_TG], BF16)
    C.A, C.A_b = S.sb("A", [128, 64, P2_TG], BF16)
    C.Y, C.Y_b = S.sb("Y", [128, 16, P2_TG], F32)
    C.wb = [S.sb(f"wb{i}", [128, 16, 256], BF16) for i in range(3)]
    C.tmp = [S.sb(f"tmp{i}", [128, P2_TG], F32) for i in range(2)]
    C.rstd, C.rstd_b = S.sb("rstd", [128, P2_TG], F32)
    C.st, C.st_b = S.sb("st", [128, 32], F32)
    C.ps = [S.ps(f"ps{i}", [128, 512], F32) for i in range(8)]
    C.wscr = None
    C.blk_i = 0
    C.bank_i = 0
    C.w_i = 0
    C.tmp_i = 0
    C.ev_i = 0
    return C


def getq(C, e, name, mkview):
    if not hasattr(C, 'qcache'):
        C.qcache = {}
    if name not in C.qcache:
        C.qcache[name] = mkview(e.snap(e.partition_id() % 4))
    return C.qcache[name]


def P2_next_bank(C):
    b = C.ps[C.bank_i % 8]
    C.bank_i += 1
    return b


def P2_next_w(C):
    w = C.wb[C.w_i % 3]
    C.w_i += 1
    return w


def P2_next_tmp(C):
    t = C.tmp[C.tmp_i % 2]
    C.tmp_i += 1
    return t


def P2_ev_eng(C):
    C.ev_i += 1
    return 'scalar' if C.ev_i % 2 else 'vector'


def wload(C, wt, wbuf, src):
    S = C.S
    if getattr(C, 'wscr', None) is not None:
        i = C.blk_i % len(C.blk_list)
        C.blk_i += 1
        assert str(C.blk_list[i]) == str(src), (i, C.blk_list[i], src)
        S.dma('sync', wt[:].rearrange("p a b -> p (a b)"), C.wscr[i], writes=[wbuf])
    else:
        S.dma('gpsimd', wt[:], src.rearrange("(kc p) n -> p kc n", p=128), writes=[wbuf])


def p2_block_list(dr):
    L_ = []
    def lin(W, K, n0, N):
        for cb in range(N // 256):
            for kb in range(K // 2048):
                L_.append(W[kb * 2048:(kb + 1) * 2048, n0 + cb * 256:n0 + (cb + 1) * 256])
    lin(dr['w_out0'], 2048, 0, 2048)
    lin(dr['w_up'][0], 2048, 0, 8192)
    lin(dr['w_dn'][0], 8192, 0, 2048)
    lin(dr['sg_in'], 2048, 4096, 4096)
    lin(dr['sg_in'], 2048, 0, 4096)
    lin(dr['sg_out'], 4096, 0, 2048)
    lin(dr['w_up'][1], 2048, 0, 8192)
    lin(dr['w_dn'][1], 8192, 0, 2048)
    return L_


def P2_linear_ws(C, xview, xbuf, K, W, n0, N, evac, tg=P2_TG):
    S = C.S
    KB = K // 2048
    for cb in range(N // 256):
        banks = [P2_next_bank(C), P2_next_bank(C)]
        for kb in range(KB):
            wt, wbuf = P2_next_w(C)
            wload(C, wt, wbuf, W[kb * 2048:(kb + 1) * 2048, n0 + cb * 256:n0 + (cb + 1) * 256])
            for cc in range(2):
                pt, pb = banks[cc]
                for kc in range(16):
                    S.mm(pt[:, :tg], wt[:, kc, cc * 128:(cc + 1) * 128], xview(kb * 16 + kc),
                         kb == 0 and kc == 0, kb == KB - 1 and kc == 15, [wbuf, xbuf], [pb])
        for cc in range(2):
            evac(cb * 2 + cc, banks[cc][0], banks[cc][1])


def P2_rms_rstd(C, src, src_b, nch, dim, tg=P2_TG):
    S = C.S
    sq = C.A[:, 0:nch, :]
    S.act(sq[:, :, :tg], src, AF.Square, [src_b], [C.A_b])
    pt, pb = P2_next_bank(C)
    for c in range(nch):
        S.mm(pt[:, :tg], C.ones16[:], sq[:, c, :tg], c == 0, c == nch - 1, [C.ones16_b, C.A_b], [pb])
    S.ts('vector', C.rstd[:, :tg], pt[:, :tg], 1.0 / dim, P2_EPS, ALU.mult, ALU.add, [pb], [C.rstd_b])
    S.act(C.rstd[:, :tg], C.rstd[:, :tg], AF.Sqrt, [C.rstd_b], [C.rstd_b])
    S.op('vector', lambda e: e.reciprocal(C.rstd[:, :tg], C.rstd[:, :tg]), [C.rstd_b], [C.rstd_b])


def P2_residual_norm(C, j):
    S = C.S
    P2_rms_rstd(C, C.Y[:], C.Y_b, 16, P2_D)
    for c in range(16):
        S.stt(C.Y[:, c, :], C.Y[:, c, :], C.nwT[:, j, c:c + 1], C.rstd[:], ALU.mult, ALU.mult,
              [C.Y_b, C.consts, C.rstd_b], [C.Y_b])
        S.tt('vector', C.H[:, c, :], C.H[:, c, :], C.Y[:, c, :], ALU.add, [C.H_b, C.Y_b], [C.H_b])


def P2_prenorm(C, j):
    S = C.S
    P2_rms_rstd(C, C.H[:], C.H_b, 16, P2_D)
    for c in range(16):
        S.stt(C.XN[:, c, :], C.H[:, c, :], C.nwT[:, j, c:c + 1], C.rstd[:], ALU.mult, ALU.mult,
              [C.H_b, C.consts, C.rstd_b], [C.XN_b])


def P2_ffn(C, w_up, w_dn):
    S = C.S

    def ev_up(c, pt, pb):
        t, tb = P2_next_tmp(C)
        S.act(t[:], pt[:, :P2_TG], AF.Relu, [pb], [tb])
        S.tt('vector', C.A[:, c, :], t[:], t[:], ALU.mult, [tb], [C.A_b])

    P2_linear_ws(C, lambda kc: C.XN[:, kc, :], C.XN_b, 2048, w_up, 0, 8192, ev_up)

    def ev_dn(c, pt, pb):
        S.copy(P2_ev_eng(C), C.Y[:, c, :], pt[:, :P2_TG], [pb], [C.Y_b])

    P2_linear_ws(C, lambda kc: C.A[:, kc, :], C.A_b, 8192, w_dn, 0, 2048, ev_dn)


def p2_setup(C, dr):
    S = C.S
    S.dma('sync', C.ident[:], dr['ident'], writes=[C.consts])
    S.dma('sync', C.nwT[:], dr['nwT'].rearrange("p (j c) -> p j c", j=8), writes=[C.consts])
    S.dma('sync', C.lnwT[:], dr['lnwT'], writes=[C.consts])
    S.dma('sync', C.lnbT[:], dr['lnbT'], writes=[C.consts])
    S.dma('sync', C.maskT[:], dr['maskT'], writes=[C.consts])
    S.dma('sync', C.wsm_f[:], dr['wsT'].rearrange("p (g t) -> p g t", g=8), writes=[C.wsm_f_b])
    S.dma('sync', C.bs_bc[:].rearrange("p g t -> p (g t)"), dr['bs'].partition_broadcast(128), writes=[C.consts])
    S.op('vector', lambda e: e.memset(C.ones_f[:], 1.0), writes=[C.ones_f_b])
    S.copy('vector', C.ones16[:], C.ones_f[:], [C.ones_f_b], [C.ones16_b])
    for g in range(8):
        S.tt('vector', C.wsm_f[:, g, :], C.wsm_f[:, g, :], C.maskT[:], ALU.mult, [C.wsm_f_b, C.consts], [C.wsm_f_b])
    S.copy('vector', C.wsm16[:], C.wsm_f[:], [C.wsm_f_b], [C.wsm16_b])
    for g in range(8):
        pt, pb = P2_next_bank(C)
        S.mm(pt[:, :128], C.ones_f[:], C.wsm_f[:, g, :], True, True, [C.ones_f_b, C.wsm_f_b], [pb])
        for q in range(4):
            c = g * 4 + q
            S.stt(C.bias2[:, c, :], pt[:, :128], C.lnbT[:, c:c + 1], C.bs_bc[:, g, :], ALU.mult, ALU.add,
                  [pb, C.consts], [C.bias2_b])


def p2_group(C, dr, gi, do_l0=True, do_l1=True, fused=False, TQ=2048):
    S = C.S
    t0 = gi * P2_TG
    NT = P2_TG // 128
    Yt = C.Y[:].rearrange("p c t -> p (c t)").rearrange("p (j d) -> p j d", j=NT)
    C.Y_b.multi = True
    for j in range(NT):
        if fused:
            def fx(e, j=j):
                xv = getq(C, e, 'sync', lambda qb: dr['x'][bass.ds(qb * TQ, TQ), :])
                return e.dma_start(out=Yt[:, j, :], in_=xv[t0 + j * 128:t0 + (j + 1) * 128, :])
            S._add('sync', fx, [], [C.Y_b], dma_buf=C.Y_b)
        else:
            S.dma('sync', Yt[:, j, :], dr['x'][t0 + j * 128:t0 + (j + 1) * 128, :], writes=[C.Y_b])
    for j in range(NT):
        for c4 in range(4):
            pt, pb = P2_next_bank(C)
            for q in range(4):
                c = c4 * 4 + q
                S.tr(pt[:, q * 128:(q + 1) * 128], Yt[:, j, c * 128:(c + 1) * 128], C.ident[:], [C.Y_b, C.consts], [pb])
            S.copy(P2_ev_eng(C), C.H[:, c4 * 4:(c4 + 1) * 4, j * 128:(j + 1) * 128],
                   pt[:].rearrange("p (q n) -> p q n", q=4), [pb], [C.H_b])

    def ev_y(c, pt, pb):
        S.copy(P2_ev_eng(C), C.Y[:, c, :], pt[:, :P2_TG], [pb], [C.Y_b])

    if do_l0:
        if fused:
            def fo(e):
                gv = getq(C, e, 'gpsimd', lambda qb: dr['ogath'][bass.ds(qb * (TQ // P2_TG), TQ // P2_TG)])
                return e.dma_start(out=C.XN[:], in_=gv[gi:gi + 1].rearrange("o (c p) t -> p (o c) t", p=128))
            S._add('gpsimd', fo, [C.ogath_b], [C.XN_b], dma_buf=C.XN_b)
        else:
            S.dma('gpsimd', C.XN[:], dr['oT'][:, t0:t0 + P2_TG].rearrange("(c p) t -> p c t", p=128), writes=[C.XN_b])
        P2_linear_ws(C, lambda kc: C.XN[:, kc, :], C.XN_b, 2048, dr['w_out0'], 0, 2048, ev_y)
        P2_residual_norm(C, 1)
        P2_prenorm(C, 2)
        P2_ffn(C, dr['w_up'][0], dr['w_dn'][0])
        P2_residual_norm(C, 3)
    if do_l1:
        P2_prenorm(C, 4)
        Vt = C.A[:].rearrange("p c t -> p (c t)").bitcast(F32).rearrange("p (j n) -> p j n", j=NT)
        Vh = C.Y[:].rearrange("p c t -> p (c t)").bitcast(BF16).rearrange("p (j n) -> p j n", j=NT)
        for cb in range(16):
            banks = [P2_next_bank(C) for _ in range(NT)]
            wt, wbuf = P2_next_w(C)
            wload(C, wt, wbuf, dr['sg_in'][0:2048, 4096 + cb * 256:4096 + (cb + 1) * 256])
            for j in range(NT):
                pt, pb = banks[j]
                for kc in range(16):
                    S.mm(pt[:, :256], C.XN[:, kc, j * 128:(j + 1) * 128], wt[:, kc, :], kc == 0, kc == 15,
                         [wbuf, C.XN_b], [pb])
            for j in range(NT):
                pt, pb = banks[j]
                S.act(Vt[:, j, cb * 256:(cb + 1) * 256], pt[:, :256], AF.Gelu_apprx_tanh, [pb], [C.A_b])
        s1 = C.st[:, 0:4]
        s2 = C.st[:, 4:8]
        mu = C.st[:, 8:12]
        rs = C.st[:, 12:16]
        nmr = C.st[:, 16:20]
        jt, junk_b = P2_next_w(C)
        junk = jt[:].rearrange("p a b -> p (a b)")
        for j in range(NT):
            S.act(junk, Vt[:, j, :], AF.Copy, [C.A_b], [junk_b, C.st_b], accum_out=s1[:, j:j + 1])
            S.act(junk, Vt[:, j, :], AF.Square, [C.A_b], [junk_b, C.st_b], accum_out=s2[:, j:j + 1])
        S.ts('vector', mu, s1, 1.0 / 4096, None, ALU.mult, None, [C.st_b], [C.st_b])
        S.ts('vector', s2, s2, 1.0 / 4096, None, ALU.mult, None, [C.st_b], [C.st_b])
        S.tt('vector', rs, mu, mu, ALU.mult, [C.st_b], [C.st_b])
        S.tt('vector', rs, s2, rs, ALU.subtract, [C.st_b], [C.st_b])
        S.ts('vector', rs, rs, P2_EPS, None, ALU.add, None, [C.st_b], [C.st_b])
        S.act(rs, rs, AF.Sqrt, [C.st_b], [C.st_b])
        S.op('vector', lambda e: e.reciprocal(rs, rs), [C.st_b], [C.st_b])
        S.tt('vector', nmr, mu, rs, ALU.mult, [C.st_b], [C.st_b])
        S.ts('vector', nmr, nmr, -1.0, None, ALU.mult, None, [C.st_b], [C.st_b])
        for j in range(NT):
            S.act(Vh[:, j, :], Vt[:, j, :], AF.Identity, [C.A_b, C.st_b], [C.Y_b],
                  scale=rs[:, j:j + 1], bias=nmr[:, j:j + 1])

        def ev_u(c, pt, pb):
            S.act(C.A[:, c, :], pt[:, :P2_TG], AF.Gelu_apprx_tanh, [pb], [C.A_b])

        P2_linear_ws(C, lambda kc: C.XN[:, kc, :], C.XN_b, 2048, dr['sg_in'], 0, 4096, ev_u)
        for c in range(32):
            g = c // 4
            pt, pb = P2_next_bank(C)
            for j in range(NT):
                S.mm(pt[:, j * 128:(j + 1) * 128], Vh[:, j, c * 128:(c + 1) * 128], C.wsm16[:, g, :], True, True,
                     [C.Y_b, C.wsm16_b], [pb])
            t, tb = P2_next_tmp(C)
            S.stt(t[:].rearrange("p (j n) -> p j n", j=NT), pt[:].rearrange("p (j n) -> p j n", j=NT),
                  C.lnwT[:, c:c + 1], C.bias2[:, c:c + 1, :].to_broadcast([128, NT, 128]), ALU.mult, ALU.add,
                  [pb, C.consts, C.bias2_b], [tb])
            S.tt('vector', C.A[:, c, :], t[:], C.A[:, c, :], ALU.mult, [tb, C.A_b], [C.A_b])
        P2_linear_ws(C, lambda kc: C.A[:, kc, :], C.A_b, 4096, dr['sg_out'], 0, 2048, ev_y)
        P2_residual_norm(C, 5)
        P2_prenorm(C, 6)
        P2_ffn(C, dr['w_up'][1], dr['w_dn'][1])
        P2_residual_norm(C, 7)
    for j in range(NT):
        for c4 in range(4):
            pt, pb = P2_next_bank(C)
            for q in range(4):
                c = c4 * 4 + q
                S.tr(pt[:, q * 128:(q + 1) * 128], C.H[:, c, j * 128:(j + 1) * 128], C.ident[:], [C.H_b, C.consts], [pb])
            S.copy(P2_ev_eng(C), Yt[:, j, c4 * 512:(c4 + 1) * 512], pt[:], [pb], [C.Y_b])
        S.dma('sync', dr['out'][t0 + j * 128:t0 + (j + 1) * 128, :], Yt[:, j, :], reads=[C.Y_b])


def p2_dram(nc, T, fused=False):
    dr = {}
    def inp(name, shape):
        dr[name] = nc.dram_tensor(name, list(shape), F32, kind="ExternalInput").ap()
    if not fused:
        inp('x', [T, P2_D]); inp('oT', [P2_D, T]); inp('ident', [128, 128])
    inp('w_out0', [2048, 2048]); inp('w_up', [2, 2048, 8192]); inp('w_dn', [2, 8192, 2048])
    inp('sg_in', [2048, 8192]); inp('sg_out', [4096, 2048])
    inp('nwT', [128, 128]); inp('lnwT', [128, 32]); inp('lnbT', [128, 32]); inp('wsT', [128, 1024])
    inp('maskT', [128, 128]); inp('bs', [1024])
    dr['out'] = nc.dram_tensor("out", [T, P2_D], F32, kind="ExternalOutput").ap()
    return dr


def build_p2(T=2048, ngroups=None, do_l0=True, do_l1=True):
    nc = bass.Bass("TRN2", target_bir_lowering=False)
    dr = p2_dram(nc, T)
    with contextlib.ExitStack() as st:
        S = Sched(nc, st)
        C = p2_alloc(S, nc)
        p2_setup(C, dr)
        for gi in range(ngroups if ngroups else T // P2_TG):
            p2_group(C, dr, gi, do_l0, do_l1)
        S.finish()
        print("p2 ops", S.stats(), "sems", S.nsem, "sbuf left", nc.sbuf_bytes_remaining)
        S.emit()
    return nc


def p2_host_inputs(inp):
    f = np.float32
    d = {}
    d['w_out0'] = np.ascontiguousarray(inp['la_w_out'][0])
    d['w_up'] = inp['ffn_w_up']
    d['w_dn'] = inp['ffn_w_down']
    d['sg_in'] = np.ascontiguousarray(inp['sg_w_in'][0])
    d['sg_out'] = np.ascontiguousarray(inp['sg_w_out'][0])
    d['nwT'] = np.ascontiguousarray(inp['norm_w'].reshape(8, 16, 128).transpose(2, 0, 1).reshape(128, 128))
    d['lnwT'] = np.ascontiguousarray(inp['sg_ln_w'][0].reshape(32, 128).T)
    d['lnbT'] = np.ascontiguousarray(inp['sg_ln_b'][0].reshape(32, 128).T)
    d['wsT'] = np.ascontiguousarray(inp['sg_w_s'][0].transpose(2, 0, 1).reshape(128, 1024))
    d['maskT'] = np.triu(np.ones((128, 128), f))
    d['bs'] = np.ascontiguousarray(inp['sg_b_s'][0].reshape(1024))
    d['ident'] = np.eye(128, dtype=f)
    return d


def gath_perm():
    perm = []
    for r in range(4):
        for k in range(4):
            perm.append((0 if k < 2 else 8) + 2 * r + (k % 2))
    return perm


import os
DBG = int(os.environ.get('FUSED_DBG', '0'))


def build_fused(L=8192):
    nc = bass.Bass("TRN2", target_bir_lowering=False)
    TQ = L // 4
    dr = p1_dram(nc, L, fused=True)
    dr2 = p2_dram(nc, TQ, fused=True)
    for k, v in dr2.items():
        dr[k] = v
    with contextlib.ExitStack() as semstack:
        oloc_b = Buf("oloc", multi=True)
        ogath_b = Buf("ogath")
        with contextlib.ExitStack() as st1:
            S1 = Sched(nc, st1, semstack, prefix="a_")
            C1 = p1_alloc(S1, nc)
            C1.oT_b = oloc_b
            p1_setup(C1, dr)
            rg = [[0, 1, 2, 3], [4, 5, 6, 7]]
            ogath_b.multi = True

            def gather(g):
                S1._add('gpsimd', lambda e: e.collective_compute("AllGather", ALU.bypass, replica_groups=rg,
                                                                 ins=[dr['oloc'][g].opt()], outs=[dr['ogath'][g].opt()]),
                        [oloc_b], [ogath_b], dma_buf=ogath_b, inc=1)

            NG1 = L // 512
            blk_list = p2_block_list(dr)
            NB = len(blk_list)
            wscr = nc.dram_tensor("wscr", [NB, 128, 4096], BF16, kind="Internal").ap()
            wscr_b = Buf("wscr", multi=True)
            per = (NB + NG1 - 1) // NG1

            def convert(g):
                for i in range(g * per, min(NB, (g + 1) * per)):
                    S1.dma('gpsimd', wscr[i].rearrange("p (kc n) -> p kc n", kc=16),
                           blk_list[i].rearrange("(kc p) n -> p kc n", p=128), writes=[wscr_b])

            def hook(g):
                if g > 0:
                    gather(g - 1)
                convert(g)

            for gi in range(NG1):
                p1_group(C1, dr, gi, L, hook=(lambda g=gi: hook(g)))
            if DBG != 1:
                gather(NG1 - 1)
            S1.finish(barrier=True)
            S1.emit()
        with contextlib.ExitStack() as st2:
            S2 = Sched(nc, st2, semstack, prefix="b_")
            C2 = p2_alloc(S2, nc)
            C2.ogath_b = Buf("ogath2")
            C2.wscr = wscr
            C2.blk_list = blk_list
            p2_setup(C2, dr)
            for gi in range(TQ // 512 if DBG not in (1, 2) else 0):
                p2_group(C2, dr, gi, fused=True, TQ=TQ)
            S2.finish()
            S2.emit()
    return nc


def fused_in_maps(inp, L=8192):
    consts = p1_host_consts(L)
    shared = p2_host_inputs(inp)
    perm = gath_perm()
    w = shared['w_out0']
    shared['w_out0'] = np.ascontiguousarray(np.concatenate([w[f * 128:(f + 1) * 128] for f in perm], 0))
    maps = []
    for c in range(8):
        m = p1_host_core(inp, c // 4, c % 4, L, consts)
        for k, v in shared.items():
            if k not in m:
                m[k] = v
        maps.append(m)
    return maps


def kernel(**inputs):
    inp = {k: np.asarray(v) for k, v in inputs.items()}
    B, L = 2, 8192
    nc = build_fused(L)
    maps = fused_in_maps(inp, L)
    res = run_bass_kernel_spmd(nc, maps, core_ids=list(range(8)))
    out = np.empty((B, L, 2048), np.float32)
    for c in range(8):
        b, j = c // 4, c % 4
        out[b, j * (L // 4):(j + 1) * (L // 4)] = res.results[c]['out']
    return out
```

```python
import contextlib
import numpy as np
import concourse.bass as bass
import concourse.mybir as mybir
from concourse.bass_utils import run_bass_kernel_spmd

F32 = mybir.dt.float32
BF16 = mybir.dt.bfloat16
ALU = mybir.AluOpType
AF = mybir.ActivationFunctionType
AX = mybir.AxisListType

ENGS = ['tensor', 'vector', 'scalar', 'gpsimd', 'sync']
SAME_ENG_SYNC = True


class Buf:
    def __init__(self, name, multi=False):
        self.name = name
        self.w = None
        self.r = {}
        self.multi = multi
        self.sem = None
        self.semcnt = 0


class Sched:
    def __init__(self, nc, stack, semstack=None, prefix=""):
        self.nc = nc
        self.stack = stack
        self.semstack = semstack if semstack is not None else stack
        self.prefix = prefix
        self.ops = {e: [] for e in ENGS}
        self.seen = {e: {} for e in ENGS}
        self.esem = {e: self.semstack.enter_context(nc.semaphore(prefix + "prog_" + e)) for e in ENGS}
        self.dma_bufs = []
        self.nsem = len(ENGS)

    def sb(self, name, shape, dtype=F32, multi=False):
        t = self.stack.enter_context(self.nc.sbuf_tensor(self.prefix + "s_" + name, list(shape), dtype))
        b = Buf(name, multi)
        return t, b

    def ps(self, name, shape, dtype=F32):
        t = self.stack.enter_context(self.nc.psum_tensor(self.prefix + "p_" + name, list(shape), dtype))
        return t, Buf(name)

    def _dsem(self, b):
        if b.sem is None:
            b.sem = self.semstack.enter_context(self.nc.semaphore(self.prefix + "d_" + b.name))
            self.dma_bufs.append(b)
            self.nsem += 1
        return b.sem

    def _add(self, eng, fn, reads, writes, dma_buf=None, inc=16):
        needs = []
        for b in reads:
            if b.w is not None:
                needs.append(b.w)
        for b in writes:
            if b.w is not None and not (dma_buf is not None and b.multi):
                needs.append(b.w)
            needs.extend(b.r.values())
        waits = []
        seen = self.seen[eng]
        for ev in needs:
            if ev[0] == 'e' and ev[1] == eng and (eng == 'tensor' or not SAME_ENG_SYNC):
                continue
            key = (ev[0], ev[1] if ev[0] == 'e' else id(ev[1]))
            if seen.get(key, -1) >= ev[2]:
                continue
            seen[key] = ev[2]
            waits.append(ev)
            if ev[0] == 'e':
                self.ops[ev[1]][ev[2]]['ms'] = True
        idx = len(self.ops[eng])
        if dma_buf is not None:
            sem = self._dsem(dma_buf)
            dma_buf.semcnt += inc
            event = ('d', sem, dma_buf.semcnt)
            rkey = ('d', id(sem))
        else:
            sem = None
            event = ('e', eng, idx)
            rkey = ('e', eng)
        self.ops[eng].append(dict(fn=fn, waits=waits, ms=False, dma=sem, inc=inc))
        for b in reads:
            b.r[rkey] = event
        for b in writes:
            b.w = event
            b.r = {}
        return event

    def op(self, eng, fn, reads=(), writes=()):
        return self._add(eng, fn, list(reads), list(writes))

    def dma(self, eng, out, in_, reads=(), writes=(), sembuf=None):
        reads = list(reads)
        writes = list(writes)
        if sembuf is None:
            sembuf = writes[0] if writes else reads[0]
        return self._add(eng, lambda e: e.dma_start(out=out, in_=in_), reads, writes, dma_buf=sembuf)

    def mm(self, out, lhsT, rhs, start, stop, reads, writes):
        return self.op('tensor', lambda e: e.matmul(out, lhsT, rhs, start=start, stop=stop), reads, writes)

    def tr(self, out, in_, ident, reads, writes):
        return self.op('tensor', lambda e: e.transpose(out, in_, ident), reads, writes)

    def act(self, out, in_, func, reads, writes, eng='scalar', **kw):
        return self.op(eng, lambda e: e.activation(out, in_, func, **kw), reads, writes)

    def tt(self, eng, out, in0, in1, op, reads, writes):
        return self.op(eng, lambda e: e.tensor_tensor(out, in0, in1, op), reads, writes)

    def ts(self, eng, out, in0, s1, s2, op0, op1, reads, writes):
        if op1 is None:
            return self.op(eng, lambda e: e.tensor_scalar(out, in0, s1, None, op0), reads, writes)
        return self.op(eng, lambda e: e.tensor_scalar(out, in0, s1, s2, op0, op1), reads, writes)

    def stt(self, out, in0, scalar, in1, op0, op1, reads, writes, eng='vector'):
        return self.op(eng, lambda e: e.scalar_tensor_tensor(out, in0, scalar, in1, op0, op1), reads, writes)

    def copy(self, eng, out, in_, reads, writes):
        if eng == 'scalar':
            return self.op(eng, lambda e: e.copy(out, in_), reads, writes)
        return self.op(eng, lambda e: e.tensor_copy(out, in_), reads, writes)

    def finish(self, eng='sync', barrier=False):
        waits = [('d', b.sem, b.semcnt) for b in self.dma_bufs if b.semcnt > 0]
        if not barrier:
            self.ops[eng].append(dict(fn=None, waits=waits, ms=False, dma=None, inc=0))
            return
        last = {}
        for e in ENGS:
            idx = None
            for i in range(len(self.ops[e]) - 1, -1, -1):
                o = self.ops[e][i]
                if o['fn'] is not None and o['dma'] is None:
                    idx = i
                    break
            if idx is not None:
                self.ops[e][idx]['ms'] = True
                last[e] = idx
        for e in ENGS:
            w = list(waits) + [('e', e2, i2) for e2, i2 in last.items() if e2 != e]
            self.ops[e].append(dict(fn=None, waits=w, ms=False, dma=None, inc=0))

    def emit(self):
        rank = {}
        for eng in ENGS:
            c = 0
            r = []
            for o in self.ops[eng]:
                if o['ms']:
                    c += 1
                r.append(c)
            rank[eng] = r
        self.rank = rank
        esem = self.esem

        def make(eng):
            ops = self.ops[eng]

            def body(e):
                for o in ops:
                    for ev in o['waits']:
                        if ev[0] == 'e':
                            e.wait_ge(esem[ev[1]], rank[ev[1]][ev[2]])
                        else:
                            e.wait_ge(ev[1], ev[2])
                    if o['fn'] is None:
                        continue
                    ins = o['fn'](e)
                    if o['dma'] is not None:
                        ins.then_inc(o['dma'], o['inc'])
                    elif o['ms']:
                        ins.then_inc(esem[eng], 1)
            return body

        with self.nc.Block() as block:
            block.tensor(make('tensor'))
            block.vector(make('vector'))
            block.scalar(make('scalar'))
            block.gpsimd(make('gpsimd'))
            block.sync(make('sync'))

    def stats(self):
        return {e: (len(self.ops[e]), sum(1 for o in self.ops[e] if o['ms'])) for e in ENGS}


P1_TG = 512
P1_NT = 4
P1_D = 2048
P1_EPS = 1e-6
P1_ISQ = float(128 ** -0.5)


class P1_Ctx:
    pass


P1_STAGE = [99]


class P1_StopBuild(Exception):
    pass


def P1_chk(k):
    if P1_STAGE[0] == k:
        raise P1_StopBuild()


def p1_dram(nc, L, fused=False):
    dr = {}

    def inp(name, shape):
        dr[name] = nc.dram_tensor(name, list(shape), F32, kind="ExternalInput").ap()
    inp('x', [L, P1_D]); inp('w_loc', [2048, 2304]); inp('nw0T', [128, 16]); inp('cw', [128, 24])
    inp('nexpA_in', [2, 1]); inp('dtb', [2, 1]); inp('onw', [128, 1])
    inp('ident', [128, 128]); inp('triU', [128, 128]); inp('SL', [128, 128]); inp('negstrict', [128, 128])
    inp('permT', [128, 128]); inp('DT', [128, 256]); inp('qdrow', [128, 1024]); inp('kdcol', [128, 4])
    inp('sel', [2, 256]); inp('cosT', [128, L]); inp('sinT', [128, L])
    if fused:
        dr['oloc'] = nc.dram_tensor("oloc", [L // 512, 512, 512], F32, kind="Internal").ap()
        dr['ogath'] = nc.dram_tensor("ogath", [L // 512, 2048, 512], F32, kind="Internal").ap()
    else:
        dr['oT'] = nc.dram_tensor("oT", [512, L], F32, kind="ExternalOutput").ap()
    return dr


def p1_alloc(S, nc):
    C = P1_Ctx()
    C.S = S
    sb = S.sb
    C.consts = Buf("consts", multi=True)
    for nm, shp in [('ident', [128, 128]), ('triU', [128, 128]), ('SL', [128, 128]), ('negstrict', [128, 128]),
                    ('permT', [128, 128]), ('DT', [128, 2, 128]), ('qdrow', [128, 2, 512]), ('kdcol', [128, 4]),
                    ('sel', [2, 2, 128]), ('nw0T', [128, 16]), ('cw', [128, 6, 4]), ('nexpA', [2, 1]), ('dtb', [2, 1]),
                    ('onw', [128, 1])]:
        t, _ = sb(nm, shp, F32)
        setattr(C, nm, t)
    C.ones_f, C.ones_f_b = sb("ones_f", [128, 128], F32)
    C.xt = [sb(f"xt{i}", [128, P1_D], F32) for i in range(2)]
    C.XN, C.XN_b = sb("XN", [128, 16, P1_TG], BF16)
    C.wb = [sb(f"wb{i}", [128, 16, 256], BF16) for i in range(4)]
    C.PRc, C.PRc_b = sb("PRc", [128, 6, P1_TG + 3], F32)
    C.PRo, C.PRo_b = sb("PRo", [128, 10, P1_TG], F32)
    C.CV = [sb(f"CV{i}", [128, P1_TG], F32) for i in range(6)]
    C.SZ = [sb(f"SZ{i}", [128, P1_TG], F32) for i in range(4)]
    C.cs = [sb(f"cs{i}", [128, P1_TG], F32) for i in range(2)]
    C.RQ = [sb(f"RQ{i}", [128, P1_TG], F32) for i in range(4)]
    C.QD = [sb(f"QD{i}", [128, P1_TG], F32) for i in range(2)]
    C.BBC = [sb(f"BBC{i}", [128, P1_TG], F32) for i in range(2)]
    C.t512 = [sb(f"t512_{i}", [128, P1_TG], F32) for i in range(3)]
    C.bg, C.bg_b = sb("bg", [2, 3, P1_TG], F32)
    C.bcol, C.bcol_b = sb("bcol", [128, P1_NT, 2], F32)
    C.gcol, C.gcol_b = sb("gcol", [128, P1_NT, 2], F32)
    C.sm = [sb(f"sm{i}", [128, 16], F32) for i in range(P1_NT)]
    C.xs, C.xs_b = sb("xs", [128, 8], F32)
    names = ['GT', 'E', 'DTi', 'NDs', 'egc', 'Kbg', 'Ktail', 'Vb', 'KbT', 'Ma', 'Mb', 'MTa', 'MTb', 'P', 'QKT', 'QdT',
             'negWT', 'On']
    C.slot = {}
    for nm in names:
        C.slot[nm] = [sb(f"{nm}{j}", [128, 128], F32) for j in range(P1_NT)]
    C.Vn = [sb(f"Vn{i}", [128, 128], F32) for i in range(2)]
    C.junk = [sb(f"junk{i}", [128, 128], F32) for i in range(2)]
    C.St = [sb(f"S{i}", [128, 128], F32) for i in range(4)]
    C.OUT, C.OUT_b = sb("OUT", [128, 4, P1_TG], F32)
    C.ps = []
    C.psq = []
    for i in range(8):
        t, bb = S.ps(f"ps{i}", [128, 512], F32)
        C.ps.append(t)
        C.psq.append([bb])
    C.oT_b = None
    C.bank_i = 0
    C.q_i = 0
    C.w_i = 0
    C.i2 = 0
    C.ev_i = 0
    return C


def P1_next_bank(C):
    b = C.bank_i % 4
    C.bank_i += 1
    return C.ps[b], C.psq[b]


def P1_next_q(C):
    i = C.q_i % 16
    C.q_i += 1
    b = 4 + i % 4
    q = i // 4
    return C.ps[b][:, q * 128:(q + 1) * 128], C.psq[b][0]


def P1_next_w(C):
    w = C.wb[C.w_i % 4]
    C.w_i += 1
    return w


def P1_rot2(C, lst):
    C.i2 += 1
    return lst[C.i2 % 2]


def P1_ev_eng(C):
    C.ev_i += 1
    return 'scalar' if C.ev_i % 2 else 'vector'


def p1_setup(C, dr):
    S = C.S
    cd = lambda t, src: S.dma('sync', t, src, writes=[C.consts])
    cd(C.ident[:], dr['ident']); cd(C.triU[:], dr['triU']); cd(C.SL[:], dr['SL']); cd(C.negstrict[:], dr['negstrict'])
    cd(C.permT[:], dr['permT']); cd(C.DT[:], dr['DT'].rearrange("p (h t) -> p h t", h=2))
    cd(C.qdrow[:], dr['qdrow'].rearrange("p (h t) -> p h t", h=2)); cd(C.kdcol[:], dr['kdcol'])
    cd(C.sel[:], dr['sel'].rearrange("p (h t) -> p h t", h=2)); cd(C.nw0T[:], dr['nw0T'])
    cd(C.cw[:], dr['cw'].rearrange("p (i k) -> p i k", i=6)); cd(C.nexpA[:], dr['nexpA_in']); cd(C.dtb[:], dr['dtb'])
    cd(C.onw[:], dr['onw'])
    S.op('vector', lambda e: e.memset(C.ones_f[:], 1.0), writes=[C.ones_f_b])
    S.act(C.nexpA[:], C.nexpA[:], AF.Exp, [C.consts], [C.consts])
    S.ts('vector', C.nexpA[:], C.nexpA[:], -1.0, None, ALU.mult, None, [C.consts], [C.consts])
    for i in range(4):
        S.op('vector', lambda e, i=i: e.memset(C.St[i][0][:], 0.0), writes=[C.St[i][1]])
    S.op('vector', lambda e: e.memset(C.PRc[:, :, 0:3], 0.0), writes=[C.PRc_b])


def P1_head_epilogue_a(C, psO, psO_b, j):
    S = C.S
    jk, jk_b = P1_rot2(C, C.junk)
    sm, sm_b = C.sm[j]
    on, on_b = C.slot['On'][j]
    S.act(jk[:], psO, AF.Square, [psO_b], [jk_b, sm_b], accum_out=sm[:, 8:9])
    S.ts('vector', sm[:, 9:10], sm[:, 8:9], 1.0 / 128, P1_EPS, ALU.mult, ALU.add, [sm_b], [sm_b])
    S.act(sm[:, 9:10], sm[:, 9:10], AF.Sqrt, [sm_b], [sm_b])
    S.op('vector', lambda e: e.reciprocal(sm[:, 9:10], sm[:, 9:10]), [sm_b], [sm_b])
    S.ts('vector', on[:], psO, sm[:, 9:10], None, ALU.mult, None, [psO_b, sm_b], [on_b])


def P1_head_epilogue_b(C, gate, gate_b, oidx, j, use_onw):
    S = C.S
    js = slice(j * 128, (j + 1) * 128)
    on, on_b = C.slot['On'][j]
    pq, pq_b = P1_next_q(C)
    S.tr(pq, on[:], C.ident[:], [on_b, C.consts], [pq_b])
    if use_onw:
        S.stt(C.OUT[:, oidx, js], pq, C.onw[:, 0:1], gate[:, js], ALU.mult, ALU.mult, [pq_b, C.consts, gate_b], [C.OUT_b])
    else:
        S.tt('vector', C.OUT[:, oidx, js], pq, gate[:, js], ALU.mult, [pq_b, gate_b], [C.OUT_b])


def P1_front_tile(C, dr, gi, j):
    S = C.S
    t0 = gi * P1_TG
    xt, xt_b = C.xt[j % 2]
    S.dma('sync', xt[:], dr['x'][t0 + j * 128:t0 + (j + 1) * 128, :], writes=[xt_b])
    jt, junk_b = P1_next_w(C)
    junk = jt[:].rearrange("p a b -> p (a b)")[:, 0:P1_D]
    S.act(junk, xt[:], AF.Square, [xt_b], [junk_b, C.xs_b], accum_out=C.xs[:, j:j + 1])
    S.ts('vector', C.xs[:, 4 + j:5 + j], C.xs[:, j:j + 1], 1.0 / P1_D, P1_EPS, ALU.mult, ALU.add, [C.xs_b], [C.xs_b])
    S.act(C.xs[:, 4 + j:5 + j], C.xs[:, 4 + j:5 + j], AF.Sqrt, [C.xs_b], [C.xs_b])
    S.op('vector', lambda e, j=j: e.reciprocal(C.xs[:, 4 + j:5 + j], C.xs[:, 4 + j:5 + j]), [C.xs_b], [C.xs_b])
    S.ts('vector', xt[:], xt[:], C.xs[:, 4 + j:5 + j], None, ALU.mult, None, [xt_b, C.xs_b], [xt_b])
    for c4 in range(4):
        pt, pbs = P1_next_bank(C)
        for q in range(4):
            c = c4 * 4 + q
            S.tr(pt[:, q * 128:(q + 1) * 128], xt[:, c * 128:(c + 1) * 128], C.ident[:], [xt_b, C.consts], pbs)
        S.tt('vector', C.XN[:, c4 * 4:(c4 + 1) * 4, j * 128:(j + 1) * 128],
             pt[:].rearrange("p (q n) -> p q n", q=4),
             C.nw0T[:, c4 * 4:(c4 + 1) * 4].unsqueeze(2).to_broadcast([128, 4, 128]), ALU.mult,
             pbs + [C.consts], [C.XN_b])


def p1_group(C, dr, gi, L, hook=None):
    S = C.S
    t0 = gi * P1_TG
    sl = C.slot
    if gi == 0:
        for j in range(P1_NT):
            P1_front_tile(C, dr, 0, j)
    P1_chk(1)
    S.dma('sync', C.cs[0][0][:], dr['cosT'][:, t0:t0 + P1_TG], writes=[C.cs[0][1]])
    S.dma('sync', C.cs[1][0][:], dr['sinT'][:, t0:t0 + P1_TG], writes=[C.cs[1][1]])
    def main_block(cb):
        banks = [P1_next_bank(C), P1_next_bank(C)]
        wt, wbuf = P1_next_w(C)
        S.dma('gpsimd', wt[:], dr['w_loc'][:, cb * 256:(cb + 1) * 256].rearrange("(kc p) n -> p kc n", p=128),
              writes=[wbuf])
        for cc in range(2):
            pt, pbs = banks[cc]
            for kc in range(16):
                S.mm(pt[:, :P1_TG], wt[:, kc, cc * 128:(cc + 1) * 128], C.XN[:, kc, :], kc == 0, kc == 15,
                     [wbuf, C.XN_b], pbs)
        for cc in range(2):
            pt, pbs = banks[cc]
            ci = cb * 2 + cc
            if ci < 6:
                S.copy(P1_ev_eng(C), C.PRc[:, ci, 3:3 + P1_TG], pt[:, :P1_TG], pbs, [C.PRc_b])
            else:
                S.copy('scalar', C.PRo[:, ci - 6, :], pt[:, :P1_TG], pbs, [C.PRo_b])

    sbanks = [P1_next_bank(C), P1_next_bank(C)]
    wt, wbuf = P1_next_w(C)
    S.dma('gpsimd', wt[:], dr['w_loc'][:, 2048:2304].rearrange("(kc p) n -> p kc n", p=128), writes=[wbuf])
    for cc in range(2):
        pt, pbs = sbanks[cc]
        for kc in range(16):
            S.mm(pt[0:2, :P1_TG], wt[:, kc, cc * 2:cc * 2 + 2], C.XN[:, kc, :], kc == 0, kc == 15, [wbuf, C.XN_b], pbs)
    S.act(C.bg[:, 0, :], sbanks[0][0][0:2, :P1_TG], AF.Sigmoid, sbanks[0][1], [C.bg_b])
    S.act(C.bg[:, 2, :], sbanks[1][0][0:2, :P1_TG], AF.Exp, sbanks[1][1] + [C.consts], [C.bg_b], bias=C.dtb[:, 0:1])
    S.act(C.bg[:, 2, :], C.bg[:, 2, :], AF.Ln, [C.bg_b], [C.bg_b], bias=1.0)
    S.ts('vector', C.bg[:, 1, :], C.bg[:, 2, :], C.nexpA[:, 0:1], None, ALU.mult, None, [C.bg_b, C.consts], [C.bg_b])
    for cb in range(3):
        main_block(cb)
    P1_chk(3)
    for i in range(6):
        cv, cv_b = C.CV[i]
        S.ts('vector', cv[:], C.PRc[:, i, 3:3 + P1_TG], C.cw[:, i, 3:4], None, ALU.mult, None, [C.PRc_b, C.consts], [cv_b])
        for k in (2, 1, 0):
            S.stt(cv[:], C.PRc[:, i, k:k + P1_TG], C.cw[:, i, k:k + 1], cv[:], ALU.mult, ALU.add,
                  [C.PRc_b, C.consts, cv_b], [cv_b])
        S.act(cv[:], cv[:], AF.Silu, [cv_b], [cv_b])
    S.copy('vector', C.PRc[:, :, 0:3], C.PRc[:, :, P1_TG:P1_TG + 3], [C.PRc_b], [C.PRc_b])
    for cb in range(3, 8):
        main_block(cb)
    if hook is not None:
        hook()
    for j in range(P1_NT):
        pq, pq_b = P1_next_q(C)
        S.tr(pq[:, 0:2], C.bg[:, 0, j * 128:(j + 1) * 128], C.ident[0:2, 0:2], [C.bg_b, C.consts], [pq_b])
        S.tr(pq[:, 2:4], C.bg[:, 1, j * 128:(j + 1) * 128], C.ident[0:2, 0:2], [C.bg_b, C.consts], [pq_b])
        S.copy('vector', C.bcol[:, j, :], pq[:, 0:2], [pq_b], [C.bcol_b])
        S.copy('vector', C.gcol[:, j, :], pq[:, 2:4], [pq_b], [C.gcol_b])
    for h in range(2):
        pt, pbs = P1_next_bank(C)
        S.mm(pt[:, :P1_TG], C.sel[:, h, :], C.bg[:, 0, :], True, True, [C.consts, C.bg_b], pbs)
        S.copy('vector', C.BBC[h][0][:], pt[:, :P1_TG], pbs, [C.BBC[h][1]])
    for h in range(2):
        S.act(C.SZ[h][0][:], C.PRo[:, h, :], AF.Silu, [C.PRo_b], [C.SZ[h][1]])
        S.act(C.SZ[2 + h][0][:], C.PRo[:, 8 + h, :], AF.Silu, [C.PRo_b], [C.SZ[2 + h][1]])
    sqb = [C.RQ[0], C.RQ[1], C.RQ[2], C.RQ[3]]
    rnb = [C.QD[0], C.QD[1], C.t512[1], C.t512[2]]
    lbanks = []
    for i in range(4):
        cv, cv_b = C.CV[i]
        S.act(sqb[i][0][:], cv[:], AF.Square, [cv_b], [sqb[i][1]])
    for i in range(4):
        pt, pbs = P1_next_bank(C)
        lbanks.append((pt, pbs))
        S.mm(pt[:, :P1_TG], C.ones_f[:], sqb[i][0][:], True, True, [C.ones_f_b, sqb[i][1]], pbs)
    for i in range(4):
        pt, pbs = lbanks[i]
        S.ts('vector', rnb[i][0][:], pt[:, :P1_TG], 1e-6, None, ALU.add, None, pbs, [rnb[i][1]])
    for i in range(4):
        S.act(rnb[i][0][:], rnb[i][0][:], AF.Sqrt, [rnb[i][1]], [rnb[i][1]])
    for i in range(4):
        cv, cv_b = C.CV[i]
        rn, rn_b = rnb[i]
        S.op('vector', lambda e, rn=rn: e.reciprocal(rn[:], rn[:]), [rn_b], [rn_b])
        if i < 2:
            S.stt(cv[:], cv[:], P1_ISQ, rn[:], ALU.mult, ALU.mult, [cv_b, rn_b], [cv_b])
        else:
            S.tt('vector', cv[:], cv[:], rn[:], ALU.mult, [cv_b, rn_b], [cv_b])
    def ret_prep(h):
        for which in range(2):
            src = C.PRo[:, 2 + 2 * which + h, :]
            rq, rq_b = C.RQ[2 * which + h]
            xs, xs_b = C.t512[0]
            if which == 1:
                S.op('scalar', lambda e, xs=xs, src=src: e.mul(xs[:], src, P1_ISQ), [C.PRo_b], [xs_b])
            else:
                S.copy('scalar', xs[:], src, [C.PRo_b], [xs_b])
            pt, pbs = P1_next_bank(C)
            S.mm(pt[:, :P1_TG], C.permT[:], xs[:], True, True, [C.consts, xs_b], pbs)
            S.tt('vector', rq[:], xs[:], C.cs[0][0][:], ALU.mult, [xs_b, C.cs[0][1]], [rq_b])
            t2, t2_b = C.t512[2]
            S.tt('vector', t2[:], pt[:, :P1_TG], C.cs[1][0][:], ALU.mult, pbs + [C.cs[1][1]], [t2_b])
            S.tt('vector', rq[:], rq[:], t2[:], ALU.add, [rq_b, t2_b], [rq_b])
        qr, qr_b = C.RQ[h]
        qd, qd_b = C.QD[h]
        S.tt('vector', qd[:], qr[:], C.qdrow[:, h, :], ALU.mult, [qr_b, C.consts], [qd_b])

    def ret_chunk(h, j):
        St, St_b = C.St[2 + h]
        qr, qr_b = C.RQ[h]
        kr, kr_b = C.RQ[2 + h]
        vT = C.PRo[:, 6 + h, :]
        qd, qd_b = C.QD[h]
        js = slice(j * 128, (j + 1) * 128)
        pK, pK_b = P1_next_q(C)
        S.tr(pK, kr[:, js], C.ident[:], [kr_b, C.consts], [pK_b])
        Kd, Kd_b = sl['GT'][j]
        S.ts('vector', Kd[:], pK, C.kdcol[:, h:h + 1], None, ALU.mult, None, [pK_b, C.consts], [Kd_b])
        pV, pV_b = P1_next_q(C)
        S.tr(pV, vT[:, js], C.ident[:], [C.PRo_b, C.consts], [pV_b])
        V, V_b = sl['E'][j]
        S.copy('scalar', V[:], pV, [pV_b], [V_b])
        pQ, pQ_b = P1_next_q(C)
        S.mm(pQ, kr[:, js], qr[:, js], True, True, [kr_b, qr_b], [pQ_b])
        QKT, QKT_b = sl['NDs'][j]
        S.tt('vector', QKT[:], pQ, C.DT[:, h, :], ALU.mult, [pQ_b, C.consts], [QKT_b])
        pO, pO_b = P1_next_q(C)
        S.mm(pO, QKT[:], V[:], True, False, [QKT_b, V_b], [pO_b])
        S.mm(pO, qd[:, js], St[:], False, True, [qd_b, St_b], [pO_b])
        pS, pS_b = P1_next_q(C)
        S.mm(pS, Kd[:], V[:], True, True, [Kd_b, V_b], [pS_b])
        S.stt(St[:], St[:], C.kdcol[:, 2 + h:3 + h], pS, ALU.mult, ALU.add, [St_b, C.consts, pS_b], [St_b])
        P1_head_epilogue_a(C, pO, pO_b, j)
        if j > 0:
            P1_head_epilogue_b(C, C.SZ[2 + h][0], C.SZ[2 + h][1], 2 + h, j - 1, False)

    def ret_finish(h):
        P1_head_epilogue_b(C, C.SZ[2 + h][0], C.SZ[2 + h][1], 2 + h, P1_NT - 1, False)

    ret_prep(0)
    ret_prep(1)
    P1_chk(4)
    for h in range(2):
        qT, qT_b = C.CV[h]
        kT, kT_b = C.CV[2 + h]
        vT, vT_b = C.CV[4 + h]
        St, St_b = C.St[h]
        JS = [slice(j * 128, (j + 1) * 128) for j in range(P1_NT)]
        for j in range(P1_NT):
            GT, GT_b = sl['GT'][j]
            S.ts('vector', GT[:], C.triU[:], C.gcol[:, j, h:h + 1], None, ALU.mult, None, [C.consts, C.gcol_b], [GT_b])
            KbT, KbT_b = sl['KbT'][j]
            S.tt('vector', KbT[:], kT[:, JS[j]], C.BBC[h][0][:, JS[j]], ALU.mult, [kT_b, C.BBC[h][1]], [KbT_b])
        pAs, pBs, pCs = [], [], []
        for j in range(P1_NT):
            GT, GT_b = sl['GT'][j]
            pA, pA_b = P1_next_q(C)
            S.mm(pA, C.SL[:], GT[:], True, True, [C.consts, GT_b], [pA_b])
            pB, pB_b = P1_next_q(C)
            S.mm(pB, C.ones_f[:], GT[:], True, True, [C.ones_f_b, GT_b], [pB_b])
            pC, pC_b = P1_next_q(C)
            S.mm(pC[:, 0:1], GT[:], C.ones_f[:, 0:1], True, True, [C.ones_f_b, GT_b], [pC_b])
            pAs.append((pA, pA_b)); pBs.append((pB, pB_b)); pCs.append((pC, pC_b))
        for j in range(P1_NT):
            sm, sm_b = C.sm[j]
            pA, pA_b = pAs[j]; pB, pB_b = pBs[j]; pC, pC_b = pCs[j]
            E, E_b = sl['E'][j]
            S.act(E[:], pA, AF.Exp, [pA_b], [E_b])
            egc, egc_b = sl['egc'][j]
            S.act(egc[:], pB, AF.Exp, [pB_b], [egc_b])
            S.copy('vector', sm[:, 0:1], pB[:, 127:128], [pB_b], [sm_b])
            S.act(sm[:, 1:2], pC[:, 0:1], AF.Exp, [pC_b], [sm_b])
            S.act(sm[:, 2:3], pC[:, 0:1], AF.Exp, [pC_b, sm_b], [sm_b], scale=-1.0, bias=sm[:, 0:1])
            S.tt('vector', sm[:, 3:4], sm[:, 1:2], C.bcol[:, j, h:h + 1], ALU.mult, [sm_b, C.bcol_b], [sm_b])
            S.tt('vector', sl['DTi'][j][0][:], E[:], C.triU[:], ALU.mult, [E_b, C.consts], [sl['DTi'][j][1]])
            S.tt('vector', sl['NDs'][j][0][:], E[:], C.negstrict[:], ALU.mult, [E_b, C.consts], [sl['NDs'][j][1]])
            S.tt('vector', sl['QdT'][j][0][:], qT[:, JS[j]], egc[:], ALU.mult, [qT_b, egc_b], [sl['QdT'][j][1]])
        pKs, pVs, pPs, pQs = [], [], [], []
        for j in range(P1_NT):
            pK, pK_b = P1_next_q(C)
            S.tr(pK, kT[:, JS[j]], C.ident[:], [kT_b, C.consts], [pK_b])
            pV, pV_b = P1_next_q(C)
            S.tr(pV, vT[:, JS[j]], C.ident[:], [vT_b, C.consts], [pV_b])
            pP, pP_b = P1_next_q(C)
            S.mm(pP, kT[:, JS[j]], sl['KbT'][j][0][:], True, True, [kT_b, sl['KbT'][j][1]], [pP_b])
            pQ, pQ_b = P1_next_q(C)
            S.mm(pQ, kT[:, JS[j]], qT[:, JS[j]], True, True, [kT_b, qT_b], [pQ_b])
            pKs.append((pK, pK_b)); pVs.append((pV, pV_b)); pPs.append((pP, pP_b)); pQs.append((pQ, pQ_b))
        for j in range(P1_NT):
            sm, sm_b = C.sm[j]
            pK, pK_b = pKs[j]; pV, pV_b = pVs[j]; pP, pP_b = pPs[j]; pQ, pQ_b = pQs[j]
            Ma, Ma_b = sl['Ma'][j]
            S.tt('vector', Ma[:], pP, sl['NDs'][j][0][:], ALU.mult, [pP_b, sl['NDs'][j][1]], [Ma_b])
            S.tt('vector', sl['P'][j][0][:], Ma[:], C.ident[:], ALU.add, [Ma_b, C.consts], [sl['P'][j][1]])
            S.ts('vector', sl['Kbg'][j][0][:], pK, sm[:, 3:4], None, ALU.mult, None, [pK_b, sm_b], [sl['Kbg'][j][1]])
            S.ts('vector', sl['Ktail'][j][0][:], pK, sm[:, 2:3], None, ALU.mult, None, [pK_b, sm_b], [sl['Ktail'][j][1]])
            S.ts('vector', sl['Vb'][j][0][:], pV, C.bcol[:, j, h:h + 1], None, ALU.mult, None, [pV_b, C.bcol_b], [sl['Vb'][j][1]])
            S.tt('vector', sl['QKT'][j][0][:], pQ, sl['DTi'][j][0][:], ALU.mult, [pQ_b, sl['DTi'][j][1]], [sl['QKT'][j][1]])
        for j in range(P1_NT):
            Ma, Ma_b = sl['Ma'][j]
            pT, pT_b = P1_next_q(C)
            S.tr(pT, Ma[:], C.ident[:], [Ma_b, C.consts], [pT_b])
            S.copy('scalar', sl['MTa'][j][0][:], pT, [pT_b], [sl['MTa'][j][1]])
        if (gi + 1) * P1_TG < L:
            P1_front_tile(C, dr, gi + 1, 2 * h)
        P1_chk(5)
        cur = ('Ma', 'MTa')
        nxt = ('Mb', 'MTb')
        for lvl in range(1, 7):
            for j in range(P1_NT):
                M, M_b = sl[cur[0]][j]
                MT, MT_b = sl[cur[1]][j]
                M2, M2_b = sl[nxt[0]][j]
                MT2, MT2_b = sl[nxt[1]][j]
                if lvl < 6:
                    p1, p1_b = P1_next_q(C)
                    S.mm(p1, MT[:], M[:], True, True, [MT_b, M_b], [p1_b])
                    S.copy('scalar', M2[:], p1, [p1_b], [M2_b])
                p2, p2_b = P1_next_q(C)
                S.mm(p2, M[:], MT[:], True, True, [MT_b, M_b], [p2_b])
                S.copy('vector' if lvl < 6 else 'scalar', MT2[:], p2, [p2_b], [MT2_b])
            for j in range(P1_NT):
                MT2, MT2_b = sl[nxt[1]][j]
                P, P_b = sl['P'][j]
                p3, p3_b = P1_next_q(C)
                S.mm(p3, MT2[:], P[:], True, True, [MT2_b, P_b], [p3_b])
                S.tt('vector', P[:], P[:], p3, ALU.add, [P_b, p3_b], [P_b])
            cur, nxt = nxt, cur
            if lvl <= P1_NT:
                ret_chunk(h, lvl - 1)
            elif lvl == P1_NT + 1:
                ret_finish(h)
        if (gi + 1) * P1_TG < L:
            P1_front_tile(C, dr, gi + 1, 2 * h + 1)
        P1_chk(6)
        for j in range(P1_NT):
            pW, pW_b = P1_next_q(C)
            S.mm(pW, sl['Kbg'][j][0][:], sl['P'][j][0][:], True, True, [sl['Kbg'][j][1], sl['P'][j][1]], [pW_b])
            S.op('scalar', lambda e, j=j, pW=pW: e.mul(sl['negWT'][j][0][:], pW, -1.0), [pW_b], [sl['negWT'][j][1]])
        P1_chk(7)
        for j in range(P1_NT):
            sm, sm_b = C.sm[j]
            TT, TT_b = sl['P'][j]
            pVn, pVn_b = P1_next_q(C)
            S.mm(pVn, TT[:], sl['Vb'][j][0][:], True, False, [TT_b, sl['Vb'][j][1]], [pVn_b])
            S.mm(pVn, sl['negWT'][j][0][:], St[:], False, True, [sl['negWT'][j][1], St_b], [pVn_b])
            Vn, Vn_b = P1_rot2(C, C.Vn)
            S.copy('scalar', Vn[:], pVn, [pVn_b], [Vn_b])
            pO, pO_b = P1_next_q(C)
            S.mm(pO, sl['QdT'][j][0][:], St[:], True, False, [sl['QdT'][j][1], St_b], [pO_b])
            S.mm(pO, sl['QKT'][j][0][:], Vn[:], False, True, [sl['QKT'][j][1], Vn_b], [pO_b])
            pS, pS_b = P1_next_q(C)
            S.mm(pS, sl['Ktail'][j][0][:], Vn[:], True, True, [sl['Ktail'][j][1], Vn_b], [pS_b])
            S.stt(St[:], St[:], sl['egc'][j][0][:, 127:128], pS, ALU.mult, ALU.add, [St_b, sl['egc'][j][1], pS_b], [St_b])
            P1_head_epilogue_a(C, pO, pO_b, j)
            if j > 0:
                P1_head_epilogue_b(C, C.SZ[h][0], C.SZ[h][1], h, j - 1, True)
        P1_head_epilogue_b(C, C.SZ[h][0], C.SZ[h][1], h, P1_NT - 1, True)
    P1_chk(8)
    for o in range(4):
        if C.oT_b is not None:
            S.dma('sync', dr['oloc'][gi][o * 128:(o + 1) * 128, :], C.OUT[:, o, :], reads=[C.OUT_b],
                  writes=[C.oT_b], sembuf=C.OUT_b)
        else:
            S.dma('sync', dr['oT'][o * 128:(o + 1) * 128, t0:t0 + P1_TG], C.OUT[:, o, :], reads=[C.OUT_b])


def build_p1(L, ngroups=None):
    nc = bass.Bass("TRN2", target_bir_lowering=False)
    dr = p1_dram(nc, L)
    with contextlib.ExitStack() as st:
        S = Sched(nc, st)
        C = p1_alloc(S, nc)
        p1_setup(C, dr)
        try:
            for gi in range(ngroups if ngroups else L // P1_TG):
                p1_group(C, dr, gi, L)
        except P1_StopBuild:
            pass
        S.finish()
        print("p1 ops", S.stats(), "sems", S.nsem, "sbuf left", nc.sbuf_bytes_remaining)
        S.emit()
    return nc


def p1_host_consts(L):
    f = np.float32
    d = {}
    d['ident'] = np.eye(128, dtype=f)
    d['triU'] = np.triu(np.ones((128, 128), f))
    d['SL'] = np.tril(np.ones((128, 128), f), -1)
    d['negstrict'] = -np.triu(np.ones((128, 128), f), 1)
    pm = np.zeros((128, 128), f)
    for m in range(64):
        pm[m + 64, m] = -1.0
        pm[m, m + 64] = 1.0
    d['permT'] = pm
    half = 64
    inv_freq = (np.float32(1.0) / np.power(np.float32(10000.0), np.linspace(0.0, 1.0, half, dtype=np.float32))).astype(f)
    pos = np.arange(L, dtype=f)
    ang = (pos[:, None] * inv_freq[None, :]).astype(f).astype(np.float64)
    cos = np.cos(ang).astype(f).T
    sin = np.sin(ang).astype(f).T
    d['cosT'] = np.ascontiguousarray(np.concatenate([cos, cos], 0))
    d['sinT'] = np.ascontiguousarray(np.concatenate([sin, sin], 0))
    return d


def p1_host_core(inp, b, hg, L, consts):
    f = np.float32
    d = dict(consts)
    d['x'] = np.ascontiguousarray(inp['x'][b, :L])
    w_in = inp['la_w_in'][0]
    offs = {'gq': 0, 'gk': 1024, 'gv': 2048, 'gz': 3072, 'gb': 4096, 'ga': 4104, 'rq': 4112, 'rk': 5136, 'rv': 6160,
            'rg': 7184}
    cols = []
    for nm in ['gq', 'gk', 'gv', 'gz', 'rq', 'rk', 'rv', 'rg']:
        for h in range(2):
            hh = 2 * hg + h
            cols.append(w_in[:, offs[nm] + hh * 128: offs[nm] + (hh + 1) * 128])
    sc = np.zeros((2048, 256), f)
    for h in range(2):
        sc[:, h] = w_in[:, offs['gb'] + 2 * hg + h]
        sc[:, 2 + h] = w_in[:, offs['ga'] + 2 * hg + h]
    cols.append(sc)
    d['w_loc'] = np.ascontiguousarray(np.concatenate(cols, 1))
    d['nw0T'] = np.ascontiguousarray(inp['norm_w'][0, 0].reshape(16, 128).T)
    cw = inp['la_conv_w'][0]
    cwl = np.zeros((128, 6, 4), f)
    for t, base in enumerate([0, 1024, 2048]):
        for h in range(2):
            hh = 2 * hg + h
            cwl[:, t * 2 + h, :] = cw[base + hh * 128: base + (hh + 1) * 128, :]
    d['cw'] = np.ascontiguousarray(cwl.reshape(128, 24))
    d['nexpA_in'] = np.ascontiguousarray(inp['la_a_log'][0, 2 * hg:2 * hg + 2].reshape(2, 1))
    d['dtb'] = np.ascontiguousarray(inp['la_dt_bias'][0, 2 * hg:2 * hg + 2].reshape(2, 1))
    d['onw'] = np.ascontiguousarray(inp['la_out_norm_w'][0].reshape(128, 1))
    DT = np.zeros((128, 2, 128), np.float64)
    qd = np.zeros((128, 2, 512), np.float64)
    kd = np.zeros((128, 4), np.float64)
    s = np.arange(128)
    for h in range(2):
        lg = np.log1p(-np.float64(2.0) ** (-5.0 - (2 * hg + h)))
        diff = s[None, :] - s[:, None]
        DT[:, h, :] = np.where(diff >= 0, np.exp(diff * lg), 0.0)
        qd[:, h, :] = np.tile(np.exp((s + 1.0) * lg), 4)[None, :]
        kd[:, h] = np.exp((127.0 - s) * lg)
        kd[:, 2 + h] = np.exp(128.0 * lg)
    d['DT'] = np.ascontiguousarray(DT.reshape(128, 256).astype(f))
    d['qdrow'] = np.ascontiguousarray(qd.reshape(128, 1024).astype(f))
    d['kdcol'] = np.ascontiguousarray(kd.astype(f))
    sel = np.zeros((2, 2, 128), f)
    sel[0, 0, :] = 1.0
    sel[1, 1, :] = 1.0
    d['sel'] = np.ascontiguousarray(sel.reshape(2, 256))
    return d


P2_TG = 512
P2_D = 2048
P2_EPS = 1e-6


class P2_Ctx:
    pass


def p2_alloc(S, nc):
    C = P2_Ctx()
    C.S = S
    C.consts = Buf("consts", multi=True)
    C.ident, _ = S.sb("ident", [128, 128], F32)
    C.ones_f, C.ones_f_b = S.sb("ones_f", [128, 128], F32)
    C.ones16, C.ones16_b = S.sb("ones16", [128, 128], BF16)
    C.nwT, _ = S.sb("nwT", [128, 8, 16], F32)
    C.lnwT, _ = S.sb("lnwT", [128, 32], F32)
    C.lnbT, _ = S.sb("lnbT", [128, 32], F32)
    C.bs_bc, _ = S.sb("bs_bc", [128, 8, 128], F32)
    C.wsm_f, C.wsm_f_b = S.sb("wsm_f", [128, 8, 128], F32)
    C.wsm16, C.wsm16_b = S.sb("wsm16", [128, 8, 128], BF16)
    C.maskT, _ = S.sb("maskT", [128, 128], F32)
    C.bias2, C.bias2_b = S.sb("bias2", [128, 32, 128], F32)
    C.H, C.H_b = S.sb("H", [128, 16, P2_TG], F32)
    C.XN, C.XN_b = S.sb("XN", [128, 16, P2_TG], BF16)
    C.A, C.A_b = S.sb("A", [128, 64, P2_TG], BF16)
    C.Y, C.Y_b = S.sb("Y", [128, 16, P2_TG], F32)
    C.wb = [S.sb(f"wb{i}", [128, 16, 256], BF16) for i in range(3)]
    C.tmp = [S.sb(f"tmp{i}", [128, P2_TG], F32) for i in range(2)]
    C.rstd, C.rstd_b = S.sb("rstd", [128, P2_TG], F32)
    C.st, C.st_b = S.sb("st", [128, 32], F32)
    C.ps = [S.ps(f"ps{i}", [128, 512], F32) for i in range(8)]
    C.wscr = None
    C.blk_i = 0
    C.bank_i = 0
    C.w_i = 0
    C.tmp_i = 0
    C.ev_i = 0
    return C


def getq(C, e, name, mkview):
    if not hasattr(C, 'qcache'):
        C.qcache = {}
    if name not in C.qcache:
        C.qcache[name] = mkview(e.snap(e.partition_id() % 4))
    return C.qcache[name]


def P2_next_bank(C):
    b = C.ps[C.bank_i % 8]
    C.bank_i += 1
    return b


def P2_next_w(C):
    w = C.wb[C.w_i % 3]
    C.w_i += 1
    return w


def P2_next_tmp(C):
    t = C.tmp[C.tmp_i % 2]
    C.tmp_i += 1
    return t


def P2_ev_eng(C):
    C.ev_i += 1
    return 'scalar' if C.ev_i % 2 else 'vector'


def wload(C, wt, wbuf, src):
    S = C.S
    if getattr(C, 'wscr', None) is not None:
        i = C.blk_i % len(C.blk_list)
        C.blk_i += 1
        assert str(C.blk_list[i]) == str(src), (i, C.blk_list[i], src)
        S.dma('sync', wt[:].rearrange("p a b -> p (a b)"), C.wscr[i], writes=[wbuf])
    else:
        S.dma('gpsimd', wt[:], src.rearrange("(kc p) n -> p kc n", p=128), writes=[wbuf])


def p2_block_list(dr):
    L_ = []
    def lin(W, K, n0, N):
        for cb in range(N // 256):
            for kb in range(K // 2048):
                L_.append(W[kb * 2048:(kb + 1) * 2048, n0 + cb * 256:n0 + (cb + 1) * 256])
    lin(dr['w_out0'], 2048, 0, 2048)
    lin(dr['w_up'][0], 2048, 0, 8192)
    lin(dr['w_dn'][0], 8192, 0, 2048)
    lin(dr['sg_in'], 2048, 4096, 4096)
    lin(dr['sg_in'], 2048, 0, 4096)
    lin(dr['sg_out'], 4096, 0, 2048)
    lin(dr['w_up'][1], 2048, 0, 8192)
    lin(dr['w_dn'][1], 8192, 0, 2048)
    return L_


def P2_linear_ws(C, xview, xbuf, K, W, n0, N, evac, tg=P2_TG):
    S = C.S
    KB = K // 2048
    for cb in range(N // 256):
        banks = [P2_next_bank(C), P2_next_bank(C)]
        for kb in range(KB):
            wt, wbuf = P2_next_w(C)
            wload(C, wt, wbuf, W[kb * 2048:(kb + 1) * 2048, n0 + cb * 256:n0 + (cb + 1) * 256])
            for cc in range(2):
                pt, pb = banks[cc]
                for kc in range(16):
                    S.mm(pt[:, :tg], wt[:, kc, cc * 128:(cc + 1) * 128], xview(kb * 16 + kc),
                         kb == 0 and kc == 0, kb == KB - 1 and kc == 15, [wbuf, xbuf], [pb])
        for cc in range(2):
            evac(cb * 2 + cc, banks[cc][0], banks[cc][1])


def P2_rms_rstd(C, src, src_b, nch, dim, tg=P2_TG):
    S = C.S
    sq = C.A[:, 0:nch, :]
    S.act(sq[:, :, :tg], src, AF.Square, [src_b], [C.A_b])
    pt, pb = P2_next_bank(C)
    for c in range(nch):
        S.mm(pt[:, :tg], C.ones16[:], sq[:, c, :tg], c == 0, c == nch - 1, [C.ones16_b, C.A_b], [pb])
    S.ts('vector', C.rstd[:, :tg], pt[:, :tg], 1.0 / dim, P2_EPS, ALU.mult, ALU.add, [pb], [C.rstd_b])
    S.act(C.rstd[:, :tg], C.rstd[:, :tg], AF.Sqrt, [C.rstd_b], [C.rstd_b])
    S.op('vector', lambda e: e.reciprocal(C.rstd[:, :tg], C.rstd[:, :tg]), [C.rstd_b], [C.rstd_b])


def P2_residual_norm(C, j):
    S = C.S
    P2_rms_rstd(C, C.Y[:], C.Y_b, 16, P2_D)
    for c in range(16):
        S.stt(C.Y[:, c, :], C.Y[:, c, :], C.nwT[:, j, c:c + 1], C.rstd[:], ALU.mult, ALU.mult,
              [C.Y_b, C.consts, C.rstd_b], [C.Y_b])
        S.tt('vector', C.H[:, c, :], C.H[:, c, :], C.Y[:, c, :], ALU.add, [C.H_b, C.Y_b], [C.H_b])


def P2_prenorm(C, j):
    S = C.S
    P2_rms_rstd(C, C.H[:], C.H_b, 16, P2_D)
    for c in range(16):
        S.stt(C.XN[:, c, :], C.H[:, c, :], C.nwT[:, j, c:c + 1], C.rstd[:], ALU.mult, ALU.mult,
              [C.H_b, C.consts, C.rstd_b], [C.XN_b])


def P2_ffn(C, w_up, w_dn):
    S = C.S

    def ev_up(c, pt, pb):
        t, tb = P2_next_tmp(C)
        S.act(t[:], pt[:, :P2_TG], AF.Relu, [pb], [tb])
        S.tt('vector', C.A[:, c, :], t[:], t[:], ALU.mult, [tb], [C.A_b])

    P2_linear_ws(C, lambda kc: C.XN[:, kc, :], C.XN_b, 2048, w_up, 0, 8192, ev_up)

    def ev_dn(c, pt, pb):
        S.copy(P2_ev_eng(C), C.Y[:, c, :], pt[:, :P2_TG], [pb], [C.Y_b])

    P2_linear_ws(C, lambda kc: C.A[:, kc, :], C.A_b, 8192, w_dn, 0, 2048, ev_dn)


def p2_setup(C, dr):
    S = C.S
    S.dma('sync', C.ident[:], dr['ident'], writes=[C.consts])
    S.dma('sync', C.nwT[:], dr['nwT'].rearrange("p (j c) -> p j c", j=8), writes=[C.consts])
    S.dma('sync', C.lnwT[:], dr['lnwT'], writes=[C.consts])
    S.dma('sync', C.lnbT[:], dr['lnbT'], writes=[C.consts])
    S.dma('sync', C.maskT[:], dr['maskT'], writes=[C.consts])
    S.dma('sync', C.wsm_f[:], dr['wsT'].rearrange("p (g t) -> p g t", g=8), writes=[C.wsm_f_b])
    S.dma('sync', C.bs_bc[:].rearrange("p g t -> p (g t)"), dr['bs'].partition_broadcast(128), writes=[C.consts])
    S.op('vector', lambda e: e.memset(C.ones_f[:], 1.0), writes=[C.ones_f_b])
    S.copy('vector', C.ones16[:], C.ones_f[:], [C.ones_f_b], [C.ones16_b])
    for g in range(8):
        S.tt('vector', C.wsm_f[:, g, :], C.wsm_f[:, g, :], C.maskT[:], ALU.mult, [C.wsm_f_b, C.consts], [C.wsm_f_b])
    S.copy('vector', C.wsm16[:], C.wsm_f[:], [C.wsm_f_b], [C.wsm16_b])
    for g in range(8):
        pt, pb = P2_next_bank(C)
        S.mm(pt[:, :128], C.ones_f[:], C.wsm_f[:, g, :], True, True, [C.ones_f_b, C.wsm_f_b], [pb])
        for q in range(4):
            c = g * 4 + q
            S.stt(C.bias2[:, c, :], pt[:, :128], C.lnbT[:, c:c + 1], C.bs_bc[:, g, :], ALU.mult, ALU.add,
                  [pb, C.consts], [C.bias2_b])


def p2_group(C, dr, gi, do_l0=True, do_l1=True, fused=False, TQ=2048):
    S = C.S
    t0 = gi * P2_TG
    NT = P2_TG // 128
    Yt = C.Y[:].rearrange("p c t -> p (c t)").rearrange("p (j d) -> p j d", j=NT)
    C.Y_b.multi = True
    for j in range(NT):
        if fused:
            def fx(e, j=j):
                xv = getq(C, e, 'sync', lambda qb: dr['x'][bass.ds(qb * TQ, TQ), :])
                return e.dma_start(out=Yt[:, j, :], in_=xv[t0 + j * 128:t0 + (j + 1) * 128, :])
            S._add('sync', fx, [], [C.Y_b], dma_buf=C.Y_b)
        else:
            S.dma('sync', Yt[:, j, :], dr['x'][t0 + j * 128:t0 + (j + 1) * 128, :], writes=[C.Y_b])
    for j in range(NT):
        for c4 in range(4):
            pt, pb = P2_next_bank(C)
            for q in range(4):
                c = c4 * 4 + q
                S.tr(pt[:, q * 128:(q + 1) * 128], Yt[:, j, c * 128:(c + 1) * 128], C.ident[:], [C.Y_b, C.consts], [pb])
            S.copy(P2_ev_eng(C), C.H[:, c4 * 4:(c4 + 1) * 4, j * 128:(j + 1) * 128],
                   pt[:].rearrange("p (q n) -> p q n", q=4), [pb], [C.H_b])

    def ev_y(c, pt, pb):
        S.copy(P2_ev_eng(C), C.Y[:, c, :], pt[:, :P2_TG], [pb], [C.Y_b])

    if do_l0:
        if fused:
            def fo(e):
                gv = getq(C, e, 'gpsimd', lambda qb: dr['ogath'][bass.ds(qb * (TQ // P2_TG), TQ // P2_TG)])
                return e.dma_start(out=C.XN[:], in_=gv[gi:gi + 1].rearrange("o (c p) t -> p (o c) t", p=128))
            S._add('gpsimd', fo, [C.ogath_b], [C.XN_b], dma_buf=C.XN_b)
        else:
            S.dma('gpsimd', C.XN[:], dr['oT'][:, t0:t0 + P2_TG].rearrange("(c p) t -> p c t", p=128), writes=[C.XN_b])
        P2_linear_ws(C, lambda kc: C.XN[:, kc, :], C.XN_b, 2048, dr['w_out0'], 0, 2048, ev_y)
        P2_residual_norm(C, 1)
        P2_prenorm(C, 2)
        P2_ffn(C, dr['w_up'][0], dr['w_dn'][0])
        P2_residual_norm(C, 3)
    if do_l1:
        P2_prenorm(C, 4)
        Vt = C.A[:].rearrange("p c t -> p (c t)").bitcast(F32).rearrange("p (j n) -> p j n", j=NT)
        Vh = C.Y[:].rearrange("p c t -> p (c t)").bitcast(BF16).rearrange("p (j n) -> p j n", j=NT)
        for cb in range(16):
            banks = [P2_next_bank(C) for _ in range(NT)]
            wt, wbuf = P2_next_w(C)
            wload(C, wt, wbuf, dr['sg_in'][0:2048, 4096 + cb * 256:4096 + (cb + 1) * 256])
            for j in range(NT):
                pt, pb = banks[j]
                for kc in range(16):
                    S.mm(pt[:, :256], C.XN[:, kc, j * 128:(j + 1) * 128], wt[:, kc, :], kc == 0, kc == 15,
                         [wbuf, C.XN_b], [pb])
            for j in range(NT):
                pt, pb = banks[j]
                S.act(Vt[:, j, cb * 256:(cb + 1) * 256], pt[:, :256], AF.Gelu_apprx_tanh, [pb], [C.A_b])
        s1 = C.st[:, 0:4]
        s2 = C.st[:, 4:8]
        mu = C.st[:, 8:12]
        rs = C.st[:, 12:16]
        nmr = C.st[:, 16:20]
        jt, junk_b = P2_next_w(C)
        junk = jt[:].rearrange("p a b -> p (a b)")
        for j in range(NT):
            S.act(junk, Vt[:, j, :], AF.Copy, [C.A_b], [junk_b, C.st_b], accum_out=s1[:, j:j + 1])
            S.act(junk, Vt[:, j, :], AF.Square, [C.A_b], [junk_b, C.st_b], accum_out=s2[:, j:j + 1])
        S.ts('vector', mu, s1, 1.0 / 4096, None, ALU.mult, None, [C.st_b], [C.st_b])
        S.ts('vector', s2, s2, 1.0 / 4096, None, ALU.mult, None, [C.st_b], [C.st_b])
        S.tt('vector', rs, mu, mu, ALU.mult, [C.st_b], [C.st_b])
        S.tt('vector', rs, s2, rs, ALU.subtract, [C.st_b], [C.st_b])
        S.ts('vector', rs, rs, P2_EPS, None, ALU.add, None, [C.st_b], [C.st_b])
        S.act(rs, rs, AF.Sqrt, [C.st_b], [C.st_b])
        S.op('vector', lambda e: e.reciprocal(rs, rs), [C.st_b], [C.st_b])
        S.tt('vector', nmr, mu, rs, ALU.mult, [C.st_b], [C.st_b])
        S.ts('vector', nmr, nmr, -1.0, None, ALU.mult, None, [C.st_b], [C.st_b])
        for j in range(NT):
            S.act(Vh[:, j, :], Vt[:, j, :], AF.Identity, [C.A_b, C.st_b], [C.Y_b],
                  scale=rs[:, j:j + 1], bias=nmr[:, j:j + 1])

        def ev_u(c, pt, pb):
            S.act(C.A[:, c, :], pt[:, :P2_TG], AF.Gelu_apprx_tanh, [pb], [C.A_b])

        P2_linear_ws(C, lambda kc: C.XN[:, kc, :], C.XN_b, 2048, dr['sg_in'], 0, 4096, ev_u)
        for c in range(32):
            g = c // 4
            pt, pb = P2_next_bank(C)
            for j in range(NT):
                S.mm(pt[:, j * 128:(j + 1) * 128], Vh[:, j, c * 128:(c + 1) * 128], C.wsm16[:, g, :], True, True,
                     [C.Y_b, C.wsm16_b], [pb])
            t, tb = P2_next_tmp(C)
            S.stt(t[:].rearrange("p (j n) -> p j n", j=NT), pt[:].rearrange("p (j n) -> p j n", j=NT),
                  C.lnwT[:, c:c + 1], C.bias2[:, c:c + 1, :].to_broadcast([128, NT, 128]), ALU.mult, ALU.add,
                  [pb, C.consts, C.bias2_b], [tb])
            S.tt('vector', C.A[:, c, :], t[:], C.A[:, c, :], ALU.mult, [tb, C.A_b], [C.A_b])
        P2_linear_ws(C, lambda kc: C.A[:, kc, :], C.A_b, 4096, dr['sg_out'], 0, 2048, ev_y)
        P2_residual_norm(C, 5)
        P2_prenorm(C, 6)
        P2_ffn(C, dr['w_up'][1], dr['w_dn'][1])
        P2_residual_norm(C, 7)
    for j in range(NT):
        for c4 in range(4):
            pt, pb = P2_next_bank(C)
            for q in range(4):
                c = c4 * 4 + q
                S.tr(pt[:, q * 128:(q + 1) * 128], C.H[:, c, j * 128:(j + 1) * 128], C.ident[:], [C.H_b, C.consts], [pb])
            S.copy(P2_ev_eng(C), Yt[:, j, c4 * 512:(c4 + 1) * 512], pt[:], [pb], [C.Y_b])
        S.dma('sync', dr['out'][t0 + j * 128:t0 + (j + 1) * 128, :], Yt[:, j, :], reads=[C.Y_b])


def p2_dram(nc, T, fused=False):
    dr = {}
    def inp(name, shape):
        dr[name] = nc.dram_tensor(name, list(shape), F32, kind="ExternalInput").ap()
    if not fused:
        inp('x', [T, P2_D]); inp('oT', [P2_D, T]); inp('ident', [128, 128])
    inp('w_out0', [2048, 2048]); inp('w_up', [2, 2048, 8192]); inp('w_dn', [2, 8192, 2048])
    inp('sg_in', [2048, 8192]); inp('sg_out', [4096, 2048])
    inp('nwT', [128, 128]); inp('lnwT', [128, 32]); inp('lnbT', [128, 32]); inp('wsT', [128, 1024])
    inp('maskT', [128, 128]); inp('bs', [1024])
    dr['out'] = nc.dram_tensor("out", [T, P2_D], F32, kind="ExternalOutput").ap()
    return dr


def build_p2(T=2048, ngroups=None, do_l0=True, do_l1=True):
    nc = bass.Bass("TRN2", target_bir_lowering=False)
    dr = p2_dram(nc, T)
    with contextlib.ExitStack() as st:
        S = Sched(nc, st)
        C = p2_alloc(S, nc)
        p2_setup(C, dr)
        for gi in range(ngroups if ngroups else T // P2_TG):
            p2_group(C, dr, gi, do_l0, do_l1)
        S.finish()
        print("p2 ops", S.stats(), "sems", S.nsem, "sbuf left", nc.sbuf_bytes_remaining)
        S.emit()
    return nc


def p2_host_inputs(inp):
    f = np.float32
    d = {}
    d['w_out0'] = np.ascontiguousarray(inp['la_w_out'][0])
    d['w_up'] = inp['ffn_w_up']
    d['w_dn'] = inp['ffn_w_down']
    d['sg_in'] = np.ascontiguousarray(inp['sg_w_in'][0])
    d['sg_out'] = np.ascontiguousarray(inp['sg_w_out'][0])
    d['nwT'] = np.ascontiguousarray(inp['norm_w'].reshape(8, 16, 128).transpose(2, 0, 1).reshape(128, 128))
    d['lnwT'] = np.ascontiguousarray(inp['sg_ln_w'][0].reshape(32, 128).T)
    d['lnbT'] = np.ascontiguousarray(inp['sg_ln_b'][0].reshape(32, 128).T)
    d['wsT'] = np.ascontiguousarray(inp['sg_w_s'][0].transpose(2, 0, 1).reshape(128, 1024))
    d['maskT'] = np.triu(np.ones((128, 128), f))
    d['bs'] = np.ascontiguousarray(inp['sg_b_s'][0].reshape(1024))
    d['ident'] = np.eye(128, dtype=f)
    return d


def gath_perm():
    perm = []
    for r in range(4):
        for k in range(4):
            perm.append((0 if k < 2 else 8) + 2 * r + (k % 2))
    return perm


import os
DBG = int(os.environ.get('FUSED_DBG', '0'))


def build_fused(L=8192):
    nc = bass.Bass("TRN2", target_bir_lowering=False)
    TQ = L // 4
    dr = p1_dram(nc, L, fused=True)
    dr2 = p2_dram(nc, TQ, fused=True)
    for k, v in dr2.items():
        dr[k] = v
    with contextlib.ExitStack() as semstack:
        oloc_b = Buf("oloc", multi=True)
        ogath_b = Buf("ogath")
        with contextlib.ExitStack() as st1:
            S1 = Sched(nc, st1, semstack, prefix="a_")
            C1 = p1_alloc(S1, nc)
            C1.oT_b = oloc_b
            p1_setup(C1, dr)
            rg = [[0, 1, 2, 3], [4, 5, 6, 7]]
            ogath_b.multi = True

            def gather(g):
                S1._add('gpsimd', lambda e: e.collective_compute("AllGather", ALU.bypass, replica_groups=rg,
                                                                 ins=[dr['oloc'][g].opt()], outs=[dr['ogath'][g].opt()]),
                        [oloc_b], [ogath_b], dma_buf=ogath_b, inc=1)

            NG1 = L // 512
            blk_list = p2_block_list(dr)
            NB = len(blk_list)
            wscr = nc.dram_tensor("wscr", [NB, 128, 4096], BF16, kind="Internal").ap()
            wscr_b = Buf("wscr", multi=True)
            per = (NB + NG1 - 1) // NG1

            def convert(g):
                for i in range(g * per, min(NB, (g + 1) * per)):
                    S1.dma('gpsimd', wscr[i].rearrange("p (kc n) -> p kc n", kc=16),
                           blk_list[i].rearrange("(kc p) n -> p kc n", p=128), writes=[wscr_b])

            def hook(g):
                if g > 0:
                    gather(g - 1)
                convert(g)

            for gi in range(NG1):
                p1_group(C1, dr, gi, L, hook=(lambda g=gi: hook(g)))
            if DBG != 1:
                gather(NG1 - 1)
            S1.finish(barrier=True)
            S1.emit()
        with contextlib.ExitStack() as st2:
            S2 = Sched(nc, st2, semstack, prefix="b_")
            C2 = p2_alloc(S2, nc)
            C2.ogath_b = Buf("ogath2")
            C2.wscr = wscr
            C2.blk_list = blk_list
            p2_setup(C2, dr)
            for gi in range(TQ // 512 if DBG not in (1, 2) else 0):
                p2_group(C2, dr, gi, fused=True, TQ=TQ)
            S2.finish()
            S2.emit()
    return nc


def fused_in_maps(inp, L=8192):
    consts = p1_host_consts(L)
    shared = p2_host_inputs(inp)
    perm = gath_perm()
    w = shared['w_out0']
    shared['w_out0'] = np.ascontiguousarray(np.concatenate([w[f * 128:(f + 1) * 128] for f in perm], 0))
    maps = []
    for c in range(8):
        m = p1_host_core(inp, c // 4, c % 4, L, consts)
        for k, v in shared.items():
            if k not in m:
                m[k] = v
        maps.append(m)
    return maps


def kernel(**inputs):
    inp = {k: np.asarray(v) for k, v in inputs.items()}
    B, L = 2, 8192
    nc = build_fused(L)
    maps = fused_in_maps(inp, L)
    res = run_bass_kernel_spmd(nc, maps, core_ids=list(range(8)))
    out = np.empty((B, L, 2048), np.float32)
    for c in range(8):
        b, j = c // 4, c % 4
        out[b, j * (L // 4):(j + 1) * (L // 4)] = res.results[c]['out']
    return out
```
